# Optimizing a Trainium2 kernel written in Bass

```python
import math
import jax, jax.numpy as jnp
from jax import lax
import numpy as np

D_MODEL = 1024
BATCH = 16
SEQ = 4096
DEPTH = 1
DEC_BATCH = 8
DEC_SEQ = 64
PAST_LEN = 1024

CHUNK = 64
QBLOCK = 128
ROPE_THETA = 500000.0
RMS_EPS = 1e-6
DA_HEADS = 4
DA_QK_DIM = 64
DA_V_DIM = 2 * DA_QK_DIM
DA_WIDTH = DA_HEADS * DA_V_DIM
DA_ROT = DA_QK_DIM // 4
DA_QK_COLS = DA_HEADS * 2 * DA_QK_DIM
MLA_HEADS = 8
MLA_Q_LORA = 384
MLA_KV_LORA = 256
MLA_NOPE = 64
MLA_ROPE = 32
MLA_V_DIM = 64
MLA_WIDTH = MLA_HEADS * MLA_V_DIM
MLA_UQ_COLS = MLA_HEADS * (MLA_NOPE + MLA_ROPE)
IN_SIZES = (DA_QK_COLS, DA_QK_COLS, DA_WIDTH, DA_WIDTH, MLA_Q_LORA, MLA_KV_LORA, MLA_ROPE, MLA_WIDTH, 2 * D_MODEL)
IN_COLS = sum(IN_SIZES)

kernel_name = "hybrid_diffattn_mla_streaming_step"


def rms_norm(x, g):
    xf = x.astype(jnp.float32)
    y = xf * lax.rsqrt(jnp.mean(xf * xf, axis=-1, keepdims=True) + RMS_EPS)
    return (y * g.astype(jnp.float32)).astype(x.dtype)


def rope(x, pos, rot):
    half = rot // 2
    inv = jnp.float32(ROPE_THETA) ** (-jnp.arange(half, dtype=jnp.float32) * 2.0 / rot)
    ang = pos.astype(jnp.float32)[:, None] * inv
    ang = ang.reshape((ang.shape[0],) + (1,) * (x.ndim - 3) + (half,))
    cos = jnp.cos(ang).astype(x.dtype)
    sin = jnp.sin(ang).astype(x.dtype)
    x1 = x[..., :half]
    x2 = x[..., half:rot]
    return jnp.concatenate([x1 * cos - x2 * sin, x2 * cos + x1 * sin, x[..., rot:]], axis=-1)


def chunk_mask(q_pos, k_pos):
    return (k_pos // CHUNK)[None, :] <= (q_pos // CHUNK)[:, None]


def masked_softmax(s, mask):
    return jax.nn.softmax(jnp.where(mask, s.astype(jnp.float32), -jnp.inf), axis=-1)


def over_query_blocks(fn, qs, q_pos):
    S = q_pos.shape[0]
    if S <= QBLOCK:
        return fn(qs, q_pos)
    nb = S // QBLOCK

    def body(i):
        start = i * QBLOCK
        qb = tuple(lax.dynamic_slice_in_dim(q, start, QBLOCK, axis=1) for q in qs)
        pb = lax.dynamic_slice_in_dim(q_pos, start, QBLOCK)
        return fn(qb, pb)

    out = lax.map(body, jnp.arange(nb))
    out = jnp.moveaxis(out, 0, 1)
    return out.reshape((out.shape[0], S) + out.shape[3:])


def diff_attention(q, k, v, q_pos, k_pos, lam):
    scale = DA_QK_DIM ** -0.5

    def fn(qs, qp):
        (qb,) = qs
        s = jnp.einsum('bqhcd,bkhcd->bchqk', qb, k) * scale
        p = masked_softmax(s, chunk_mask(qp, k_pos))
        a = (p[:, 0] - lam * p[:, 1]).astype(v.dtype)
        return jnp.einsum('bhqk,bkhd->bqhd', a, v)

    return over_query_blocks(fn, (q,), q_pos)


def mla_attention(q, k, v, q_pos, k_pos):
    scale = (MLA_NOPE + MLA_ROPE) ** -0.5

    def fn(qs, qp):
        (qb,) = qs
        s = jnp.einsum('bqhd,bkhd->bhqk', qb, k) * scale
        p = masked_softmax(s, chunk_mask(qp, k_pos)).astype(v.dtype)
        return jnp.einsum('bhqk,bkhd->bqhd', p, v)

    return over_query_blocks(fn, (q,), q_pos)


def layer(x, past, lam_init, norm_g, w_in, gate_b, da_lambda, da_head_norm_g, mla_q_norm_g,
          mla_kv_norm_g, mla_w_uq, mla_w_uk, mla_w_uv, w_branch_a, w_branch_b, w_out):
    B, S, _ = x.shape
    past_len = 0 if past is None else past[0].shape[1]
    pos = past_len + jnp.arange(S, dtype=jnp.int32)
    k_pos = jnp.arange(past_len + S, dtype=jnp.int32)

    h = rms_norm(x, norm_g)
    proj = h @ w_in
    idx = np.cumsum(IN_SIZES)[:-1].tolist()
    da_q, da_k, da_v, da_z, cq, ckv, krope, mla_z, gates = jnp.split(proj, idx, axis=-1)

    q = rope(da_q.reshape(B, S, DA_HEADS, 2, DA_QK_DIM), pos, DA_ROT)
    k = rope(da_k.reshape(B, S, DA_HEADS, 2, DA_QK_DIM), pos, DA_ROT)
    v = da_v.reshape(B, S, DA_HEADS, DA_V_DIM)
    new_k = k.reshape(B, S, DA_HEADS, 2 * DA_QK_DIM)
    new_v = v
    c_kv = rms_norm(ckv, mla_kv_norm_g)
    k_r = rope(krope, pos, MLA_ROPE)

    if past is None:
        k_all, v_all, c_all, kr_all = k, v, c_kv, k_r
    else:
        pk, pv, pc, pr = past
        k_all = jnp.concatenate([pk.reshape(B, past_len, DA_HEADS, 2, DA_QK_DIM), k], axis=1)
        v_all = jnp.concatenate([pv, v], axis=1)
        c_all = jnp.concatenate([pc, c_kv], axis=1)
        kr_all = jnp.concatenate([pr, k_r], axis=1)
    T = past_len + S

    lf = da_lambda.astype(jnp.float32)
    lam = jnp.exp(jnp.sum(lf[0] * lf[1])) - jnp.exp(jnp.sum(lf[2] * lf[3])) + lam_init
    o_a = diff_attention(q, k_all, v_all, pos, k_pos, lam)
    o_a = (rms_norm(o_a, da_head_norm_g) * (1.0 - lam_init)).reshape(B, S, DA_WIDTH)

    qf = (rms_norm(cq, mla_q_norm_g) @ mla_w_uq).reshape(B, S, MLA_HEADS, MLA_NOPE + MLA_ROPE)
    q_m = jnp.concatenate([qf[..., :MLA_NOPE], rope(qf[..., MLA_NOPE:], pos, MLA_ROPE)], axis=-1)
    k_nope = (c_all @ mla_w_uk).reshape(B, T, MLA_HEADS, MLA_NOPE)
    v_m = (c_all @ mla_w_uv).reshape(B, T, MLA_HEADS, MLA_V_DIM)
    k_m = jnp.concatenate([k_nope, jnp.broadcast_to(kr_all[:, :, None, :], (B, T, MLA_HEADS, MLA_ROPE))], axis=-1)
    o_b = mla_attention(q_m, k_m, v_m, pos, k_pos).reshape(B, S, MLA_WIDTH)

    y_a = (o_a * jax.nn.silu(da_z)) @ w_branch_a
    y_b = (o_b * jax.nn.silu(mla_z)) @ w_branch_b
    g = jax.nn.sigmoid(gates + gate_b)
    m = g[..., :D_MODEL] * y_a + g[..., D_MODEL:] * y_b
    out = x + m @ w_out
    return out, (new_k, new_v, c_kv, k_r)


def setup_inputs(seed: int = 0) -> dict:
    key = jax.random.key(seed)
    ks = jax.random.split(key, 24)
    nrm = lambda k, shape, s: jax.random.normal(k, shape, jnp.float32) * s
    return {
        'x_prompt': nrm(ks[0], (BATCH, SEQ, D_MODEL), 1.0),
        'x_sample': nrm(ks[1], (DEC_BATCH, DEC_SEQ, D_MODEL), 1.0),
        'cache_da_k': nrm(ks[2], (DEPTH, DEC_BATCH, PAST_LEN, DA_HEADS, 2 * DA_QK_DIM), 1.0),
        'cache_da_v': nrm(ks[3], (DEPTH, DEC_BATCH, PAST_LEN, DA_HEADS, DA_V_DIM), 1.0),
        'cache_mla_latent': nrm(ks[4], (DEPTH, DEC_BATCH, PAST_LEN, MLA_KV_LORA), 1.0),
        'cache_mla_krope': nrm(ks[5], (DEPTH, DEC_BATCH, PAST_LEN, MLA_ROPE), 1.0),
        'norm_g': 1.0 + nrm(ks[6], (DEPTH, D_MODEL), 0.01),
        'w_in': nrm(ks[7], (DEPTH, D_MODEL, IN_COLS), D_MODEL ** -0.5),
        'gate_b': nrm(ks[8], (DEPTH, 2 * D_MODEL), 0.01),
        'da_lambda': nrm(ks[9], (DEPTH, 4, DA_QK_DIM), 0.1),
        'da_head_norm_g': 1.0 + nrm(ks[10], (DEPTH, DA_V_DIM), 0.01),
        'mla_q_norm_g': 1.0 + nrm(ks[11], (DEPTH, MLA_Q_LORA), 0.01),
        'mla_kv_norm_g': 1.0 + nrm(ks[12], (DEPTH, MLA_KV_LORA), 0.01),
        'mla_w_uq': nrm(ks[13], (DEPTH, MLA_Q_LORA, MLA_UQ_COLS), MLA_Q_LORA ** -0.5),
        'mla_w_uk': nrm(ks[14], (DEPTH, MLA_KV_LORA, MLA_HEADS * MLA_NOPE), MLA_KV_LORA ** -0.5),
        'mla_w_uv': nrm(ks[15], (DEPTH, MLA_KV_LORA, MLA_WIDTH), MLA_KV_LORA ** -0.5),
        'w_branch_a': nrm(ks[16], (DEPTH, DA_WIDTH, D_MODEL), DA_WIDTH ** -0.5),
        'w_branch_b': nrm(ks[17], (DEPTH, MLA_WIDTH, D_MODEL), MLA_WIDTH ** -0.5),
        'w_out': nrm(ks[18], (DEPTH, D_MODEL, D_MODEL), D_MODEL ** -0.5),
        'final_norm_g': 1.0 + nrm(ks[19], (D_MODEL,), 0.01),
    }


def reference(x_prompt, x_sample, cache_da_k, cache_da_v, cache_mla_latent, cache_mla_krope,
              norm_g, w_in, gate_b, da_lambda, da_head_norm_g, mla_q_norm_g, mla_kv_norm_g,
              mla_w_uq, mla_w_uk, mla_w_uv, w_branch_a, w_branch_b, w_out, final_norm_g):
    hp, hs = x_prompt, x_sample
    rows_p = ([], [], [], [])
    rows_s = ([], [], [], [])
    for l in range(DEPTH):
        lam_init = 0.8 - 0.6 * math.exp(-0.3 * l)
        w = (norm_g[l], w_in[l], gate_b[l], da_lambda[l], da_head_norm_g[l], mla_q_norm_g[l],
             mla_kv_norm_g[l], mla_w_uq[l], mla_w_uk[l], mla_w_uv[l], w_branch_a[l], w_branch_b[l], w_out[l])
        hp, new_p = layer(hp, None, lam_init, *w)
        past = (cache_da_k[l], cache_da_v[l], cache_mla_latent[l], cache_mla_krope[l])
        hs, new_s = layer(hs, past, lam_init, *w)
        for acc, r in zip(rows_p, new_p):
            acc.append(r)
        for acc, r in zip(rows_s, new_s):
            acc.append(r)
    y_prompt = rms_norm(hp, final_norm_g)
    y_sample = rms_norm(hs, final_norm_g)
    new_da_k_p = jnp.stack(rows_p[0], 0)
    new_da_v_p = jnp.stack(rows_p[1], 0)
    new_lat_p = jnp.stack(rows_p[2], 0)
    new_krope_p = jnp.stack(rows_p[3], 0)
    new_da_k_s = jnp.stack(rows_s[0], 0)
    new_da_v_s = jnp.stack(rows_s[1], 0)
    new_lat_s = jnp.stack(rows_s[2], 0)
    new_krope_s = jnp.stack(rows_s[3], 0)
    return (y_prompt, y_sample, new_da_k_p, new_da_v_p, new_lat_p, new_krope_p,
            new_da_k_s, new_da_v_s, new_lat_s, new_krope_s)
```

```python
import math
import numpy as np
from contextlib import ExitStack
import concourse.bass as bass
import concourse.mybir as mybir
from concourse.bass_utils import run_bass_kernel_spmd

F32 = mybir.dt.float32
BF16 = mybir.dt.bfloat16
AF = mybir.ActivationFunctionType
ALU = mybir.AluOpType

D = 1024
SEM_ROT = 12000
SCL = {}
EPS = 1e-6
import os
STOP = int(os.environ.get('KSTOP', '99'))
SUB = int(os.environ.get('KSUB', '99'))
C_DAQ, C_DAK, C_DAV, C_DAZ, C_CQ, C_CKV, C_KR, C_MZ, C_G = 0, 512, 1024, 1536, 2048, 2432, 2688, 2720, 3232
IN_COLS = 5280
LAM_INIT = 0.8 - 0.6 * math.exp(-0.3 * 0)
SC_DA = 64 ** -0.5
SC_MLA = 96 ** -0.5


class Buf:
    __slots__ = ("name", "w", "rs", "excl")

    def __init__(self, name="", excl=False):
        self.name = name
        self.w = None
        self.rs = []
        self.excl = excl


class DmaSem:
    def __init__(self, sem):
        self.sem = sem
        self.count = 0
        self.last_group = None


class DmaGroup:
    def __init__(self, ds):
        self.ds = ds
        self.final = None
        self.last_op = None


class Op:
    __slots__ = ("eng", "name", "kw", "deps", "signal", "sem", "val", "idx", "group", "is_dma", "gidx", "region", "fin")

    def __init__(self, eng, name, kw):
        self.eng = eng
        self.name = name
        self.kw = kw
        self.deps = []
        self.signal = False
        self.sem = None
        self.val = None
        self.idx = None
        self.group = None
        self.is_dma = False


class Sched:
    ENGS = ("pe", "act", "dve", "pool", "sp")

    def __init__(self, nc, stack):
        self.nc = nc
        self.ops = {e: [] for e in self.ENGS}
        self.dma_sems = []
        self._stack = stack
        self.bar_deps = []
        self.bar_seen = {e: True for e in self.ENGS}
        self.all_ops = []
        self.region = 0

    def new_sem(self, name):
        return self._stack.enter_context(self.nc.semaphore(name))

    def dma_sem(self, name):
        ds = DmaSem(self.new_sem(name))
        self.dma_sems.append(ds)
        return ds

    def dma_ring(self, name, n):
        return DmaRing([self.dma_sem(f"{name}{i}") for i in range(n)])

    def barrier(self):
        self.region += 1

    def _collect(self, op, reads, writes, extra):
        deps = []
        for b in reads:
            if b.w is not None:
                deps.append(b.w)
            if b.excl:
                for r in b.rs:
                    if r.eng != op.eng:
                        deps.append(r)
        for b in writes:
            if b.w is not None:
                deps.append(b.w)
            deps.extend(b.rs)
        deps.extend(extra)
        seen = set()
        out = []
        for d in deps:
            if d is op or id(d) in seen:
                continue
            seen.add(id(d))
            out.append(d)
        op.deps = out
        op.gidx = len(self.all_ops)
        op.region = self.region
        self.all_ops.append(op)
        for b in writes:
            b.w = op
            b.rs = []
        for b in reads:
            b.rs.append(op)

    def add(self, eng, name, kw, reads=(), writes=(), extra=()):
        op = Op(eng, name, kw)
        self._collect(op, reads, writes, extra)
        self.ops[eng].append(op)
        return op

    def dma(self, eng, name, kw, ds, reads=(), writes=(), extra=()):
        op = Op(eng, name, kw)
        op.is_dma = True
        extra = list(extra)
        g = DmaGroup(ds)
        if ds.last_group is not None:
            extra.append(ds.last_group.last_op)
        ds.last_group = g
        ds.count += 1
        g.final = 16 * ds.count
        g.last_op = op
        op.group = g
        op.sem = ds.sem
        op.signal = True
        self._collect(op, reads, writes, extra)
        self.ops[eng].append(op)
        return op

    @staticmethod
    def _fsize(ap):
        n = 1
        for x in ap.shape[1:]:
            n *= x
        return n

    def _dur(self, op):
        return self._dur0(op) * SCL.get("dma" if op.is_dma else op.eng, 1.0)

    def _dur0(self, op):
        kw = op.kw
        if op.is_dma:
            o = kw["out"]
            nbytes = self._fsize(o) * o.shape[0] * (4 if o.dtype == F32 else 2)
            return 2200.0 + nbytes / 120.0
        if op.eng == "pe":
            if op.name == "transpose":
                return 70.0
            return 8.0 + 0.41 * self._fsize(kw["rhs"])
        a = kw.get("in_", kw.get("in0", kw.get("out", kw.get("ap"))))
        f = self._fsize(a)
        if op.eng == "act":
            if os.environ.get("KEXP2") and f == 512 and kw.get("func") == AF.Exp:
                return (190.0 + 1024 / 1.2) / 2
            return 190.0 + f / 1.2 + (100.0 if "accum_out" in kw else 0.0)
        if op.eng == "dve":
            if op.name == "reciprocal":
                return 80.0 + 6.6 * f
            return 100.0 + f / 0.85
        return 200.0 + f / 0.48

    def schedule(self, dry=False, beta=None):
        import heapq
        if beta is None:
            beta = float(os.environ.get("KBETA", "0.3"))
        regions = {}
        for op in self.all_ops:
            regions.setdefault(op.region, []).append(op)
        new_ops = {e: [] for e in self.ENGS}
        t0 = 0.0
        tail = []
        for r in sorted(regions):
            ops = regions[r]
            reg_first = {}
            reg_last = {}
            dma_last = {}
            inreg = set(id(o) for o in ops)
            succ = {}
            indeg = {}
            ready = {}
            for o in ops:
                cnt = 0
                for d in o.deps:
                    if id(d) in inreg:
                        cnt += 1
                        succ.setdefault(id(d), []).append(o)
                indeg[id(o)] = cnt
                ready[id(o)] = t0
            bl = {}
            if beta:
                for o in reversed(ops):
                    m = 0.0
                    for sc in succ.get(id(o), ()):
                        v = bl[id(sc)]
                        if v > m:
                            m = v
                    bl[id(o)] = m + self._dur(o)
            heap = [(t0 - beta * bl.get(id(o), 0.0), o.gidx, o) for o in ops if indeg[id(o)] == 0]
            heapq.heapify(heap)
            free = {e: t0 for e in self.ENGS}
            tmax = t0
            while heap:
                _, _, o = heapq.heappop(heap)
                rt = ready[id(o)]
                st = max(rt, free[o.eng])
                if o.is_dma:
                    issue = 1000.0 if o.eng == "pool" else 80.0
                    free[o.eng] = st + issue
                    fin = st + issue + self._dur(o)
                else:
                    fin = st + self._dur(o)
                    free[o.eng] = fin
                o.fin = fin
                tmax = max(tmax, fin)
                new_ops[o.eng].append(o)
                if o.eng not in reg_first:
                    reg_first[o.eng] = o
                if o.is_dma:
                    cur = dma_last.get(id(o.sem))
                    if cur is None or o.group.final > cur.group.final:
                        dma_last[id(o.sem)] = o
                else:
                    reg_last[o.eng] = o
                for sc in succ.get(id(o), ()):
                    if o.eng == "pe" and sc.eng == "pe" and not sc.is_dma and not o.is_dma:
                        lat = 0.0
                    elif o.eng == sc.eng and not o.is_dma:
                        lat = 60.0
                    else:
                        lat = 150.0
                    lat *= SCL.get("lat", 1.0)
                    ready[id(sc)] = max(ready[id(sc)], fin + lat)
                    indeg[id(sc)] -= 1
                    if indeg[id(sc)] == 0:
                        heapq.heappush(heap, (ready[id(sc)] - beta * bl.get(id(sc), 0.0), sc.gidx, sc))
            if not dry:
                for e, o in reg_first.items():
                    have = set(id(d) for d in o.deps)
                    o.deps = list(o.deps) + [d for d in tail if id(d) not in have and d is not o]
            tail = list(reg_last.values()) + list(dma_last.values())
            if os.environ.get("KDEBUG"):
                busy = {}
                for o in ops:
                    if not o.is_dma:
                        busy[o.eng] = busy.get(o.eng, 0.0) + self._dur(o)
                print("region", r, "ops", len(ops), "dur us", round((tmax - t0) / 1000, 1), {k: round(v / 1000) for k, v in busy.items()})
            t0 = tmax
        assert sum(len(v) for v in new_ops.values()) == len(self.all_ops)
        if dry:
            return t0
        self.ops = new_ops
        self.est_ns = t0

    def finalize(self):
        if os.environ.get("KSCHED", "1") == "1":
            self.schedule()
        for e in self.ENGS:
            for i, op in enumerate(self.ops[e]):
                op.idx = i
        for e in self.ENGS:
            for op in self.ops[e]:
                best = {}
                out = []
                seen_groups = set()
                for d in op.deps:
                    if d.is_dma:
                        if id(d.group) not in seen_groups:
                            seen_groups.add(id(d.group))
                            out.append(d)
                    else:
                        if d.eng == "pe" and op.eng == "pe" and not op.is_dma:
                            continue
                        cur = best.get(d.eng)
                        if cur is None or d.idx > cur.idx:
                            best[d.eng] = d
                out.extend(best.values())
                for d in out:
                    d.signal = True
                op.deps = out
        for e in ("pe", "act", "dve", "pool"):
            nsem = 0
            cnt = 0
            cur = None
            for op in self.ops[e]:
                if op.is_dma or not op.signal:
                    continue
                if cur is None or cnt >= SEM_ROT:
                    cur = self.new_sem(f"s_{e}{nsem}")
                    nsem += 1
                    cnt = 0
                cnt += 1
                op.sem = cur
                op.val = cnt
        for e in self.ENGS:
            for op in self.ops[e]:
                if op.is_dma:
                    op.val = op.group.final

    def emit(self, block):
        self.finalize()
        stats = {}

        def run(e):
            def body(eng):
                seen = {}
                nw = 0
                for op in self.ops[e]:
                    for d in op.deps:
                        k = id(d.sem)
                        if seen.get(k, 0) >= d.val:
                            continue
                        seen[k] = d.val
                        eng.wait_ge(d.sem, d.val)
                        nw += 1
                    ins = getattr(eng, op.name)(**op.kw)
                    if op.signal:
                        ins.then_inc(op.sem, 16 if op.is_dma else 1)
                if e == "sp":
                    for ds in self.dma_sems:
                        if ds.count:
                            eng.wait_ge(ds.sem, 16 * ds.count)
                stats[e] = (len(self.ops[e]), nw)
            return body

        block.tensor(run("pe"))
        block.scalar(run("act"))
        block.vector(run("dve"))
        block.gpsimd(run("pool"))
        block.sync(run("sp"))
        return stats


class DmaRing:
    def __init__(self, sems):
        self.sems = sems
        self.i = 0

    def next(self):
        s = self.sems[self.i % len(self.sems)]
        self.i += 1
        return s


class R:
    __slots__ = ("ap", "b")

    def __init__(self, ap, name=""):
        self.ap = ap
        self.b = Buf(name)


class Ring:
    def __init__(self, items):
        self.items = items
        self.i = 0

    def next(self):
        r = self.items[self.i % len(self.items)]
        self.i += 1
        return r


class Arena:
    def __init__(self, regions):
        self.regions = regions
        self.reset()

    def reset(self):
        self.off = [0 for _ in self.regions]

    def alloc(self, shape, dtype, name=""):
        n = 1
        for s in shape:
            n *= s
        esz = 4 if dtype == F32 else 2
        nel = n * esz // 2
        for i, reg in enumerate(self.regions):
            o = (self.off[i] + 1) // 2 * 2
            if o + nel <= reg.shape[1]:
                self.off[i] = o + nel
                ap = reg[:, o:o + nel]
                if dtype != BF16:
                    ap = ap.bitcast(dtype)
                if len(shape) == 2:
                    ap = ap.rearrange("p (a b) -> p a b", b=shape[1])
                elif len(shape) == 3:
                    ap = ap.rearrange("p (a b c) -> p a b c", b=shape[1], c=shape[2])
                return R(ap, name)
        raise RuntimeError(f"arena overflow allocating {name} {shape}; offs={self.off}")


def run_skewed(stages, items):
    n, ns = len(items), len(stages)
    for step in range(n + ns - 1):
        for si in range(ns):
            j = step - si
            if 0 <= j < n:
                stages[si](items[j])


class SubTile:
    def __init__(self, row0, n, kt, tt, c0):
        self.row0, self.n, self.kt, self.tt, self.c0 = row0, n, kt, tt, c0


class SeqDesc:
    pass


class Builder:
    def __init__(self, NSEQ, SL, SAMPLE, parts=("mla", "da", "fin")):
        self.NSEQ, self.SL, self.SAMPLE = NSEQ, SL, SAMPLE
        self.parts = parts
        self.TK = max(SL, 1152 if SAMPLE else 0)
        self.NTT = SL // 128 + 1

    def dram(self, name, shape, kind="ExternalInput"):
        return self.nc.dram_tensor(name, list(shape), F32, kind=kind).ap()

    def sb(self, name, shape, dtype):
        return self.st.enter_context(self.nc.sbuf_tensor("sb_" + name, list(shape), dtype))

    def build(self):
        nc = self.nc = bass.Bass("TRN2", target_bir_lowering=False)
        NSEQ, SL, TK = self.NSEQ, self.SL, self.TK
        NP = NSEQ * SL
        dr = self.dram
        self.xp = dr("xp", [NP, D])
        self.w_in = dr("w_in", [D, IN_COLS])
        self.i_normg = dr("norm_g", [1, D])
        self.i_gateb = dr("gate_bT", [128, 16])
        self.i_lam = dr("da_lambda", [1, 256])
        self.i_hng = dr("hng", [128, 1])
        self.i_gq = dr("gq", [1, 384])
        self.i_gkv = dr("gkv", [1, 256])
        self.i_wuq = dr("w_uq", [384, 768])
        self.i_wuk = dr("w_uk", [256, 512])
        self.i_wuv = dr("w_uv", [256, 512])
        self.i_wba = dr("w_ba", [512, D])
        self.i_wbb = dr("w_bb", [512, D])
        self.i_wout = dr("w_out", [D, D])
        self.i_gf = dr("gf", [1, D])
        self.i_ident = dr("ident", [128, 128])
        self.i_shift = dr("shiftm", [128, 128])
        NTT = self.NTT
        self.i_cosD = dr("cosD", [128, NTT * 8])
        self.i_sinD = dr("sinD", [128, NTT * 8])
        self.i_cosM = dr("cosM", [128, NTT * 16])
        self.i_sinM = dr("sinM", [128, NTT * 16])
        o = lambda n, s: dr(n, s, kind="ExternalOutput")
        self.o_y = o("yp", [NP, D])
        self.o_k = o("kp", [NP, 512])
        self.o_v = o("vp", [NP, 512])
        self.o_lat = o("latp", [NP, 256])
        self.o_kr = o("krp", [NP, 32])
        if self.SAMPLE:
            self.xs = dr("xs", [64, D])
            self.cdk = dr("cdk", [1024, 512])
            self.cdv = dr("cdv", [1024, 512])
            self.clat = dr("clat", [1024, 256])
            self.ckr = dr("ckr", [1024, 32])
            self.o_ys = o("ys", [64, D])
            self.o_ks = o("ks", [64, 512])
            self.o_vs = o("vs", [64, 512])
            self.o_lats = o("lats", [64, 256])
            self.o_krs = o("krs", [64, 32])

        with ExitStack() as st:
            self.st = st
            S = self.S = Sched(nc, st)
            sb = self.sb
            self.U_N = max(8 * TK + (TK // 128 + 1) * 520, 8 * TK + 12288, 40960)
            self.U = sb("U", [128, self.U_N], BF16)
            self.SLX = SL
            SLX = self.SLX
            self.OB = sb("OB", [128, 4, SLX], BF16)
            self.OA_N = max(4 * SLX, 9728)
            self.OA = sb("OA", [128, self.OA_N], BF16)
            self.OBb = [Buf(f"OB{i}") for i in range(SL // 512 + 1)]
            self.OAb = [Buf(f"OA{i}") for i in range(SL // 512 + 1)]
            self.Ub = Buf("U")
            self.ident = R(sb("ident", [128, 128], BF16))
            self.ones = R(sb("ones", [128, 128], BF16))
            self.shiftm = R(sb("shiftm", [128, 128], BF16))
            self.sel64 = R(sb("sel64", [128, 64], BF16))
            self.cosD = R(sb("cosD", [128, NTT, 8], F32))
            self.sinD = R(sb("sinD", [128, NTT, 8], F32))
            self.cosM = R(sb("cosM", [128, NTT, 16], F32))
            self.sinM = R(sb("sinM", [128, NTT, 16], F32))
            self.g_in = R(sb("g_in", [128, D], F32))
            self.gateb = R(sb("gateb", [128, 16], F32))
            self.vec = R(sb("vec", [128, 16], F32))
            self.lamt = R(sb("lamt", [128, 256], F32))
            self.stats = Ring([R(sb(f"stat{i}", [128, 4], F32)) for i in range(6)])
            self.cstage = R(sb("cstage", [128, 128], F32))
            rem = nc.sbuf_bytes_remaining
            tn = (rem - 64) // 2 // 2 * 2
            self.Tt = sb("T", [128, tn], BF16)
            self.PS = [R(st.enter_context(nc.psum_tensor(f"ps{i}", [128, 512], F32)), f"ps{i}") for i in range(8)]
            for r_ in self.PS:
                r_.b.excl = True
            self.ds_const = S.dma_sem("dconst")
            self.dr_w = S.dma_ring("dw", 4)
            self.dr_x = S.dma_ring("dx", 2)
            self.dr_x2 = S.dma_ring("dxo", 2)
            self.dr_o = {k: S.dma_ring("do" + k, 2) for k in ("k", "v", "lat", "kr", "y")}
            self.ds_cache = S.dma_sem("dcache")

            self.consts()
            seqs = []
            for s in range(NSEQ):
                sd = SeqDesc()
                sd.x = self.xp
                sd.prompt = True
                sd.ncache = 0
                sd.blocks = []
                for b in range(SL // 512):
                    sd.blocks.append([SubTile(s * SL + (4 * b + j) * 128, 128, 4 * b + j, 4 * b + j, (4 * b + j) * 128) for j in range(4)])
                sd.o_y, sd.o_k, sd.o_v, sd.o_lat, sd.o_kr = self.o_y, self.o_k, self.o_v, self.o_lat, self.o_kr
                sd.obi = list(range(SL // 512))
                sd.OB = self.OB
                sd.OA = self.OA[:, 0:4 * SL].rearrange("p (h t) -> p h t", t=SL)
                seqs.append(sd)
            if self.SAMPLE:
                sd = SeqDesc()
                sd.x = self.xs
                sd.prompt = False
                sd.ncache = 8
                sd.blocks = [[SubTile(0, 64, 8, NTT - 1, 0)]]
                sd.obi = [SL // 512]
                lb = self.lamt.ap[:, :].bitcast(BF16)
                sd.OB = lb[:, 0:256].rearrange("p (h t) -> p h t", t=64)
                sd.OA = lb[:, 256:512].rearrange("p (h t) -> p h t", t=64)
                sd.o_y, sd.o_k, sd.o_v, sd.o_lat, sd.o_kr = self.o_ys, self.o_ks, self.o_vs, self.o_lats, self.o_krs
                seqs.append(sd)
            prompts = [q for q in seqs if q.prompt]
            smp = [q for q in seqs if not q.prompt]
            groups = [[q] for q in prompts[:-1]] + [prompts[-1:] + smp]
            for grp in groups:
                for pname, fn in (("mla", self.mla_pass), ("da", self.da_pass), ("fin", self.fin_pass)):
                    if pname in self.parts:
                        for gi, sd in enumerate(grp):
                            fn(sd, reload=(gi == 0))
            with nc.allow_low_precision(reason="bf16 matmul operands by design"), nc.Block() as block:
                self.stats_out = S.emit(block)
        return nc

    def consts(self):
        S = self.S
        ds = self.ds_const
        cs = self.cstage
        for src, dst in ((self.i_ident, self.ident), (self.i_shift, self.shiftm)):
            S.dma("sp", "dma_start", dict(out=cs.ap[:], in_=src), ds, writes=[cs.b])
            S.add("dve", "tensor_copy", dict(out=dst.ap[:], in_=cs.ap[:]), reads=[cs.b], writes=[dst.b])
        S.add("pool", "memset", dict(ap=self.ones.ap[:], constant=1.0), writes=[self.ones.b])
        S.add("pool", "memset", dict(ap=self.sel64.ap[:], constant=0.0), writes=[self.sel64.b])
        S.add("pool", "memset", dict(ap=self.sel64.ap[64:65, :], constant=1.0), writes=[self.sel64.b])
        NTT = self.NTT
        for src, dst, k in ((self.i_cosD, self.cosD, 8), (self.i_sinD, self.sinD, 8), (self.i_cosM, self.cosM, 16), (self.i_sinM, self.sinM, 16)):
            S.dma("sp", "dma_start", dict(out=dst.ap[:], in_=src.rearrange("p (t k) -> p t k", k=k)), ds, writes=[dst.b])
        S.dma("sp", "dma_start", dict(out=self.g_in.ap[:], in_=self.i_normg.partition_broadcast(128)), ds, writes=[self.g_in.b])
        S.dma("sp", "dma_start", dict(out=self.gateb.ap[:], in_=self.i_gateb), ds, writes=[self.gateb.b])
        S.dma("sp", "dma_start", dict(out=self.lamt.ap[:], in_=self.i_lam.partition_broadcast(128)), ds, writes=[self.lamt.b])
        v = self.vec
        S.dma("sp", "dma_start", dict(out=v.ap[:, 1:2], in_=self.i_hng), ds, writes=[v.b])
        lt = self.lamt
        S.add("dve", "tensor_tensor", dict(out=lt.ap[:, 0:64], in0=lt.ap[:, 0:64], in1=lt.ap[:, 64:128], op=ALU.mult), reads=[lt.b], writes=[lt.b])
        S.add("dve", "tensor_tensor", dict(out=lt.ap[:, 128:192], in0=lt.ap[:, 128:192], in1=lt.ap[:, 192:256], op=ALU.mult), reads=[lt.b], writes=[lt.b])
        S.add("dve", "tensor_reduce", dict(out=v.ap[:, 2:3], in_=lt.ap[:, 0:64], op=ALU.add, axis=mybir.AxisListType.X), reads=[lt.b], writes=[v.b])
        S.add("dve", "tensor_reduce", dict(out=v.ap[:, 3:4], in_=lt.ap[:, 128:192], op=ALU.add, axis=mybir.AxisListType.X), reads=[lt.b], writes=[v.b])
        S.add("act", "activation", dict(out=v.ap[:, 4:6], in_=v.ap[:, 2:4], func=AF.Exp), reads=[v.b], writes=[v.b])
        S.add("dve", "scalar_tensor_tensor", dict(out=v.ap[:, 0:1], in0=v.ap[:, 5:6], scalar=-LAM_INIT, in1=v.ap[:, 4:5], op0=ALU.add, op1=ALU.subtract), reads=[v.b], writes=[v.b])
        S.add("dve", "tensor_scalar", dict(out=v.ap[:, 1:2], in0=v.ap[:, 1:2], scalar1=(1.0 - LAM_INIT), scalar2=None, op0=ALU.mult), reads=[v.b], writes=[v.b])

    def new_arena(self, extra_regions=()):
        self.S.barrier()
        self.ar = Arena([self.Tt[:, :]] + list(extra_regions))
        ar = self.ar
        self.xring = Ring([ar.alloc([D], F32, f"xt{i}") for i in range(2)])
        self.hbring = Ring([ar.alloc([D], BF16, f"hb{i}") for i in range(2)])

    def rstd(self, src_ap, src_bufs, n, F, junk):
        S = self.S
        stt = self.stats.next()
        S.add("act", "activation", dict(out=junk.ap[0:n, 0:F], in_=src_ap, func=AF.Square, accum_out=stt.ap[0:n, 0:1]),
              reads=src_bufs, writes=[junk.b, stt.b])
        S.add("act", "activation", dict(out=stt.ap[0:n, 1:2], in_=stt.ap[0:n, 0:1], func=AF.Ln, scale=1.0 / F, bias=EPS),
              reads=[stt.b], writes=[stt.b])
        S.add("act", "activation", dict(out=stt.ap[0:n, 2:3], in_=stt.ap[0:n, 1:2], func=AF.Exp, scale=-0.5),
              reads=[stt.b], writes=[stt.b])
        return stt

    def load_x(self, sd, stl, ring=None):
        S = self.S
        xt = self.xring.next()
        n = stl.n
        S.dma("sp", "dma_start", dict(out=xt.ap[0:n, :], in_=sd.x[stl.row0:stl.row0 + n, :]), (ring or self.dr_x).next(), writes=[xt.b])
        return xt

    def prologue(self, sd, stl, hT_ap, hT_buf, copy_eng="act", outs=None):
        S = self.S
        n = stl.n
        xt = self.load_x(sd, stl)
        hb = self.hbring.next()
        stt = self.rstd(xt.ap[0:n, :], [xt.b], n, D, hb)
        S.add("dve", "scalar_tensor_tensor", dict(out=hb.ap[0:n, :], in0=xt.ap[0:n, :], scalar=stt.ap[0:n, 2:3], in1=self.g_in.ap[0:n, :],
                                                       op0=ALU.mult, op1=ALU.mult), reads=[xt.b, stt.b, self.g_in.b], writes=[hb.b])
        pb = self.PS[0]
        pbf = pb.ap[:, :].bitcast(BF16).rearrange("p (c t) -> p c t", t=128)
        for c in range(8):
            S.add("pe", "transpose", dict(out=pbf[:, c, 0:n], in_=hb.ap[0:n, c * 128:(c + 1) * 128], identity=self.ident.ap[0:n, 0:n]),
                  reads=[hb.b, self.ident.b], writes=[pb.b])
        if outs is None:
            outs = [(hT_ap, 0, 8, [hT_buf])]
        for ap_, lo, hi, bufs in outs:
            if copy_eng == "act":
                S.add("act", "copy", dict(out=ap_, in_=pbf[:, lo:hi, 0:n]), reads=[pb.b], writes=bufs)
            else:
                S.add("dve", "tensor_copy", dict(out=ap_, in_=pbf[:, lo:hi, 0:n]), reads=[pb.b], writes=bufs)
        return xt

    def proj_tok(self, hT, n, w_ap, ncols, bank):
        S = self.S
        for c in range(8):
            S.add("pe", "matmul", dict(out=bank.ap[0:n, 0:ncols], lhsT=hT.ap[:, c, 0:n], rhs=w_ap[:, c, :], start=(c == 0), stop=(c == 7)),
                  reads=[hT.b, self.wb], writes=[bank.b])

    def load_w(self, dst_ap, src_ap, K, c0, c1):
        S = self.S
        a = c0
        while a < c1:
            b = min(a + 1024, c1)
            S.dma("pool", "dma_start", dict(out=dst_ap[:, :, a - c0:b - c0], in_=src_ap[:, a:b].rearrange("(c p) n -> p c n", p=128)),
                  self.dr_w.next(), writes=[self.wb])
            a = b

    def rope(self, eng, src4, dst4, n, cos_ap, sin_ap, G, half, tmp, src_bufs, dst_bufs):
        S = self.S
        t4 = tmp.ap[0:n, 0:G * 2 * half].rearrange("p (g k) -> p g k", k=2 * half)
        cb = cos_ap.unsqueeze(1).broadcast_to([n, G, half])
        sbb = sin_ap.unsqueeze(1).broadcast_to([n, G, half])
        rb = src_bufs + [self.cosM.b, self.sinM.b, self.cosD.b, self.sinD.b]
        S.add(eng, "scalar_tensor_tensor", dict(out=t4[:, :, 0:half], in0=src4[:, :, half:2 * half], scalar=-1.0, in1=sbb, op0=ALU.mult, op1=ALU.mult),
              reads=rb, writes=[tmp.b])
        S.add(eng, "tensor_tensor", dict(out=t4[:, :, half:2 * half], in0=src4[:, :, 0:half], in1=sbb, op=ALU.mult), reads=rb, writes=[tmp.b])
        S.add(eng, "tensor_tensor", dict(out=dst4[:, :, 0:half], in0=src4[:, :, 0:half], in1=cb, op=ALU.mult), reads=rb, writes=dst_bufs)
        S.add(eng, "tensor_tensor", dict(out=dst4[:, :, half:2 * half], in0=src4[:, :, half:2 * half], in1=cb, op=ALU.mult), reads=rb, writes=dst_bufs)
        S.add(eng, "tensor_tensor", dict(out=dst4, in0=dst4, in1=t4, op=ALU.add), reads=[tmp.b] + dst_bufs, writes=dst_bufs)

    def key_tiles(self, sd, blk, bi):
        if sd.prompt:
            out = [(kt, 128, 0, False) for kt in range(4 * bi)]
            for j in range(4):
                out.append((4 * bi + j, 128, 128 * j, True))
            return out
        return [(kt, 128, 0, False) for kt in range(sd.ncache)] + [(sd.ncache, 64, 0, False)]

    def mla_pass(self, sd, reload=True):
        S = self.S
        TK = self.TK
        NKT = TK // 128
        w1 = self.OA[:, 0:9728]
        regs = [self.OA[:, 9728:self.OA_N]] if self.OA_N > 9728 + 1024 else []
        if not sd.prompt and self.SL >= 2048:
            regs.append(self.U[:, 8 * TK + 10 * 520:8 * TK + NKT * 520])
        self.new_arena(regs)
        ar = self.ar
        if reload:
            self.wb = Buf("w1")
        w1in = w1[:, 0:5376].rearrange("p (c n) -> p c n", n=672)
        wuq = w1[:, 5376:7680].rearrange("p (c n) -> p c n", n=768)
        wuk = w1[:, 7680:8704].rearrange("p (c n) -> p c n", n=512)
        wuv = w1[:, 8704:9728].rearrange("p (c n) -> p c n", n=512)
        if reload:
            self.load_w(w1in, self.w_in, 8, C_CQ, C_MZ)
            self.load_w(wuq, self.i_wuq, 3, 0, 768)
            self.load_w(wuk, self.i_wuk, 2, 0, 512)
            self.load_w(wuv, self.i_wuv, 2, 0, 512)
        KT = self.U[:, 0:8 * TK].rearrange("p (h t) -> p h t", t=TK)
        VM = self.U[:, 8 * TK:8 * TK + NKT * 520].rearrange("p (k c) -> p k c", c=520)
        VMf = self.U[:, 8 * TK:8 * TK + (NKT + 1) * 520]
        ktb = [Buf(f"ktm{i}") for i in range(NKT)]
        vmb = [Buf(f"vm{i}") for i in range(NKT)]
        gq = ar.alloc([384], F32, "gq")
        gkv = ar.alloc([256], F32, "gkv")
        S.dma("sp", "dma_start", dict(out=gq.ap[:], in_=self.i_gq.partition_broadcast(128)), self.ds_const, writes=[gq.b])
        S.dma("sp", "dma_start", dict(out=gkv.ap[:], in_=self.i_gkv.partition_broadcast(128)), self.ds_const, writes=[gkv.b])
        nkc = (sd.ncache + 1) * 128 if not sd.prompt else TK
        nvz = (nkc // 128 + 1) * 520
        S.add("dve", "memset", dict(ap=VMf[:, 0:nvz // 2], constant=0.0), writes=vmb)
        S.add("pool", "memset", dict(ap=VMf[:, nvz // 2:nvz], constant=0.0), writes=vmb)
        S.add("pool", "memset", dict(ap=VM[:, 0:nkc // 128, 512:520], constant=1.0), writes=vmb)
        S.add("pool", "memset", dict(ap=KT[96:128, 0:4, 0:nkc], constant=0.0), writes=ktb)
        S.add("dve", "memset", dict(ap=KT[96:128, 4:8, 0:nkc], constant=0.0), writes=ktb)
        if STOP <= 1:
            return
        hTr = Ring([ar.alloc([8, 128], BF16, f"hTs{i}") for i in range(2)])
        pTr = Ring([ar.alloc([512], BF16, f"pT{i}") for i in range(3)])
        QT = ar.alloc([8, 512], BF16, "QT")
        cfr = Ring([ar.alloc([256], F32, f"cf{i}") for i in range(1)])
        krfr = Ring([ar.alloc([32], F32, f"krf{i}") for i in range(2)])
        cbf = ar.alloc([256], BF16, "cbf")
        cqbf = ar.alloc([384], BF16, "cqbf")
        ccT = ar.alloc([5, 128], BF16, "ccT")
        kfull = ar.alloc([8, 96], BF16, "kfull")
        qtok = ar.alloc([8, 96], BF16, "qtok")
        rtmp = ar.alloc([8 * 32], F32, "rtmp")
        u_sb = ar.alloc([512], F32, "u_sb")
        on_sb = ar.alloc([512], BF16, "on_sb")
        rr = ar.alloc([512], BF16, "rr")
        S.add("pool", "memset", dict(ap=rr.ap[:, :], constant=0.0), writes=[rr.b])
        rr32 = R(u_sb.ap, "rr32")
        junk = ar.alloc([384], BF16, "junk")
        S.add("pool", "memset", dict(ap=QT.ap[96:128, :, :], constant=0.0), writes=[QT.b])
        S.add("pool", "memset", dict(ap=on_sb.ap[:, :], constant=0.0), writes=[on_sb.b])
        PS = self.PS

        ccTr = Ring([ccT, ar.alloc([5, 128], BF16, "ccT1")])
        kfr_ = Ring([kfull, ar.alloc([8, 96], BF16, "kfull1")])
        rtmpC = ar.alloc([4 * 32], F32, "rtmpC")
        if os.environ.get("KDEBUG"):
            print("MLA arena", ar.off, [r.shape for r in ar.regions])

        def ctrans(cc, cb_ap, cb_b, n):
            pb = PS[4]
            pbf = pb.ap[:, :].bitcast(BF16).rearrange("p (c t) -> p c t", t=128)
            for c in range(2):
                S.add("pe", "transpose", dict(out=pbf[:, c, 0:n], in_=cb_ap[0:n, c * 128:(c + 1) * 128], identity=self.ident.ap[0:n, 0:n]),
                      reads=[cb_b, self.ident.b], writes=[pb.b])
            S.add("dve", "tensor_copy", dict(out=cc.ap[:, 0:2, 0:n], in_=pbf[:, 0:2, 0:n]), reads=[pb.b], writes=[cc.b])

        def kside2(cc, kf_, kt, n):
            for c in range(2):
                S.add("pe", "matmul", dict(out=PS[5].ap[0:n, :], lhsT=cc.ap[:, c, 0:n], rhs=wuk[:, c, :], start=(c == 0), stop=(c == 1)),
                      reads=[cc.b, self.wb], writes=[PS[5].b])
            for c in range(2):
                S.add("pe", "matmul", dict(out=PS[6].ap[0:n, :], lhsT=cc.ap[:, c, 0:n], rhs=wuv[:, c, :], start=(c == 0), stop=(c == 1)),
                      reads=[cc.b, self.wb], writes=[PS[6].b])
            S.add("dve", "tensor_copy", dict(out=kf_.ap[0:n, :, 0:64], in_=PS[5].ap[0:n, :].rearrange("p (h d) -> p h d", d=64)),
                  reads=[PS[5].b], writes=[kf_.b])
            S.add("dve", "tensor_copy", dict(out=VM[0:n, kt, 0:512], in_=PS[6].ap[0:n, :]), reads=[PS[6].b], writes=[vmb[kt]])
            pb2 = PS[7]
            pbf2 = pb2.ap[:, :].bitcast(BF16).rearrange("p (c t) -> p c t", t=128)
            for h in range(8):
                S.add("pe", "transpose", dict(out=pbf2[0:96, h, 0:n], in_=kf_.ap[0:n, h, :], identity=self.ident.ap[0:n, 0:n]),
                      reads=[kf_.b, self.ident.b], writes=[pb2.b])
            S.add("dve", "tensor_copy", dict(out=KT[0:96, :, kt * 128:kt * 128 + n], in_=pbf2[0:96, :, 0:n]), reads=[pb2.b], writes=[ktb[kt]])

        if sd.ncache:
            cst = ar.alloc([8, 256], BF16, "cst")
            kst = ar.alloc([8, 32], BF16, "kst")
            S.dma("pool", "dma_start", dict(out=cst.ap[:], in_=self.clat.rearrange("(t p) c -> p t c", p=128)), self.ds_cache, writes=[cst.b])
            S.dma("pool", "dma_start", dict(out=kst.ap[:], in_=self.ckr.rearrange("(t p) c -> p t c", p=128)), self.ds_cache, writes=[kst.b])

            def cB(t):
                cc, kf_ = ccTr.next(), kfr_.next()
                S.add("pool", "tensor_copy", dict(out=kf_.ap[:, :, 64:96], in_=kst.ap[:, t, :].unsqueeze(1).broadcast_to([128, 8, 32])),
                      reads=[kst.b], writes=[kf_.b])
                ctrans(cc, cst.ap[:, t, :], cst.b, 128)
                cctx[t] = (cc, kf_)

            def cC(t):
                cc, kf_ = cctx[t]
                kside2(cc, kf_, t, 128)
            cctx = {}
            run_skewed([cB, cC], list(range(sd.ncache)))

        def stA(c):
            c["hT"] = hTr.next()
            self.prologue(sd, c["stl"], c["hT"].ap[:, :, 0:c["stl"].n], c["hT"].b, copy_eng="dve")

        def stB(c):
            stl, hT = c["stl"], c["hT"]
            n = stl.n
            self.proj_tok(hT, n, w1in[:, :, 0:384], 384, PS[1])
            self.proj_tok(hT, n, w1in[:, :, 384:672], 288, PS[2])
            cc, kf_ = ccTr.next(), kfr_.next()
            c["cc"], c["kf"] = cc, kf_
            stt = self.rstd(PS[2].ap[0:n, 0:256], [PS[2].b], n, 256, junk)
            cf = cfr.next()
            S.add("dve", "scalar_tensor_tensor", dict(out=cf.ap[0:n, :], in0=PS[2].ap[0:n, 0:256], scalar=stt.ap[0:n, 2:3], in1=gkv.ap[0:n, :],
                                                      op0=ALU.mult, op1=ALU.mult), reads=[PS[2].b, stt.b, gkv.b], writes=[cf.b])
            S.dma("pool", "dma_start", dict(out=sd.o_lat[stl.row0:stl.row0 + n, :], in_=cf.ap[0:n, :]), self.dr_o["lat"].next(), reads=[cf.b])
            S.add("pool", "tensor_copy", dict(out=cbf.ap[0:n, :], in_=cf.ap[0:n, :]), reads=[cf.b], writes=[cbf.b])
            krf = krfr.next()
            src = PS[2].ap[0:n, 256:288].unsqueeze(1)
            dst = krf.ap[0:n, :].unsqueeze(1)
            self.rope("dve", src, dst, n, self.cosM.ap[0:n, stl.tt, :], self.sinM.ap[0:n, stl.tt, :], 1, 16, rtmp, [PS[2].b], [krf.b])
            S.dma("pool", "dma_start", dict(out=sd.o_kr[stl.row0:stl.row0 + n, :], in_=krf.ap[0:n, :]), self.dr_o["kr"].next(), reads=[krf.b])
            S.add("pool", "tensor_copy", dict(out=kf_.ap[0:n, :, 64:96], in_=krf.ap[0:n, :].unsqueeze(1).broadcast_to([n, 8, 32])),
                  reads=[krf.b], writes=[kf_.b])
            stq = self.rstd(PS[1].ap[0:n, 0:384], [PS[1].b], n, 384, junk)
            S.add("dve", "scalar_tensor_tensor", dict(out=cqbf.ap[0:n, :], in0=PS[1].ap[0:n, 0:384], scalar=stq.ap[0:n, 2:3], in1=gq.ap[0:n, :],
                                                      op0=ALU.mult, op1=ALU.mult), reads=[PS[1].b, stq.b, gq.b], writes=[cqbf.b])
            pb = PS[3]
            pbf = pb.ap[:, :].bitcast(BF16).rearrange("p (c t) -> p c t", t=128)
            for ch in range(3):
                S.add("pe", "transpose", dict(out=pbf[:, ch, 0:n], in_=cqbf.ap[0:n, ch * 128:(ch + 1) * 128], identity=self.ident.ap[0:n, 0:n]),
                      reads=[cqbf.b, self.ident.b], writes=[pb.b])
            S.add("dve", "tensor_copy", dict(out=cc.ap[:, 2:5, 0:n], in_=pbf[:, 0:3, 0:n]), reads=[pb.b], writes=[cc.b])
            ctrans(cc, cbf.ap, cbf.b, n)

        def stC(c):
            stl, cc, kf_, j = c["stl"], c["cc"], c["kf"], c["j"]
            n = stl.n
            kside2(cc, kf_, stl.kt, n)
            for hf in range(2):
                bank = PS[5 + hf]
                for ch in range(3):
                    S.add("pe", "matmul", dict(out=bank.ap[0:n, 0:384], lhsT=cc.ap[:, 2 + ch, 0:n], rhs=wuq[:, ch, hf * 384:(hf + 1) * 384],
                                               start=(ch == 0), stop=(ch == 2)), reads=[cc.b, self.wb], writes=[bank.b])
                q3 = bank.ap[0:n, 0:384].rearrange("p (h d) -> p h d", d=96)
                S.add("dve", "tensor_copy", dict(out=qtok.ap[0:n, 4 * hf:4 * hf + 4, 0:64], in_=q3[:, :, 0:64]), reads=[bank.b], writes=[qtok.b])
                self.rope("dve", q3[:, :, 64:96], qtok.ap[0:n, 4 * hf:4 * hf + 4, 64:96], n, self.cosM.ap[0:n, stl.tt, :], self.sinM.ap[0:n, stl.tt, :],
                          4, 16, rtmpC, [bank.b], [qtok.b])
            pb2 = PS[7]
            pbf2 = pb2.ap[:, :].bitcast(BF16).rearrange("p (c t) -> p c t", t=128)
            for h in range(8):
                S.add("pe", "transpose", dict(out=pbf2[0:96, h, 0:n], in_=qtok.ap[0:n, h, :], identity=self.ident.ap[0:n, 0:n]),
                      reads=[qtok.b, self.ident.b], writes=[pb2.b])
            S.add("act", "copy", dict(out=QT.ap[0:96, :, j * 128:j * 128 + n], in_=pbf2[0:96, :, 0:n]), reads=[pb2.b], writes=[QT.b])

        for bi, blk in enumerate(sd.blocks):
            nq = sum(s.n for s in blk)
            run_skewed([stA, stB, stC], [dict(stl=stl, j=j) for j, stl in enumerate(blk)])
            if STOP <= 7:
                continue
            tiles = self.key_tiles(sd, blk, bi)
            c0 = blk[0].c0
            units = [(h, ti) for h in range(8) for ti in range(len(tiles))]
            accs = Ring([PS[3], PS[4], PS[5]])
            sring = Ring([PS[0], PS[1], PS[2]])
            pend = []
            cur_acc = {}

            def issue_S(u):
                h, ti = u
                kt, nk, q0, mask = tiles[ti]
                sb_ = sring.next()
                S.add("pe", "matmul", dict(out=sb_.ap[0:nk, q0:nq], lhsT=KT[:, h, kt * 128:kt * 128 + nk], rhs=QT.ap[:, h, q0:nq], start=True, stop=True),
                      reads=[ktb[kt], QT.b], writes=[sb_.b])
                pT = pTr.next()
                S.add("act", "activation", dict(out=pT.ap[0:nk, q0:nq], in_=sb_.ap[0:nk, q0:nq], func=AF.Exp, scale=SC_MLA), reads=[sb_.b], writes=[pT.b])
                if mask:
                    S.add("pool", "memset", dict(ap=pT.ap[64:128, q0:q0 + 64], constant=0.0), writes=[pT.b])
                return pT

            def issue_AV(u, pT):
                h, ti = u
                kt, nk, q0, mask = tiles[ti]
                if ti == 0:
                    cur_acc[h] = accs.next()
                acc = cur_acc[h]
                S.add("pe", "matmul", dict(out=acc.ap[:, q0:nq], lhsT=VMf[0:nk, kt * 520 + h:kt * 520 + h + 1017:8], rhs=pT.ap[0:nk, q0:nq], start=(ti == 0), stop=(ti == len(tiles) - 1)),
                      reads=[vmb[kt], pT.b], writes=[acc.b])
                if ti == len(tiles) - 1:
                    fin(h, acc)

            deferred = []

            def defer(n, fn):
                deferred.append([n, fn])

            def tick(flush=False):
                while True:
                    due = [d for d in deferred if flush or d[0] <= 0]
                    if not due:
                        break
                    for d in due:
                        deferred.remove(d)
                    for d in due:
                        d[1]()
                for d in deferred:
                    d[0] -= 1

            def fin(h, acc):
                t, od = h // 2, h % 2
                S.add("act", "activation", dict(out=rr32.ap[64:65, 0:nq], in_=acc.ap[64:65, 0:nq], func=AF.Ln), reads=[acc.b], writes=[rr32.b])
                S.add("act", "activation", dict(out=rr.ap[64:65, 0:nq], in_=rr32.ap[64:65, 0:nq], func=AF.Exp, scale=-1.0), reads=[rr32.b], writes=[rr.b])
                S.add("dve", "tensor_copy", dict(out=u_sb.ap[0:64, 0:nq], in_=acc.ap[0:64, 0:nq]), reads=[acc.b], writes=[u_sb.b])

                def stage_b():
                    S.add("pe", "matmul", dict(out=PS[6].ap[0:64, 0:nq], lhsT=self.sel64.ap[:, 0:64], rhs=rr.ap[:, 0:nq], start=True, stop=True),
                          reads=[rr.b, self.sel64.b], writes=[PS[6].b])
                    if od == 0:
                        S.add("dve", "tensor_tensor", dict(out=sd.OB[0:64, t, c0:c0 + nq], in0=u_sb.ap[0:64, 0:nq], in1=PS[6].ap[0:64, 0:nq], op=ALU.mult),
                              reads=[u_sb.b, PS[6].b], writes=[self.OBb[sd.obi[bi]]])
                    else:
                        S.add("dve", "tensor_tensor", dict(out=on_sb.ap[0:64, 0:nq], in0=u_sb.ap[0:64, 0:nq], in1=PS[6].ap[0:64, 0:nq], op=ALU.mult),
                              reads=[u_sb.b, PS[6].b], writes=[on_sb.b])

                        def stage_c():
                            S.add("pe", "matmul", dict(out=PS[7].ap[:, 0:nq], lhsT=self.shiftm.ap[:, :], rhs=on_sb.ap[:, 0:nq], start=True, stop=True),
                                  reads=[on_sb.b, self.shiftm.b], writes=[PS[7].b])
                            S.add("dve", "tensor_copy", dict(out=sd.OB[64:128, t, c0:c0 + nq], in_=PS[7].ap[64:128, 0:nq]), reads=[PS[7].b], writes=[self.OBb[sd.obi[bi]]])
                        defer(2, stage_c)
                defer(3, stage_b)

            LOOK = 2
            for i, u in enumerate(units):
                pend.append((u, issue_S(u)))
                if len(pend) > LOOK:
                    issue_AV(*pend.pop(0))
                    tick()
            while pend:
                issue_AV(*pend.pop(0))
                tick()
            tick(flush=True)

    def da_pass(self, sd, reload=True):
        S = self.S
        TK = self.TK
        NKT = TK // 128
        regs = [self.U[:, 8 * TK + 12288:self.U_N]] if self.U_N - (8 * TK + 12288) > 1024 else []
        if not sd.prompt and self.SL >= 2048:
            regs.append(self.U[:, 4 * TK + 9 * 512:8 * TK])
        self.new_arena(regs)
        ar = self.ar
        w2 = self.U[:, 8 * TK:8 * TK + 12288].rearrange("p (c n) -> p c n", n=1536)
        if reload:
            self.wb = Buf("w2")
            self.load_w(w2, self.w_in, 8, 0, 1536)
        KT = self.U[:, 0:4 * TK].rearrange("p (h t) -> p h t", t=TK)
        VD = self.U[:, 4 * TK:8 * TK].rearrange("p (k c) -> p k c", c=512)
        ktb = [Buf(f"ktd{i}") for i in range(NKT)]
        vdb = [Buf(f"vd{i}") for i in range(NKT)]
        hTr = Ring([ar.alloc([8, 128], BF16, f"hTs{i}") for i in range(2)])
        pTr = Ring([ar.alloc([512], BF16, f"pT{i}") for i in range(4)])
        QT = ar.alloc([4, 2, 512], BF16, "QT")
        S.add("pool", "memset", dict(ap=QT.ap[64:128, :, 0, :], constant=0.0), writes=[QT.b])
        S.add("pool", "memset", dict(ap=QT.ap[0:64, :, 1, :], constant=0.0), writes=[QT.b])
        kfr = Ring([ar.alloc([512], F32, f"kf{i}") for i in range(1)])
        vfr = Ring([ar.alloc([512], F32, f"vf{i}") for i in range(1)])
        kb = ar.alloc([512], BF16, "kb")
        qb = ar.alloc([512], BF16, "qb")
        rtmp = ar.alloc([8 * 16], F32, "rtmp")
        t0 = ar.alloc([512], F32, "t0")
        t1 = ar.alloc([512], F32, "t1")
        oraw = ar.alloc([512], F32, "oraw")
        sq = kb
        lnr = ar.alloc([512], F32, "lnr")
        PS = self.PS
        OA = sd.OA

        def k_transposes(kb_ap, kb_b, kt, n):
            pb = PS[4]
            pbf = pb.ap[:, :].bitcast(BF16).rearrange("p (c t) -> p c t", t=128)
            for h in range(4):
                S.add("pe", "transpose", dict(out=pbf[:, h, 0:n], in_=kb_ap[0:n, h * 128:(h + 1) * 128], identity=self.ident.ap[0:n, 0:n]),
                      reads=[kb_b, self.ident.b], writes=[pb.b])
            S.add("act", "copy", dict(out=KT[:, :, kt * 128:kt * 128 + n], in_=pbf[:, 0:4, 0:n]), reads=[pb.b], writes=[ktb[kt]])

        if sd.ncache:
            kst = ar.alloc([8, 512], BF16, "kst")
            S.dma("pool", "dma_start", dict(out=kst.ap[:], in_=self.cdk.rearrange("(t p) c -> p t c", p=128)), self.ds_cache, writes=[kst.b])
            S.dma("pool", "dma_start", dict(out=VD[:, 0:8, :], in_=self.cdv.rearrange("(t p) c -> p t c", p=128)), self.ds_cache, writes=vdb[0:8])
            for t in range(sd.ncache):
                k_transposes(kst.ap[:, t, :], kst.b, t, 128)

        for bi, blk in enumerate(sd.blocks):
            nq = sum(s.n for s in blk)
            c0 = blk[0].c0
            for j, stl in enumerate(blk):
                n = stl.n
                kt = stl.kt
                hT = hTr.next()
                self.prologue(sd, stl, hT.ap[:, :, 0:n], hT.b)
                self.proj_tok(hT, n, w2[:, :, 0:512], 512, PS[1])
                self.proj_tok(hT, n, w2[:, :, 512:1024], 512, PS[2])
                self.proj_tok(hT, n, w2[:, :, 1024:1536], 512, PS[3])
                cosd = self.cosD.ap[0:n, stl.tt, :]
                sind = self.sinD.ap[0:n, stl.tt, :]
                kf = kfr.next()
                S.add("act", "copy", dict(out=kf.ap[0:n, :], in_=PS[2].ap[0:n, :]), reads=[PS[2].b], writes=[kf.b])
                k4 = PS[2].ap[0:n, :].rearrange("p (g d) -> p g d", d=64)[:, :, 0:16]
                kf4 = kf.ap[0:n, :].rearrange("p (g d) -> p g d", d=64)[:, :, 0:16]
                self.rope("dve", k4, kf4, n, cosd, sind, 8, 8, rtmp, [PS[2].b], [kf.b])
                S.dma("pool", "dma_start", dict(out=sd.o_k[stl.row0:stl.row0 + stl.n, :], in_=kf.ap[0:stl.n, :]), self.dr_o["k"].next(), reads=[kf.b])
                S.add("pool", "tensor_copy", dict(out=kb.ap[0:n, :], in_=kf.ap[0:n, :]), reads=[kf.b], writes=[kb.b])
                k_transposes(kb.ap, kb.b, kt, n)
                vf = vfr.next()
                S.add("act", "copy", dict(out=vf.ap[0:n, :], in_=PS[3].ap[0:n, :]), reads=[PS[3].b], writes=[vf.b])
                S.dma("pool", "dma_start", dict(out=sd.o_v[stl.row0:stl.row0 + stl.n, :], in_=vf.ap[0:stl.n, :]), self.dr_o["v"].next(), reads=[vf.b])
                S.add("dve", "tensor_copy", dict(out=VD[0:n, kt, :], in_=PS[3].ap[0:n, :]), reads=[PS[3].b], writes=[vdb[kt]])
                S.add("act", "copy", dict(out=qb.ap[0:n, :], in_=PS[1].ap[0:n, :]), reads=[PS[1].b], writes=[qb.b])
                q4 = PS[1].ap[0:n, :].rearrange("p (g d) -> p g d", d=64)[:, :, 0:16]
                qb4 = qb.ap[0:n, :].rearrange("p (g d) -> p g d", d=64)[:, :, 0:16]
                self.rope("dve", q4, qb4, n, cosd, sind, 8, 8, rtmp, [PS[1].b], [qb.b])
                pb = PS[5]
                pbf = pb.ap[:, :].bitcast(BF16).rearrange("p (c t) -> p c t", t=128)
                for h in range(4):
                    S.add("pe", "transpose", dict(out=pbf[:, h, 0:n], in_=qb.ap[0:n, h * 128:(h + 1) * 128], identity=self.ident.ap[0:n, 0:n]),
                          reads=[qb.b, self.ident.b], writes=[pb.b])
                S.add("act", "copy", dict(out=QT.ap[0:64, :, 0, j * 128:j * 128 + n], in_=pbf[0:64, 0:4, 0:n]), reads=[pb.b], writes=[QT.b])
                S.add("dve", "tensor_copy", dict(out=QT.ap[64:128, :, 1, j * 128:j * 128 + n], in_=pbf[64:128, 0:4, 0:n]), reads=[pb.b], writes=[QT.b])
            tiles = self.key_tiles(sd, blk, bi)
            units = [(h, c, ti) for h in range(4) for c in range(2) for ti in range(len(tiles))]
            accs = Ring([(PS[3], PS[4]), (PS[5], PS[6])])
            sring = Ring([PS[0], PS[1], PS[2]])
            pend = []
            cur_acc = {}
            tt = {0: t0, 1: t1}

            def issue_S(u):
                h, c, ti = u
                kt, nk, q0, mask = tiles[ti]
                sb_ = sring.next()
                S.add("pe", "matmul", dict(out=sb_.ap[0:nk, q0:nq], lhsT=KT[:, h, kt * 128:kt * 128 + nk], rhs=QT.ap[:, h, c, q0:nq],
                                               start=True, stop=True), reads=[ktb[kt], QT.b], writes=[sb_.b])
                pT = pTr.next()
                S.add("act", "activation", dict(out=pT.ap[0:nk, q0:nq], in_=sb_.ap[0:nk, q0:nq], func=AF.Exp, scale=SC_DA), reads=[sb_.b], writes=[pT.b])
                if mask:
                    S.add("pool", "memset", dict(ap=pT.ap[64:128, q0:q0 + 64], constant=0.0), writes=[pT.b])
                return pT

            def issue_AV(u, pT):
                h, c, ti = u
                kt, nk, q0, mask = tiles[ti]
                if ti == 0:
                    cur_acc[(h, c)] = accs.next()
                au, asum = cur_acc[(h, c)]
                first, last = (ti == 0), (ti == len(tiles) - 1)
                S.add("pe", "matmul", dict(out=au.ap[:, q0:nq], lhsT=VD[0:nk, kt, h * 128:(h + 1) * 128], rhs=pT.ap[0:nk, q0:nq], start=first, stop=last),
                      reads=[vdb[kt], pT.b], writes=[au.b])
                S.add("pe", "matmul", dict(out=asum.ap[:, q0:nq], lhsT=self.ones.ap[0:nk, :], rhs=pT.ap[0:nk, q0:nq], start=first, stop=last),
                      reads=[self.ones.b, pT.b], writes=[asum.b])
                if last:
                    t = tt[c]
                    if c == 0:
                        S.add("dve", "reciprocal", dict(out=t.ap[:, 0:nq], in_=asum.ap[:, 0:nq]), reads=[asum.b], writes=[t.b])
                    else:
                        S.add("act", "activation", dict(out=t.ap[:, 0:nq], in_=asum.ap[:, 0:nq], func=AF.Ln), reads=[asum.b], writes=[t.b])
                        S.add("act", "activation", dict(out=t.ap[:, 0:nq], in_=t.ap[:, 0:nq], func=AF.Exp, scale=-1.0), reads=[t.b], writes=[t.b])
                    S.add("dve", "tensor_tensor", dict(out=t.ap[:, 0:nq], in0=t.ap[:, 0:nq], in1=au.ap[:, 0:nq], op=ALU.mult), reads=[au.b, t.b], writes=[t.b])
                    if c == 1:
                        fin(h)

            deferred = []

            def defer(n, fn):
                deferred.append([n, fn])

            def tick(flush=False):
                while True:
                    due = [d for d in deferred if flush or d[0] <= 0]
                    if not due:
                        break
                    for d in due:
                        deferred.remove(d)
                    for d in due:
                        d[1]()
                for d in deferred:
                    d[0] -= 1

            def fin(h):
                v = self.vec
                S.add("dve", "scalar_tensor_tensor", dict(out=oraw.ap[:, 0:nq], in0=t1.ap[:, 0:nq], scalar=v.ap[:, 0:1], in1=t0.ap[:, 0:nq], op0=ALU.mult, op1=ALU.add),
                      reads=[t0.b, t1.b, v.b], writes=[oraw.b])
                S.add("act", "activation", dict(out=sq.ap[:, 0:nq], in_=oraw.ap[:, 0:nq], func=AF.Square), reads=[oraw.b], writes=[sq.b])

                def stage_b():
                    S.add("pe", "matmul", dict(out=PS[7].ap[:, 0:nq], lhsT=self.ones.ap[:, :], rhs=sq.ap[:, 0:nq], start=True, stop=True), reads=[sq.b, self.ones.b], writes=[PS[7].b])
                    S.add("act", "activation", dict(out=lnr.ap[:, 0:nq], in_=PS[7].ap[:, 0:nq], func=AF.Ln, scale=1.0 / 128, bias=EPS), reads=[PS[7].b], writes=[lnr.b])
                    S.add("act", "activation", dict(out=lnr.ap[:, 0:nq], in_=lnr.ap[:, 0:nq], func=AF.Exp, scale=-0.5), reads=[lnr.b], writes=[lnr.b])
                    S.add("dve", "scalar_tensor_tensor", dict(out=OA[:, h, c0:c0 + nq], in0=oraw.ap[:, 0:nq], scalar=v.ap[:, 1:2], in1=lnr.ap[:, 0:nq], op0=ALU.mult, op1=ALU.mult),
                          reads=[oraw.b, lnr.b, v.b], writes=[self.OAb[sd.obi[bi]]])
                defer(5, stage_b)

            LOOK = 2
            for i, u in enumerate(units):
                pend.append((u, issue_S(u)))
                if len(pend) > LOOK:
                    issue_AV(*pend.pop(0))
                    tick()
            while pend:
                issue_AV(*pend.pop(0))
                tick()
            tick(flush=True)

    def fin_pass(self, sd, reload=True):
        S = self.S
        SL = self.SL
        self.new_arena([self.U[:, 40960:self.U_N]] if self.U_N - 40960 > 1024 else [])
        ar = self.ar
        if reload:
            self.wb = Buf("w3")
        U = self.U
        wz = U[:, 0:4096].rearrange("p (c n) -> p c n", n=512)
        wg = U[:, 4096:24576].rearrange("p (c n) -> p c n", n=2560)
        wba = U[:, 24576:28672].rearrange("p (c n) -> p c n", n=1024)
        wbb = U[:, 28672:32768].rearrange("p (c n) -> p c n", n=1024)
        wout = U[:, 32768:40960].rearrange("p (c n) -> p c n", n=1024)
        if reload:
            self.load_w(wz, self.w_in, 8, C_DAZ, C_DAZ + 512)
            self.load_w(wg, self.w_in, 8, C_MZ, IN_COLS)
            self.load_w(wba, self.i_wba, 4, 0, 1024)
            self.load_w(wbb, self.i_wbb, 4, 0, 1024)
            self.load_w(wout, self.i_wout, 8, 0, 1024)
        gf = ar.alloc([D], F32, "gf")
        S.dma("sp", "dma_start", dict(out=gf.ap[:], in_=self.i_gf.partition_broadcast(128)), self.ds_const, writes=[gf.b])
        hT = ar.alloc([8, 512], BF16, "hT")
        oz = ar.alloc([8, 512], BF16, "oz")
        mT = ar.alloc([8, 512], BF16, "mT")
        sg = Ring([ar.alloc([512], F32, f"sg{i}") for i in range(2)])
        tg = Ring([ar.alloc([512], F32, f"tg{i}") for i in range(2)])
        PS = self.PS
        OA = sd.OA
        OB = sd.OB
        zb = Ring([PS[i] for i in range(1, 8)])
        ojunk = [None]
        pro_x = Ring([self.xring.items[0]])
        out_x = Ring([self.xring.items[1]])
        for bi, blk in enumerate(sd.blocks):
            nq = sum(s.n for s in blk)
            c0 = blk[0].c0
            self.xring = pro_x
            if sd.prompt and bi >= 1:
                c0p = sd.blocks[bi - 1][0].c0
                hch = [OA[:, c, c0p:c0p + 512] for c in range(4)] + [OB[:, c, c0p:c0p + 512] for c in range(4)]
                hbufs = [self.OAb[bi - 1], self.OBb[bi - 1]]
                for j, stl in enumerate(blk):
                    self.prologue(sd, stl, None, None, outs=[(OA[:, :, c0p + j * 128:c0p + j * 128 + stl.n], 0, 4, [self.OAb[bi - 1]]),
                                                             (OB[:, :, c0p + j * 128:c0p + j * 128 + stl.n], 4, 8, [self.OBb[bi - 1]])])
            else:
                hch = [hT.ap[:, c, :] for c in range(8)]
                hbufs = [hT.b]
                for j, stl in enumerate(blk):
                    self.prologue(sd, stl, hT.ap[:, :, j * 128:j * 128 + stl.n], hT.b)
            for m in range(8):
                bank = zb.next()
                wsrc = wz[:, :, m * 128:(m + 1) * 128] if m < 4 else wg[:, :, (m - 4) * 128:(m - 3) * 128]
                for c in range(8):
                    S.add("pe", "matmul", dict(out=bank.ap[:, 0:nq], lhsT=wsrc[:, c, :], rhs=hch[c][:, 0:nq], start=(c == 0), stop=(c == 7)),
                          reads=hbufs + [self.wb], writes=[bank.b])
                s_ = sg.next()
                S.add("act", "activation", dict(out=s_.ap[:, 0:nq], in_=bank.ap[:, 0:nq], func=AF.Silu), reads=[bank.b], writes=[s_.b])
                osrc = OA[:, m, c0:c0 + nq] if m < 4 else OB[:, m - 4, c0:c0 + nq]
                ob = [self.OAb[sd.obi[bi]]] if m < 4 else [self.OBb[sd.obi[bi]]]
                S.add("dve", "tensor_tensor", dict(out=oz.ap[:, m, 0:nq], in0=s_.ap[:, 0:nq], in1=osrc, op=ALU.mult), reads=[s_.b] + ob, writes=[oz.b])
            for m in range(8):
                bya, byb, bga, bgb = zb.next(), zb.next(), zb.next(), zb.next()
                for c in range(4):
                    S.add("pe", "matmul", dict(out=bya.ap[:, 0:nq], lhsT=wba[:, c, m * 128:(m + 1) * 128], rhs=oz.ap[:, c, 0:nq], start=(c == 0), stop=(c == 3)),
                          reads=[oz.b, self.wb], writes=[bya.b])
                for c in range(4):
                    S.add("pe", "matmul", dict(out=byb.ap[:, 0:nq], lhsT=wbb[:, c, m * 128:(m + 1) * 128], rhs=oz.ap[:, 4 + c, 0:nq], start=(c == 0), stop=(c == 3)),
                          reads=[oz.b, self.wb], writes=[byb.b])
                for c in range(8):
                    S.add("pe", "matmul", dict(out=bga.ap[:, 0:nq], lhsT=wg[:, c, 512 + m * 128:512 + (m + 1) * 128], rhs=hch[c][:, 0:nq], start=(c == 0), stop=(c == 7)),
                          reads=hbufs + [self.wb], writes=[bga.b])
                for c in range(8):
                    S.add("pe", "matmul", dict(out=bgb.ap[:, 0:nq], lhsT=wg[:, c, 1536 + m * 128:1536 + (m + 1) * 128], rhs=hch[c][:, 0:nq], start=(c == 0), stop=(c == 7)),
                          reads=hbufs + [self.wb], writes=[bgb.b])
                ga, gb_ = sg.next(), sg.next()
                S.add("act", "activation", dict(out=ga.ap[:, 0:nq], in_=bga.ap[:, 0:nq], func=AF.Sigmoid, bias=self.gateb.ap[:, m:m + 1]), reads=[bga.b, self.gateb.b], writes=[ga.b])
                S.add("act", "activation", dict(out=gb_.ap[:, 0:nq], in_=bgb.ap[:, 0:nq], func=AF.Sigmoid, bias=self.gateb.ap[:, 8 + m:9 + m]), reads=[bgb.b, self.gateb.b], writes=[gb_.b])
                ta, tb = tg.next(), tg.next()
                S.add("dve", "tensor_tensor", dict(out=ta.ap[:, 0:nq], in0=ga.ap[:, 0:nq], in1=bya.ap[:, 0:nq], op=ALU.mult), reads=[ga.b, bya.b], writes=[ta.b])
                S.add("dve", "tensor_tensor", dict(out=tb.ap[:, 0:nq], in0=gb_.ap[:, 0:nq], in1=byb.ap[:, 0:nq], op=ALU.mult), reads=[gb_.b, byb.b], writes=[tb.b])
                S.add("pool", "tensor_tensor", dict(out=mT.ap[:, m, 0:nq], in0=ta.ap[:, 0:nq], in1=tb.ap[:, 0:nq], op=ALU.add), reads=[ta.b, tb.b], writes=[mT.b])
            if sd.prompt and bi == 0 and len(sd.blocks) > 1:
                x2 = R(hT.ap[:, 0:4, :].rearrange("p a b -> p (a b)").bitcast(F32), "xt2")
                x2.b.w = hT.b.w
                x2.b.rs = list(hT.b.rs)
                out_x.items.append(x2)
                jk = R(hT.ap[:, 4:6, :].rearrange("p a b -> p (a b)"), "ojunk")
                jk.b.w = hT.b.w
                jk.b.rs = list(hT.b.rs)
                ojunk[0] = jk
            self.xring = out_x
            for j, stl in enumerate(blk):
                n = stl.n
                xt = self.load_x(sd, stl, self.dr_x2)
                for hf in range(2):
                    bank = zb.next()
                    for c in range(8):
                        S.add("pe", "matmul", dict(out=bank.ap[0:n, :], lhsT=mT.ap[:, c, j * 128:j * 128 + n], rhs=wout[:, c, hf * 512:(hf + 1) * 512],
                                                                                  start=(c == 0), stop=(c == 7)), reads=[mT.b, self.wb], writes=[bank.b])
                    S.add("dve", "tensor_tensor", dict(out=xt.ap[0:n, hf * 512:(hf + 1) * 512], in0=xt.ap[0:n, hf * 512:(hf + 1) * 512], in1=bank.ap[0:n, :], op=ALU.add),
                          reads=[bank.b, xt.b], writes=[xt.b])
                stt = self.rstd(xt.ap[0:n, :], [xt.b], n, D, ojunk[0] if ojunk[0] is not None else self.hbring.next())
                S.add("dve", "scalar_tensor_tensor", dict(out=xt.ap[0:n, :], in0=xt.ap[0:n, :], scalar=stt.ap[0:n, 2:3], in1=gf.ap[0:n, :], op0=ALU.mult, op1=ALU.mult),
                      reads=[xt.b, stt.b, gf.b], writes=[xt.b])
                S.dma("pool", "dma_start", dict(out=sd.o_y[stl.row0:stl.row0 + stl.n, :], in_=xt.ap[0:stl.n, :]), self.dr_o["y"].next(), reads=[xt.b])


def rope_tables(SL, sample):
    ntt = SL // 128 + 1
    pos = np.zeros((128, ntt), np.float64)
    for t in range(ntt - 1):
        pos[:, t] = t * 128 + np.arange(128)
    pos[:, ntt - 1] = 1024 + np.arange(128)
    out = {}
    for name, rot in (("D", 16), ("M", 32)):
        half = rot // 2
        inv = (np.float32(500000.0) ** (-np.arange(half, dtype=np.float32) * np.float32(2.0) / np.float32(rot))).astype(np.float32)
        ang = (pos.astype(np.float32)[:, :, None] * inv[None, None, :]).astype(np.float32)
        out["cos" + name] = np.cos(ang.astype(np.float64)).astype(np.float32).reshape(128, ntt * half)
        out["sin" + name] = np.sin(ang.astype(np.float64)).astype(np.float32).reshape(128, ntt * half)
    return out


_CACHE = {}


def get_nc(NSEQ, SL, SAMPLE, parts=("mla", "da", "fin")):
    key = (NSEQ, SL, SAMPLE, parts)
    if key not in _CACHE:
        b = Builder(NSEQ, SL, SAMPLE, parts)
        nc = b.build()
        _CACHE[key] = (nc, b)
    return _CACHE[key]


def shared_inputs(inp, SL):
    f = lambda a: np.ascontiguousarray(np.asarray(a, dtype=np.float32))
    sh = {
        "w_in": f(inp["w_in"][0]),
        "norm_g": f(inp["norm_g"][0]).reshape(1, D),
        "gate_bT": f(np.asarray(inp["gate_b"][0]).reshape(16, 128).T),
        "da_lambda": f(inp["da_lambda"][0]).reshape(1, 256),
        "hng": f(inp["da_head_norm_g"][0]).reshape(128, 1),
        "gq": f(inp["mla_q_norm_g"][0]).reshape(1, 384),
        "gkv": f(inp["mla_kv_norm_g"][0]).reshape(1, 256),
        "w_uq": f(inp["mla_w_uq"][0]),
        "w_uk": f(inp["mla_w_uk"][0]),
        "w_uv": f(np.asarray(inp["mla_w_uv"][0]).reshape(256, 8, 64).transpose(0, 2, 1).reshape(256, 512)),
        "w_ba": f(inp["w_branch_a"][0]),
        "w_bb": f(inp["w_branch_b"][0]),
        "w_out": f(inp["w_out"][0]),
        "gf": f(inp["final_norm_g"]).reshape(1, D),
        "ident": np.eye(128, dtype=np.float32),
        "shiftm": np.eye(128, k=64, dtype=np.float32),
    }
    sh.update(rope_tables(SL, True))
    return sh


def kernel(**inputs):
    NCORES = 8
    xp = np.asarray(inputs["x_prompt"], dtype=np.float32)
    xs = np.asarray(inputs["x_sample"], dtype=np.float32)
    B, SL, _ = xp.shape
    NSEQ = B // NCORES
    nc, _ = get_nc(NSEQ, SL, True)
    sh = shared_inputs(inputs, SL)
    cdk = np.asarray(inputs["cache_da_k"], dtype=np.float32)[0]
    cdv = np.asarray(inputs["cache_da_v"], dtype=np.float32)[0]
    clat = np.asarray(inputs["cache_mla_latent"], dtype=np.float32)[0]
    ckr = np.asarray(inputs["cache_mla_krope"], dtype=np.float32)[0]
    in_maps = []
    for c in range(NCORES):
        m = dict(sh)
        m["xp"] = np.ascontiguousarray(xp[c * NSEQ:(c + 1) * NSEQ].reshape(NSEQ * SL, D))
        m["xs"] = np.ascontiguousarray(xs[c])
        m["cdk"] = np.ascontiguousarray(cdk[c].reshape(1024, 512))
        m["cdv"] = np.ascontiguousarray(cdv[c].reshape(1024, 512))
        m["clat"] = np.ascontiguousarray(clat[c])
        m["ckr"] = np.ascontiguousarray(ckr[c])
        in_maps.append(m)
    res = run_bass_kernel_spmd(nc, in_maps, core_ids=list(range(NCORES))).results
    cat = lambda k: np.concatenate([np.asarray(r[k]) for r in res], axis=0)
    y_p = cat("yp").reshape(B, SL, D)
    y_s = cat("ys").reshape(NCORES, 64, D)
    k_p = cat("kp").reshape(1, B, SL, 4, 128)
    v_p = cat("vp").reshape(1, B, SL, 4, 128)
    lat_p = cat("latp").reshape(1, B, SL, 256)
    kr_p = cat("krp").reshape(1, B, SL, 32)
    k_s = cat("ks").reshape(1, NCORES, 64, 4, 128)
    v_s = cat("vs").reshape(1, NCORES, 64, 4, 128)
    lat_s = cat("lats").reshape(1, NCORES, 64, 256)
    kr_s = cat("krs").reshape(1, NCORES, 64, 32)
    return tuple(np.ascontiguousarray(a, dtype=np.float32) for a in (y_p, y_s, k_p, v_p, lat_p, kr_p, k_s, v_s, lat_s, kr_s))
```

```python
import math
import numpy as np
from contextlib import ExitStack
import concourse.bass as bass
import concourse.mybir as mybir
from concourse.bass_utils import run_bass_kernel_spmd

F32 = mybir.dt.float32
BF16 = mybir.dt.bfloat16
AF = mybir.ActivationFunctionType
ALU = mybir.AluOpType

D = 1024
SEM_ROT = 12000
SCL = {}
EPS = 1e-6
import os
STOP = int(os.environ.get('KSTOP', '99'))
SUB = int(os.environ.get('KSUB', '99'))
C_DAQ, C_DAK, C_DAV, C_DAZ, C_CQ, C_CKV, C_KR, C_MZ, C_G = 0, 512, 1024, 1536, 2048, 2432, 2688, 2720, 3232
IN_COLS = 5280
LAM_INIT = 0.8 - 0.6 * math.exp(-0.3 * 0)
SC_DA = 64 ** -0.5
SC_MLA = 96 ** -0.5


class Buf:
    __slots__ = ("name", "w", "rs", "excl")

    def __init__(self, name="", excl=False):
        self.name = name
        self.w = None
        self.rs = []
        self.excl = excl


class DmaSem:
    def __init__(self, sem):
        self.sem = sem
        self.count = 0
        self.last_group = None


class DmaGroup:
    def __init__(self, ds):
        self.ds = ds
        self.final = None
        self.last_op = None


class Op:
    __slots__ = ("eng", "name", "kw", "deps", "signal", "sem", "val", "idx", "group", "is_dma", "gidx", "region", "fin")

    def __init__(self, eng, name, kw):
        self.eng = eng
        self.name = name
        self.kw = kw
        self.deps = []
        self.signal = False
        self.sem = None
        self.val = None
        self.idx = None
        self.group = None
        self.is_dma = False


class Sched:
    ENGS = ("pe", "act", "dve", "pool", "sp")

    def __init__(self, nc, stack):
        self.nc = nc
        self.ops = {e: [] for e in self.ENGS}
        self.dma_sems = []
        self._stack = stack
        self.bar_deps = []
        self.bar_seen = {e: True for e in self.ENGS}
        self.all_ops = []
        self.region = 0

    def new_sem(self, name):
        return self._stack.enter_context(self.nc.semaphore(name))

    def dma_sem(self, name):
        ds = DmaSem(self.new_sem(name))
        self.dma_sems.append(ds)
        return ds

    def dma_ring(self, name, n):
        return DmaRing([self.dma_sem(f"{name}{i}") for i in range(n)])

    def barrier(self):
        self.region += 1

    def _collect(self, op, reads, writes, extra):
        deps = []
        for b in reads:
            if b.w is not None:
                deps.append(b.w)
            if b.excl:
                for r in b.rs:
                    if r.eng != op.eng:
                        deps.append(r)
        for b in writes:
            if b.w is not None:
                deps.append(b.w)
            deps.extend(b.rs)
        deps.extend(extra)
        seen = set()
        out = []
        for d in deps:
            if d is op or id(d) in seen:
                continue
            seen.add(id(d))
            out.append(d)
        op.deps = out
        op.gidx = len(self.all_ops)
        op.region = self.region
        self.all_ops.append(op)
        for b in writes:
            b.w = op
            b.rs = []
        for b in reads:
            b.rs.append(op)

    def add(self, eng, name, kw, reads=(), writes=(), extra=()):
        op = Op(eng, name, kw)
        self._collect(op, reads, writes, extra)
        self.ops[eng].append(op)
        return op

    def dma(self, eng, name, kw, ds, reads=(), writes=(), extra=()):
        op = Op(eng, name, kw)
        op.is_dma = True
        extra = list(extra)
        g = DmaGroup(ds)
        if ds.last_group is not None:
            extra.append(ds.last_group.last_op)
        ds.last_group = g
        ds.count += 1
        g.final = 16 * ds.count
        g.last_op = op
        op.group = g
        op.sem = ds.sem
        op.signal = True
        self._collect(op, reads, writes, extra)
        self.ops[eng].append(op)
        return op

    @staticmethod
    def _fsize(ap):
        n = 1
        for x in ap.shape[1:]:
            n *= x
        return n

    def _dur(self, op):
        return self._dur0(op) * SCL.get("dma" if op.is_dma else op.eng, 1.0)

    def _dur0(self, op):
        kw = op.kw
        if op.is_dma:
            o = kw["out"]
            nbytes = self._fsize(o) * o.shape[0] * (4 if o.dtype == F32 else 2)
            return 2200.0 + nbytes / 120.0
        if op.eng == "pe":
            if op.name == "transpose":
                return 70.0
            return 8.0 + 0.41 * self._fsize(kw["rhs"])
        a = kw.get("in_", kw.get("in0", kw.get("out", kw.get("ap"))))
        f = self._fsize(a)
        if op.eng == "act":
            if os.environ.get("KEXP2") and f == 512 and kw.get("func") == AF.Exp:
                return (190.0 + 1024 / 1.2) / 2
            return 190.0 + f / 1.2 + (100.0 if "accum_out" in kw else 0.0)
        if op.eng == "dve":
            if op.name == "reciprocal":
                return 80.0 + 6.6 * f
            return 100.0 + f / 0.85
        return 200.0 + f / 0.48

    def schedule(self, dry=False, beta=None):
        import heapq
        if beta is None:
            beta = float(os.environ.get("KBETA", "0.3"))
        regions = {}
        for op in self.all_ops:
            regions.setdefault(op.region, []).append(op)
        new_ops = {e: [] for e in self.ENGS}
        t0 = 0.0
        tail = []
        for r in sorted(regions):
            ops = regions[r]
            reg_first = {}
            reg_last = {}
            dma_last = {}
            inreg = set(id(o) for o in ops)
            succ = {}
            indeg = {}
            ready = {}
            for o in ops:
                cnt = 0
                for d in o.deps:
                    if id(d) in inreg:
                        cnt += 1
                        succ.setdefault(id(d), []).append(o)
                indeg[id(o)] = cnt
                ready[id(o)] = t0
            bl = {}
            if beta:
                for o in reversed(ops):
                    m = 0.0
                    for sc in succ.get(id(o), ()):
                        v = bl[id(sc)]
                        if v > m:
                            m = v
                    bl[id(o)] = m + self._dur(o)
            heap = [(t0 - beta * bl.get(id(o), 0.0), o.gidx, o) for o in ops if indeg[id(o)] == 0]
            heapq.heapify(heap)
            free = {e: t0 for e in self.ENGS}
            tmax = t0
            while heap:
                _, _, o = heapq.heappop(heap)
                rt = ready[id(o)]
                st = max(rt, free[o.eng])
                if o.is_dma:
                    issue = 1000.0 if o.eng == "pool" else 80.0
                    free[o.eng] = st + issue
                    fin = st + issue + self._dur(o)
                else:
                    fin = st + self._dur(o)
                    free[o.eng] = fin
                o.fin = fin
                tmax = max(tmax, fin)
                new_ops[o.eng].append(o)
                if o.eng not in reg_first:
                    reg_first[o.eng] = o
                if o.is_dma:
                    cur = dma_last.get(id(o.sem))
                    if cur is None or o.group.final > cur.group.final:
                        dma_last[id(o.sem)] = o
                else:
                    reg_last[o.eng] = o
                for sc in succ.get(id(o), ()):
                    if o.eng == "pe" and sc.eng == "pe" and not sc.is_dma and not o.is_dma:
                        lat = 0.0
                    elif o.eng == sc.eng and not o.is_dma:
                        lat = 60.0
                    else:
                        lat = 150.0
                    lat *= SCL.get("lat", 1.0)
                    ready[id(sc)] = max(ready[id(sc)], fin + lat)
                    indeg[id(sc)] -= 1
                    if indeg[id(sc)] == 0:
                        heapq.heappush(heap, (ready[id(sc)] - beta * bl.get(id(sc), 0.0), sc.gidx, sc))
            if not dry:
                for e, o in reg_first.items():
                    have = set(id(d) for d in o.deps)
                    o.deps = list(o.deps) + [d for d in tail if id(d) not in have and d is not o]
            tail = list(reg_last.values()) + list(dma_last.values())
            if os.environ.get("KDEBUG"):
                busy = {}
                for o in ops:
                    if not o.is_dma:
                        busy[o.eng] = busy.get(o.eng, 0.0) + self._dur(o)
                print("region", r, "ops", len(ops), "dur us", round((tmax - t0) / 1000, 1), {k: round(v / 1000) for k, v in busy.items()})
            t0 = tmax
        assert sum(len(v) for v in new_ops.values()) == len(self.all_ops)
        if dry:
            return t0
        self.ops = new_ops
        self.est_ns = t0

    def finalize(self):
        if os.environ.get("KSCHED", "1") == "1":
            self.schedule()
        for e in self.ENGS:
            for i, op in enumerate(self.ops[e]):
                op.idx = i
        for e in self.ENGS:
            for op in self.ops[e]:
                best = {}
                out = []
                seen_groups = set()
                for d in op.deps:
                    if d.is_dma:
                        if id(d.group) not in seen_groups:
                            seen_groups.add(id(d.group))
                            out.append(d)
                    else:
                        if d.eng == "pe" and op.eng == "pe" and not op.is_dma:
                            continue
                        cur = best.get(d.eng)
                        if cur is None or d.idx > cur.idx:
                            best[d.eng] = d
                out.extend(best.values())
                for d in out:
                    d.signal = True
                op.deps = out
        for e in ("pe", "act", "dve", "pool"):
            nsem = 0
            cnt = 0
            cur = None
            for op in self.ops[e]:
                if op.is_dma or not op.signal:
                    continue
                if cur is None or cnt >= SEM_ROT:
                    cur = self.new_sem(f"s_{e}{nsem}")
                    nsem += 1
                    cnt = 0
                cnt += 1
                op.sem = cur
                op.val = cnt
        for e in self.ENGS:
            for op in self.ops[e]:
                if op.is_dma:
                    op.val = op.group.final

    def emit(self, block):
        self.finalize()
        stats = {}

        def run(e):
            def body(eng):
                seen = {}
                nw = 0
                for op in self.ops[e]:
                    for d in op.deps:
                        k = id(d.sem)
                        if seen.get(k, 0) >= d.val:
                            continue
                        seen[k] = d.val
                        eng.wait_ge(d.sem, d.val)
                        nw += 1
                    ins = getattr(eng, op.name)(**op.kw)
                    if op.signal:
                        ins.then_inc(op.sem, 16 if op.is_dma else 1)
                if e == "sp":
                    for ds in self.dma_sems:
                        if ds.count:
                            eng.wait_ge(ds.sem, 16 * ds.count)
                stats[e] = (len(self.ops[e]), nw)
            return body

        block.tensor(run("pe"))
        block.scalar(run("act"))
        block.vector(run("dve"))
        block.gpsimd(run("pool"))
        block.sync(run("sp"))
        return stats


class DmaRing:
    def __init__(self, sems):
        self.sems = sems
        self.i = 0

    def next(self):
        s = self.sems[self.i % len(self.sems)]
        self.i += 1
        return s


class R:
    __slots__ = ("ap", "b")

    def __init__(self, ap, name=""):
        self.ap = ap
        self.b = Buf(name)


class Ring:
    def __init__(self, items):
        self.items = items
        self.i = 0

    def next(self):
        r = self.items[self.i % len(self.items)]
        self.i += 1
        return r


class Arena:
    def __init__(self, regions):
        self.regions = regions
        self.reset()

    def reset(self):
        self.off = [0 for _ in self.regions]

    def alloc(self, shape, dtype, name=""):
        n = 1
        for s in shape:
            n *= s
        esz = 4 if dtype == F32 else 2
        nel = n * esz // 2
        for i, reg in enumerate(self.regions):
            o = (self.off[i] + 1) // 2 * 2
            if o + nel <= reg.shape[1]:
                self.off[i] = o + nel
                ap = reg[:, o:o + nel]
                if dtype != BF16:
                    ap = ap.bitcast(dtype)
                if len(shape) == 2:
                    ap = ap.rearrange("p (a b) -> p a b", b=shape[1])
                elif len(shape) == 3:
                    ap = ap.rearrange("p (a b c) -> p a b c", b=shape[1], c=shape[2])
                return R(ap, name)
        raise RuntimeError(f"arena overflow allocating {name} {shape}; offs={self.off}")


def run_skewed(stages, items):
    n, ns = len(items), len(stages)
    for step in range(n + ns - 1):
        for si in range(ns):
            j = step - si
            if 0 <= j < n:
                stages[si](items[j])


class SubTile:
    def __init__(self, row0, n, kt, tt, c0):
        self.row0, self.n, self.kt, self.tt, self.c0 = row0, n, kt, tt, c0


class SeqDesc:
    pass


class Builder:
    def __init__(self, NSEQ, SL, SAMPLE, parts=("mla", "da", "fin")):
        self.NSEQ, self.SL, self.SAMPLE = NSEQ, SL, SAMPLE
        self.parts = parts
        self.TK = max(SL, 1152 if SAMPLE else 0)
        self.NTT = SL // 128 + 1

    def dram(self, name, shape, kind="ExternalInput"):
        return self.nc.dram_tensor(name, list(shape), F32, kind=kind).ap()

    def sb(self, name, shape, dtype):
        return self.st.enter_context(self.nc.sbuf_tensor("sb_" + name, list(shape), dtype))

    def build(self):
        nc = self.nc = bass.Bass("TRN2", target_bir_lowering=False)
        NSEQ, SL, TK = self.NSEQ, self.SL, self.TK
        NP = NSEQ * SL
        dr = self.dram
        self.xp = dr("xp", [NP, D])
        self.w_in = dr("w_in", [D, IN_COLS])
        self.i_normg = dr("norm_g", [1, D])
        self.i_gateb = dr("gate_bT", [128, 16])
        self.i_lam = dr("da_lambda", [1, 256])
        self.i_hng = dr("hng", [128, 1])
        self.i_gq = dr("gq", [1, 384])
        self.i_gkv = dr("gkv", [1, 256])
        self.i_wuq = dr("w_uq", [384, 768])
        self.i_wuk = dr("w_uk", [256, 512])
        self.i_wuv = dr("w_uv", [256, 512])
        self.i_wba = dr("w_ba", [512, D])
        self.i_wbb = dr("w_bb", [512, D])
        self.i_wout = dr("w_out", [D, D])
        self.i_gf = dr("gf", [1, D])
        self.i_ident = dr("ident", [128, 128])
        self.i_shift = dr("shiftm", [128, 128])
        NTT = self.NTT
        self.i_cosD = dr("cosD", [128, NTT * 8])
        self.i_sinD = dr("sinD", [128, NTT * 8])
        self.i_cosM = dr("cosM", [128, NTT * 16])
        self.i_sinM = dr("sinM", [128, NTT * 16])
        o = lambda n, s: dr(n, s, kind="ExternalOutput")
        self.o_y = o("yp", [NP, D])
        self.o_k = o("kp", [NP, 512])
        self.o_v = o("vp", [NP, 512])
        self.o_lat = o("latp", [NP, 256])
        self.o_kr = o("krp", [NP, 32])
        if self.SAMPLE:
            self.xs = dr("xs", [64, D])
            self.cdk = dr("cdk", [1024, 512])
            self.cdv = dr("cdv", [1024, 512])
            self.clat = dr("clat", [1024, 256])
            self.ckr = dr("ckr", [1024, 32])
            self.o_ys = o("ys", [64, D])
            self.o_ks = o("ks", [64, 512])
            self.o_vs = o("vs", [64, 512])
            self.o_lats = o("lats", [64, 256])
            self.o_krs = o("krs", [64, 32])

        with ExitStack() as st:
            self.st = st
            S = self.S = Sched(nc, st)
            sb = self.sb
            self.U_N = max(8 * TK + (TK // 128 + 1) * 520, 8 * TK + 12288, 40960)
            self.U = sb("U", [128, self.U_N], BF16)
            self.SLX = SL
            SLX = self.SLX
            self.OB = sb("OB", [128, 4, SLX], BF16)
            self.OA_N = max(4 * SLX, 9728)
            self.OA = sb("OA", [128, self.OA_N], BF16)
            self.OBb = [Buf(f"OB{i}") for i in range(SL // 512 + 1)]
            self.OAb = [Buf(f"OA{i}") for i in range(SL // 512 + 1)]
            self.Ub = Buf("U")
            self.ident = R(sb("ident", [128, 128], BF16))
            self.ones = R(sb("ones", [128, 128], BF16))
            self.shiftm = R(sb("shiftm", [128, 128], BF16))
            self.sel64 = R(sb("sel64", [128, 64], BF16))
            self.cosD = R(sb("cosD", [128, NTT, 8], F32))
            self.sinD = R(sb("sinD", [128, NTT, 8], F32))
            self.cosM = R(sb("cosM", [128, NTT, 16], F32))
            self.sinM = R(sb("sinM", [128, NTT, 16], F32))
            self.g_in = R(sb("g_in", [128, D], F32))
            self.gateb = R(sb("gateb", [128, 16], F32))
            self.vec = R(sb("vec", [128, 16], F32))
            self.lamt = R(sb("lamt", [128, 256], F32))
            self.stats = Ring([R(sb(f"stat{i}", [128, 4], F32)) for i in range(6)])
            self.cstage = R(sb("cstage", [128, 128], F32))
            rem = nc.sbuf_bytes_remaining
            tn = (rem - 64) // 2 // 2 * 2
            self.Tt = sb("T", [128, tn], BF16)
            self.PSALL = st.enter_context(nc.psum_tensor("psall", [128, 4096], F32))
            self.PS = [R(self.PSALL[:, i * 512:(i + 1) * 512], f"ps{i}") for i in range(8)]
            for r_ in self.PS:
                r_.b.excl = True
            self.ds_const = S.dma_sem("dconst")
            self.dr_w = S.dma_ring("dw", 4)
            self.dr_x = S.dma_ring("dx", 2)
            self.dr_x2 = S.dma_ring("dxo", 2)
            self.dr_o = {k: S.dma_ring("do" + k, 2) for k in ("k", "v", "lat", "kr", "y")}
            self.ds_cache = S.dma_sem("dcache")

            self.consts()
            seqs = []
            for s in range(NSEQ):
                sd = SeqDesc()
                sd.x = self.xp
                sd.prompt = True
                sd.ncache = 0
                sd.blocks = []
                for b in range(SL // 512):
                    sd.blocks.append([SubTile(s * SL + (4 * b + j) * 128, 128, 4 * b + j, 4 * b + j, (4 * b + j) * 128) for j in range(4)])
                sd.o_y, sd.o_k, sd.o_v, sd.o_lat, sd.o_kr = self.o_y, self.o_k, self.o_v, self.o_lat, self.o_kr
                sd.obi = list(range(SL // 512))
                sd.OB = self.OB
                sd.OA = self.OA[:, 0:4 * SL].rearrange("p (h t) -> p h t", t=SL)
                seqs.append(sd)
            if self.SAMPLE:
                sd = SeqDesc()
                sd.x = self.xs
                sd.prompt = False
                sd.ncache = 8
                sd.blocks = [[SubTile(0, 64, 8, NTT - 1, 0)]]
                sd.obi = [SL // 512]
                lb = self.lamt.ap[:, :].bitcast(BF16)
                sd.OB = lb[:, 0:256].rearrange("p (h t) -> p h t", t=64)
                sd.OA = lb[:, 256:512].rearrange("p (h t) -> p h t", t=64)
                sd.o_y, sd.o_k, sd.o_v, sd.o_lat, sd.o_kr = self.o_ys, self.o_ks, self.o_vs, self.o_lats, self.o_krs
                seqs.append(sd)
            prompts = [q for q in seqs if q.prompt]
            smp = [q for q in seqs if not q.prompt]
            groups = [[q] for q in prompts[:-1]] + [prompts[-1:] + smp]
            for grp in groups:
                for pname, fn in (("mla", self.mla_pass), ("da", self.da_pass), ("fin", self.fin_pass)):
                    if pname in self.parts:
                        for gi, sd in enumerate(grp):
                            fn(sd, reload=(gi == 0))
            with nc.allow_low_precision(reason="bf16 matmul operands by design"), nc.Block() as block:
                self.stats_out = S.emit(block)
        return nc

    def consts(self):
        S = self.S
        ds = self.ds_const
        cs = self.cstage
        for src, dst in ((self.i_ident, self.ident), (self.i_shift, self.shiftm)):
            S.dma("sp", "dma_start", dict(out=cs.ap[:], in_=src), ds, writes=[cs.b])
            S.add("dve", "tensor_copy", dict(out=dst.ap[:], in_=cs.ap[:]), reads=[cs.b], writes=[dst.b])
        S.add("pool", "memset", dict(ap=self.ones.ap[:], constant=1.0), writes=[self.ones.b])
        S.add("pool", "memset", dict(ap=self.sel64.ap[:], constant=0.0), writes=[self.sel64.b])
        S.add("pool", "memset", dict(ap=self.sel64.ap[64:65, :], constant=1.0), writes=[self.sel64.b])
        NTT = self.NTT
        for src, dst, k in ((self.i_cosD, self.cosD, 8), (self.i_sinD, self.sinD, 8), (self.i_cosM, self.cosM, 16), (self.i_sinM, self.sinM, 16)):
            S.dma("sp", "dma_start", dict(out=dst.ap[:], in_=src.rearrange("p (t k) -> p t k", k=k)), ds, writes=[dst.b])
        S.dma("sp", "dma_start", dict(out=self.g_in.ap[:], in_=self.i_normg.partition_broadcast(128)), ds, writes=[self.g_in.b])
        S.dma("sp", "dma_start", dict(out=self.gateb.ap[:], in_=self.i_gateb), ds, writes=[self.gateb.b])
        S.dma("sp", "dma_start", dict(out=self.lamt.ap[:], in_=self.i_lam.partition_broadcast(128)), ds, writes=[self.lamt.b])
        v = self.vec
        S.dma("sp", "dma_start", dict(out=v.ap[:, 1:2], in_=self.i_hng), ds, writes=[v.b])
        lt = self.lamt
        S.add("dve", "tensor_tensor", dict(out=lt.ap[:, 0:64], in0=lt.ap[:, 0:64], in1=lt.ap[:, 64:128], op=ALU.mult), reads=[lt.b], writes=[lt.b])
        S.add("dve", "tensor_tensor", dict(out=lt.ap[:, 128:192], in0=lt.ap[:, 128:192], in1=lt.ap[:, 192:256], op=ALU.mult), reads=[lt.b], writes=[lt.b])
        S.add("dve", "tensor_reduce", dict(out=v.ap[:, 2:3], in_=lt.ap[:, 0:64], op=ALU.add, axis=mybir.AxisListType.X), reads=[lt.b], writes=[v.b])
        S.add("dve", "tensor_reduce", dict(out=v.ap[:, 3:4], in_=lt.ap[:, 128:192], op=ALU.add, axis=mybir.AxisListType.X), reads=[lt.b], writes=[v.b])
        S.add("act", "activation", dict(out=v.ap[:, 4:6], in_=v.ap[:, 2:4], func=AF.Exp), reads=[v.b], writes=[v.b])
        S.add("dve", "scalar_tensor_tensor", dict(out=v.ap[:, 0:1], in0=v.ap[:, 5:6], scalar=-LAM_INIT, in1=v.ap[:, 4:5], op0=ALU.add, op1=ALU.subtract), reads=[v.b], writes=[v.b])
        S.add("dve", "tensor_scalar", dict(out=v.ap[:, 1:2], in0=v.ap[:, 1:2], scalar1=(1.0 - LAM_INIT), scalar2=None, op0=ALU.mult), reads=[v.b], writes=[v.b])

    def new_arena(self, extra_regions=()):
        self.S.barrier()
        self.ar = Arena([self.Tt[:, :]] + list(extra_regions))
        ar = self.ar
        self.xring = Ring([ar.alloc([D], F32, f"xt{i}") for i in range(2)])
        self.hbring = Ring([ar.alloc([D], BF16, f"hb{i}") for i in range(2)])

    def rstd(self, src_ap, src_bufs, n, F, junk):
        S = self.S
        stt = self.stats.next()
        S.add("act", "activation", dict(out=junk.ap[0:n, 0:F], in_=src_ap, func=AF.Square, accum_out=stt.ap[0:n, 0:1]),
              reads=src_bufs, writes=[junk.b, stt.b])
        S.add("act", "activation", dict(out=stt.ap[0:n, 1:2], in_=stt.ap[0:n, 0:1], func=AF.Ln, scale=1.0 / F, bias=EPS),
              reads=[stt.b], writes=[stt.b])
        S.add("act", "activation", dict(out=stt.ap[0:n, 2:3], in_=stt.ap[0:n, 1:2], func=AF.Exp, scale=-0.5),
              reads=[stt.b], writes=[stt.b])
        return stt

    def load_x(self, sd, stl, ring=None):
        S = self.S
        xt = self.xring.next()
        n = stl.n
        S.dma("sp", "dma_start", dict(out=xt.ap[0:n, :], in_=sd.x[stl.row0:stl.row0 + n, :]), (ring or self.dr_x).next(), writes=[xt.b])
        return xt

    def prologue(self, sd, stl, hT_ap, hT_buf, copy_eng="act", outs=None):
        S = self.S
        n = stl.n
        xt = self.load_x(sd, stl)
        hb = self.hbring.next()
        stt = self.rstd(xt.ap[0:n, :], [xt.b], n, D, hb)
        S.add("dve", "scalar_tensor_tensor", dict(out=hb.ap[0:n, :], in0=xt.ap[0:n, :], scalar=stt.ap[0:n, 2:3], in1=self.g_in.ap[0:n, :],
                                                       op0=ALU.mult, op1=ALU.mult), reads=[xt.b, stt.b, self.g_in.b], writes=[hb.b])
        pb = self.PS[0]
        pbf = pb.ap[:, :].bitcast(BF16).rearrange("p (c t) -> p c t", t=128)
        for c in range(8):
            S.add("pe", "transpose", dict(out=pbf[:, c, 0:n], in_=hb.ap[0:n, c * 128:(c + 1) * 128], identity=self.ident.ap[0:n, 0:n]),
                  reads=[hb.b, self.ident.b], writes=[pb.b])
        if outs is None:
            outs = [(hT_ap, 0, 8, [hT_buf])]
        for ap_, lo, hi, bufs in outs:
            if copy_eng == "act":
                S.add("act", "copy", dict(out=ap_, in_=pbf[:, lo:hi, 0:n]), reads=[pb.b], writes=bufs)
            else:
                S.add("dve", "tensor_copy", dict(out=ap_, in_=pbf[:, lo:hi, 0:n]), reads=[pb.b], writes=bufs)
        return xt

    def proj_tok(self, hT, n, w_ap, ncols, bank):
        S = self.S
        for c in range(8):
            S.add("pe", "matmul", dict(out=bank.ap[0:n, 0:ncols], lhsT=hT.ap[:, c, 0:n], rhs=w_ap[:, c, :], start=(c == 0), stop=(c == 7)),
                  reads=[hT.b, self.wb], writes=[bank.b])

    def load_w(self, dst_ap, src_ap, K, c0, c1):
        S = self.S
        a = c0
        while a < c1:
            b = min(a + 1024, c1)
            S.dma("pool", "dma_start", dict(out=dst_ap[:, :, a - c0:b - c0], in_=src_ap[:, a:b].rearrange("(c p) n -> p c n", p=128)),
                  self.dr_w.next(), writes=[self.wb])
            a = b

    def rope(self, eng, src4, dst4, n, cos_ap, sin_ap, G, half, tmp, src_bufs, dst_bufs):
        S = self.S
        t4 = tmp.ap[0:n, 0:G * 2 * half].rearrange("p (g k) -> p g k", k=2 * half)
        cb = cos_ap.unsqueeze(1).broadcast_to([n, G, half])
        sbb = sin_ap.unsqueeze(1).broadcast_to([n, G, half])
        rb = src_bufs + [self.cosM.b, self.sinM.b, self.cosD.b, self.sinD.b]
        S.add(eng, "scalar_tensor_tensor", dict(out=t4[:, :, 0:half], in0=src4[:, :, half:2 * half], scalar=-1.0, in1=sbb, op0=ALU.mult, op1=ALU.mult),
              reads=rb, writes=[tmp.b])
        S.add(eng, "tensor_tensor", dict(out=t4[:, :, half:2 * half], in0=src4[:, :, 0:half], in1=sbb, op=ALU.mult), reads=rb, writes=[tmp.b])
        S.add(eng, "tensor_tensor", dict(out=dst4[:, :, 0:half], in0=src4[:, :, 0:half], in1=cb, op=ALU.mult), reads=rb, writes=dst_bufs)
        S.add(eng, "tensor_tensor", dict(out=dst4[:, :, half:2 * half], in0=src4[:, :, half:2 * half], in1=cb, op=ALU.mult), reads=rb, writes=dst_bufs)
        S.add(eng, "tensor_tensor", dict(out=dst4, in0=dst4, in1=t4, op=ALU.add), reads=[tmp.b] + dst_bufs, writes=dst_bufs)

    def key_tiles(self, sd, blk, bi):
        if sd.prompt:
            out = [(kt, 128, 0, False) for kt in range(4 * bi)]
            for j in range(4):
                out.append((4 * bi + j, 128, 128 * j, True))
            return out
        return [(kt, 128, 0, False) for kt in range(sd.ncache)] + [(sd.ncache, 64, 0, False)]

    def mla_pass(self, sd, reload=True):
        S = self.S
        TK = self.TK
        NKT = TK // 128
        w1 = self.OA[:, 0:9728]
        regs = [self.OA[:, 9728:self.OA_N]] if self.OA_N > 9728 + 1024 else []
        if not sd.prompt and self.SL >= 2048:
            regs.append(self.U[:, 8 * TK + 10 * 520:8 * TK + NKT * 520])
        self.new_arena(regs)
        ar = self.ar
        if reload:
            self.wb = Buf("w1")
        w1in = w1[:, 0:5376].rearrange("p (c n) -> p c n", n=672)
        wuq = w1[:, 5376:7680].rearrange("p (c n) -> p c n", n=768)
        wuk = w1[:, 7680:8704].rearrange("p (c n) -> p c n", n=512)
        wuv = w1[:, 8704:9728].rearrange("p (c n) -> p c n", n=512)
        if reload:
            self.load_w(w1in, self.w_in, 8, C_CQ, C_MZ)
            self.load_w(wuq, self.i_wuq, 3, 0, 768)
            self.load_w(wuk, self.i_wuk, 2, 0, 512)
            self.load_w(wuv, self.i_wuv, 2, 0, 512)
        KT = self.U[:, 0:8 * TK].rearrange("p (h t) -> p h t", t=TK)
        VM = self.U[:, 8 * TK:8 * TK + NKT * 520].rearrange("p (k c) -> p k c", c=520)
        VMf = self.U[:, 8 * TK:8 * TK + (NKT + 1) * 520]
        ktb = [Buf(f"ktm{i}") for i in range(NKT)]
        vmb = [Buf(f"vm{i}") for i in range(NKT)]
        gq = ar.alloc([384], F32, "gq")
        gkv = ar.alloc([256], F32, "gkv")
        S.dma("sp", "dma_start", dict(out=gq.ap[:], in_=self.i_gq.partition_broadcast(128)), self.ds_const, writes=[gq.b])
        S.dma("sp", "dma_start", dict(out=gkv.ap[:], in_=self.i_gkv.partition_broadcast(128)), self.ds_const, writes=[gkv.b])
        nkc = (sd.ncache + 1) * 128 if not sd.prompt else TK
        nvz = (nkc // 128 + 1) * 520
        S.add("dve", "memset", dict(ap=VMf[:, 0:nvz // 2], constant=0.0), writes=vmb)
        S.add("pool", "memset", dict(ap=VMf[:, nvz // 2:nvz], constant=0.0), writes=vmb)
        S.add("pool", "memset", dict(ap=VM[:, 0:nkc // 128, 512:520], constant=1.0), writes=vmb)
        S.add("pool", "memset", dict(ap=KT[96:128, 0:4, 0:nkc], constant=0.0), writes=ktb)
        S.add("dve", "memset", dict(ap=KT[96:128, 4:8, 0:nkc], constant=0.0), writes=ktb)
        if STOP <= 1:
            return
        hTr = Ring([ar.alloc([8, 128], BF16, f"hTs{i}") for i in range(2)])
        pT2r = Ring([ar.alloc([1024], BF16, f"pT{i}") for i in range(2)])
        QT = ar.alloc([8, 512], BF16, "QT")
        cfr = Ring([ar.alloc([256], F32, f"cf{i}") for i in range(1)])
        krfr = Ring([ar.alloc([32], F32, f"krf{i}") for i in range(2)])
        cbf = ar.alloc([256], BF16, "cbf")
        cqbf = ar.alloc([384], BF16, "cqbf")
        ccT = ar.alloc([5, 128], BF16, "ccT")
        kfull = ar.alloc([8, 96], BF16, "kfull")
        qtok = ar.alloc([8, 96], BF16, "qtok")
        rtmp = ar.alloc([8 * 32], F32, "rtmp")
        u_sb = ar.alloc([512], F32, "u_sb")
        on_sb = ar.alloc([512], BF16, "on_sb")
        rr = ar.alloc([512], BF16, "rr")
        S.add("pool", "memset", dict(ap=rr.ap[:, :], constant=0.0), writes=[rr.b])
        rr32 = R(u_sb.ap, "rr32")
        S.add("pool", "memset", dict(ap=QT.ap[96:128, :, :], constant=0.0), writes=[QT.b])
        S.add("pool", "memset", dict(ap=on_sb.ap[:, :], constant=0.0), writes=[on_sb.b])
        PS = self.PS

        ccTr = Ring([ccT, ar.alloc([5, 128], BF16, "ccT1")])
        kfr_ = Ring([kfull, ar.alloc([8, 96], BF16, "kfull1")])
        rtmpC = ar.alloc([4 * 32], F32, "rtmpC")
        if os.environ.get("KDEBUG"):
            print("MLA arena", ar.off, [r.shape for r in ar.regions])

        def ctrans(cc, cb_ap, cb_b, n):
            pb = PS[4]
            pbf = pb.ap[:, :].bitcast(BF16).rearrange("p (c t) -> p c t", t=128)
            for c in range(2):
                S.add("pe", "transpose", dict(out=pbf[:, c, 0:n], in_=cb_ap[0:n, c * 128:(c + 1) * 128], identity=self.ident.ap[0:n, 0:n]),
                      reads=[cb_b, self.ident.b], writes=[pb.b])
            S.add("dve", "tensor_copy", dict(out=cc.ap[:, 0:2, 0:n], in_=pbf[:, 0:2, 0:n]), reads=[pb.b], writes=[cc.b])

        def kside2(cc, kf_, kt, n):
            for c in range(2):
                S.add("pe", "matmul", dict(out=PS[5].ap[0:n, :], lhsT=cc.ap[:, c, 0:n], rhs=wuk[:, c, :], start=(c == 0), stop=(c == 1)),
                      reads=[cc.b, self.wb], writes=[PS[5].b])
            for c in range(2):
                S.add("pe", "matmul", dict(out=PS[6].ap[0:n, :], lhsT=cc.ap[:, c, 0:n], rhs=wuv[:, c, :], start=(c == 0), stop=(c == 1)),
                      reads=[cc.b, self.wb], writes=[PS[6].b])
            S.add("dve", "tensor_copy", dict(out=kf_.ap[0:n, :, 0:64], in_=PS[5].ap[0:n, :].rearrange("p (h d) -> p h d", d=64)),
                  reads=[PS[5].b], writes=[kf_.b])
            S.add("dve", "tensor_copy", dict(out=VM[0:n, kt, 0:512], in_=PS[6].ap[0:n, :]), reads=[PS[6].b], writes=[vmb[kt]])
            pb2 = PS[7]
            pbf2 = pb2.ap[:, :].bitcast(BF16).rearrange("p (c t) -> p c t", t=128)
            for h in range(8):
                S.add("pe", "transpose", dict(out=pbf2[0:96, h, 0:n], in_=kf_.ap[0:n, h, :], identity=self.ident.ap[0:n, 0:n]),
                      reads=[kf_.b, self.ident.b], writes=[pb2.b])
            S.add("dve", "tensor_copy", dict(out=KT[0:96, :, kt * 128:kt * 128 + n], in_=pbf2[0:96, :, 0:n]), reads=[pb2.b], writes=[ktb[kt]])

        if sd.ncache:
            cst = ar.alloc([8, 256], BF16, "cst")
            kst = ar.alloc([8, 32], BF16, "kst")
            S.dma("pool", "dma_start", dict(out=cst.ap[:], in_=self.clat.rearrange("(t p) c -> p t c", p=128)), self.ds_cache, writes=[cst.b])
            S.dma("pool", "dma_start", dict(out=kst.ap[:], in_=self.ckr.rearrange("(t p) c -> p t c", p=128)), self.ds_cache, writes=[kst.b])

            def cB(t):
                cc, kf_ = ccTr.next(), kfr_.next()
                S.add("pool", "tensor_copy", dict(out=kf_.ap[:, :, 64:96], in_=kst.ap[:, t, :].unsqueeze(1).broadcast_to([128, 8, 32])),
                      reads=[kst.b], writes=[kf_.b])
                ctrans(cc, cst.ap[:, t, :], cst.b, 128)
                cctx[t] = (cc, kf_)

            def cC(t):
                cc, kf_ = cctx[t]
                kside2(cc, kf_, t, 128)
            cctx = {}
            run_skewed([cB, cC], list(range(sd.ncache)))

        def stA(c):
            c["hT"] = hTr.next()
            self.prologue(sd, c["stl"], c["hT"].ap[:, :, 0:c["stl"].n], c["hT"].b, copy_eng="dve")

        def stB(c):
            stl, hT = c["stl"], c["hT"]
            n = stl.n
            self.proj_tok(hT, n, w1in[:, :, 0:384], 384, PS[1])
            self.proj_tok(hT, n, w1in[:, :, 384:672], 288, PS[2])
            cc, kf_ = ccTr.next(), kfr_.next()
            c["cc"], c["kf"] = cc, kf_
            stt = self.rstd(PS[2].ap[0:n, 0:256], [PS[2].b], n, 256, cbf)
            cf = cfr.next()
            S.add("dve", "scalar_tensor_tensor", dict(out=cf.ap[0:n, :], in0=PS[2].ap[0:n, 0:256], scalar=stt.ap[0:n, 2:3], in1=gkv.ap[0:n, :],
                                                      op0=ALU.mult, op1=ALU.mult), reads=[PS[2].b, stt.b, gkv.b], writes=[cf.b])
            S.dma("pool", "dma_start", dict(out=sd.o_lat[stl.row0:stl.row0 + n, :], in_=cf.ap[0:n, :]), self.dr_o["lat"].next(), reads=[cf.b])
            S.add("pool", "tensor_copy", dict(out=cbf.ap[0:n, :], in_=cf.ap[0:n, :]), reads=[cf.b], writes=[cbf.b])
            krf = krfr.next()
            src = PS[2].ap[0:n, 256:288].unsqueeze(1)
            dst = krf.ap[0:n, :].unsqueeze(1)
            self.rope("dve", src, dst, n, self.cosM.ap[0:n, stl.tt, :], self.sinM.ap[0:n, stl.tt, :], 1, 16, rtmp, [PS[2].b], [krf.b])
            S.dma("pool", "dma_start", dict(out=sd.o_kr[stl.row0:stl.row0 + n, :], in_=krf.ap[0:n, :]), self.dr_o["kr"].next(), reads=[krf.b])
            S.add("pool", "tensor_copy", dict(out=kf_.ap[0:n, :, 64:96], in_=krf.ap[0:n, :].unsqueeze(1).broadcast_to([n, 8, 32])),
                  reads=[krf.b], writes=[kf_.b])
            stq = self.rstd(PS[1].ap[0:n, 0:384], [PS[1].b], n, 384, cqbf)
            S.add("dve", "scalar_tensor_tensor", dict(out=cqbf.ap[0:n, :], in0=PS[1].ap[0:n, 0:384], scalar=stq.ap[0:n, 2:3], in1=gq.ap[0:n, :],
                                                      op0=ALU.mult, op1=ALU.mult), reads=[PS[1].b, stq.b, gq.b], writes=[cqbf.b])
            pb = PS[3]
            pbf = pb.ap[:, :].bitcast(BF16).rearrange("p (c t) -> p c t", t=128)
            for ch in range(3):
                S.add("pe", "transpose", dict(out=pbf[:, ch, 0:n], in_=cqbf.ap[0:n, ch * 128:(ch + 1) * 128], identity=self.ident.ap[0:n, 0:n]),
                      reads=[cqbf.b, self.ident.b], writes=[pb.b])
            S.add("dve", "tensor_copy", dict(out=cc.ap[:, 2:5, 0:n], in_=pbf[:, 0:3, 0:n]), reads=[pb.b], writes=[cc.b])
            ctrans(cc, cbf.ap, cbf.b, n)

        def stC(c):
            stl, cc, kf_, j = c["stl"], c["cc"], c["kf"], c["j"]
            n = stl.n
            kside2(cc, kf_, stl.kt, n)
            for hf in range(2):
                bank = PS[5 + hf]
                for ch in range(3):
                    S.add("pe", "matmul", dict(out=bank.ap[0:n, 0:384], lhsT=cc.ap[:, 2 + ch, 0:n], rhs=wuq[:, ch, hf * 384:(hf + 1) * 384],
                                               start=(ch == 0), stop=(ch == 2)), reads=[cc.b, self.wb], writes=[bank.b])
                q3 = bank.ap[0:n, 0:384].rearrange("p (h d) -> p h d", d=96)
                S.add("dve", "tensor_copy", dict(out=qtok.ap[0:n, 4 * hf:4 * hf + 4, 0:64], in_=q3[:, :, 0:64]), reads=[bank.b], writes=[qtok.b])
                self.rope("dve", q3[:, :, 64:96], qtok.ap[0:n, 4 * hf:4 * hf + 4, 64:96], n, self.cosM.ap[0:n, stl.tt, :], self.sinM.ap[0:n, stl.tt, :],
                          4, 16, rtmpC, [bank.b], [qtok.b])
            pb2 = PS[7]
            pbf2 = pb2.ap[:, :].bitcast(BF16).rearrange("p (c t) -> p c t", t=128)
            for h in range(8):
                S.add("pe", "transpose", dict(out=pbf2[0:96, h, 0:n], in_=qtok.ap[0:n, h, :], identity=self.ident.ap[0:n, 0:n]),
                      reads=[qtok.b, self.ident.b], writes=[pb2.b])
            S.add("act", "copy", dict(out=QT.ap[0:96, :, j * 128:j * 128 + n], in_=pbf2[0:96, :, 0:n]), reads=[pb2.b], writes=[QT.b])

        for bi, blk in enumerate(sd.blocks):
            nq = sum(s.n for s in blk)
            run_skewed([stA, stB, stC], [dict(stl=stl, j=j) for j, stl in enumerate(blk)])
            if STOP <= 7:
                continue
            tiles = self.key_tiles(sd, blk, bi)
            c0 = blk[0].c0
            units = []
            for h in range(8):
                ti = 0
                while ti < len(tiles):
                    full = lambda t: (tiles[t][1] == 128 and tiles[t][2] == 0 and not tiles[t][3])
                    if nq == 512 and ti + 1 < len(tiles) and full(ti) and full(ti + 1):
                        units.append((h, [ti, ti + 1]))
                        ti += 2
                    else:
                        units.append((h, [ti]))
                        ti += 1
            accs = Ring([PS[4], PS[5]])
            sring = Ring([(0, PS[0], PS[1]), (2, PS[2], PS[3])])
            pend = []
            cur_acc = {}

            def issue_S(u):
                h, tl = u
                bi0, bA, bB = sring.next()
                pT = pT2r.next()
                if len(tl) == 2:
                    for ti, bk in zip(tl, (bA, bB)):
                        kt = tiles[ti][0]
                        S.add("pe", "matmul", dict(out=bk.ap[:, 0:512], lhsT=KT[:, h, kt * 128:kt * 128 + 128], rhs=QT.ap[:, h, 0:512], start=True, stop=True),
                              reads=[ktb[kt], QT.b], writes=[bk.b])
                    S.add("act", "activation", dict(out=pT.ap[:, 0:1024], in_=self.PSALL[:, bi0 * 512:bi0 * 512 + 1024], func=AF.Exp, scale=SC_MLA),
                          reads=[bA.b, bB.b], writes=[pT.b])
                else:
                    kt, nk, q0, mask = tiles[tl[0]]
                    S.add("pe", "matmul", dict(out=bA.ap[0:nk, q0:nq], lhsT=KT[:, h, kt * 128:kt * 128 + nk], rhs=QT.ap[:, h, q0:nq], start=True, stop=True),
                          reads=[ktb[kt], QT.b], writes=[bA.b])
                    S.add("act", "activation", dict(out=pT.ap[0:nk, q0:nq], in_=bA.ap[0:nk, q0:nq], func=AF.Exp, scale=SC_MLA), reads=[bA.b], writes=[pT.b])
                    if mask:
                        S.add("pool", "memset", dict(ap=pT.ap[64:128, q0:q0 + 64], constant=0.0), writes=[pT.b])
                return pT

            def issue_AV(u, pT):
                h, tl = u
                for k, ti in enumerate(tl):
                    kt, nk, q0, mask = tiles[ti]
                    if ti == 0:
                        cur_acc[h] = accs.next()
                    acc = cur_acc[h]
                    off = 512 * k
                    S.add("pe", "matmul", dict(out=acc.ap[:, q0:nq], lhsT=VMf[0:nk, kt * 520 + h:kt * 520 + h + 1017:8], rhs=pT.ap[0:nk, off + q0:off + nq],
                                               start=(ti == 0), stop=(ti == len(tiles) - 1)), reads=[vmb[kt], pT.b], writes=[acc.b])
                    if ti == len(tiles) - 1:
                        fin(h, acc)

            deferred = []

            def defer(n, fn):
                deferred.append([n, fn])

            def tick(flush=False):
                while True:
                    due = [d for d in deferred if flush or d[0] <= 0]
                    if not due:
                        break
                    for d in due:
                        deferred.remove(d)
                    for d in due:
                        d[1]()
                for d in deferred:
                    d[0] -= 1

            def fin(h, acc):
                t, od = h // 2, h % 2
                S.add("act", "activation", dict(out=rr32.ap[64:65, 0:nq], in_=acc.ap[64:65, 0:nq], func=AF.Ln), reads=[acc.b], writes=[rr32.b])
                S.add("act", "activation", dict(out=rr.ap[64:65, 0:nq], in_=rr32.ap[64:65, 0:nq], func=AF.Exp, scale=-1.0), reads=[rr32.b], writes=[rr.b])
                S.add("dve", "tensor_copy", dict(out=u_sb.ap[0:64, 0:nq], in_=acc.ap[0:64, 0:nq]), reads=[acc.b], writes=[u_sb.b])

                def stage_b():
                    S.add("pe", "matmul", dict(out=PS[6].ap[0:64, 0:nq], lhsT=self.sel64.ap[:, 0:64], rhs=rr.ap[:, 0:nq], start=True, stop=True),
                          reads=[rr.b, self.sel64.b], writes=[PS[6].b])
                    if od == 0:
                        S.add("dve", "tensor_tensor", dict(out=sd.OB[0:64, t, c0:c0 + nq], in0=u_sb.ap[0:64, 0:nq], in1=PS[6].ap[0:64, 0:nq], op=ALU.mult),
                              reads=[u_sb.b, PS[6].b], writes=[self.OBb[sd.obi[bi]]])
                    else:
                        S.add("dve", "tensor_tensor", dict(out=on_sb.ap[0:64, 0:nq], in0=u_sb.ap[0:64, 0:nq], in1=PS[6].ap[0:64, 0:nq], op=ALU.mult),
                              reads=[u_sb.b, PS[6].b], writes=[on_sb.b])

                        def stage_c():
                            S.add("pe", "matmul", dict(out=PS[7].ap[:, 0:nq], lhsT=self.shiftm.ap[:, :], rhs=on_sb.ap[:, 0:nq], start=True, stop=True),
                                  reads=[on_sb.b, self.shiftm.b], writes=[PS[7].b])
                            S.add("dve", "tensor_copy", dict(out=sd.OB[64:128, t, c0:c0 + nq], in_=PS[7].ap[64:128, 0:nq]), reads=[PS[7].b], writes=[self.OBb[sd.obi[bi]]])
                        defer(2, stage_c)
                defer(3, stage_b)

            LOOK = 1
            for i, u in enumerate(units):
                pend.append((u, issue_S(u)))
                if len(pend) > LOOK:
                    issue_AV(*pend.pop(0))
                    tick()
            while pend:
                issue_AV(*pend.pop(0))
                tick()
            tick(flush=True)

    def da_pass(self, sd, reload=True):
        S = self.S
        TK = self.TK
        NKT = TK // 128
        regs = [self.U[:, 8 * TK + 12288:self.U_N]] if self.U_N - (8 * TK + 12288) > 1024 else []
        if not sd.prompt and self.SL >= 2048:
            regs.append(self.U[:, 4 * TK + 9 * 512:8 * TK])
        self.new_arena(regs)
        ar = self.ar
        w2 = self.U[:, 8 * TK:8 * TK + 12288].rearrange("p (c n) -> p c n", n=1536)
        if reload:
            self.wb = Buf("w2")
            self.load_w(w2, self.w_in, 8, 0, 1536)
        KT = self.U[:, 0:4 * TK].rearrange("p (h t) -> p h t", t=TK)
        VD = self.U[:, 4 * TK:8 * TK].rearrange("p (k c) -> p k c", c=512)
        ktb = [Buf(f"ktd{i}") for i in range(NKT)]
        vdb = [Buf(f"vd{i}") for i in range(NKT)]
        hTr = Ring([ar.alloc([8, 128], BF16, f"hTs{i}") for i in range(2)])
        pTr = Ring([ar.alloc([512], BF16, f"pT{i}") for i in range(4)])
        QT = ar.alloc([4, 2, 512], BF16, "QT")
        S.add("pool", "memset", dict(ap=QT.ap[64:128, :, 0, :], constant=0.0), writes=[QT.b])
        S.add("pool", "memset", dict(ap=QT.ap[0:64, :, 1, :], constant=0.0), writes=[QT.b])
        kfr = Ring([ar.alloc([512], F32, f"kf{i}") for i in range(1)])
        vfr = Ring([ar.alloc([512], F32, f"vf{i}") for i in range(1)])
        kb = ar.alloc([512], BF16, "kb")
        qb = ar.alloc([512], BF16, "qb")
        rtmp = ar.alloc([8 * 16], F32, "rtmp")
        t0 = ar.alloc([512], F32, "t0")
        t1 = ar.alloc([512], F32, "t1")
        oraw = ar.alloc([512], F32, "oraw")
        sq = kb
        lnr = ar.alloc([512], F32, "lnr")
        PS = self.PS
        OA = sd.OA

        def k_transposes(kb_ap, kb_b, kt, n):
            pb = PS[4]
            pbf = pb.ap[:, :].bitcast(BF16).rearrange("p (c t) -> p c t", t=128)
            for h in range(4):
                S.add("pe", "transpose", dict(out=pbf[:, h, 0:n], in_=kb_ap[0:n, h * 128:(h + 1) * 128], identity=self.ident.ap[0:n, 0:n]),
                      reads=[kb_b, self.ident.b], writes=[pb.b])
            S.add("act", "copy", dict(out=KT[:, :, kt * 128:kt * 128 + n], in_=pbf[:, 0:4, 0:n]), reads=[pb.b], writes=[ktb[kt]])

        if sd.ncache:
            kst = ar.alloc([8, 512], BF16, "kst")
            S.dma("pool", "dma_start", dict(out=kst.ap[:], in_=self.cdk.rearrange("(t p) c -> p t c", p=128)), self.ds_cache, writes=[kst.b])
            S.dma("pool", "dma_start", dict(out=VD[:, 0:8, :], in_=self.cdv.rearrange("(t p) c -> p t c", p=128)), self.ds_cache, writes=vdb[0:8])
            for t in range(sd.ncache):
                k_transposes(kst.ap[:, t, :], kst.b, t, 128)

        for bi, blk in enumerate(sd.blocks):
            nq = sum(s.n for s in blk)
            c0 = blk[0].c0
            for j, stl in enumerate(blk):
                n = stl.n
                kt = stl.kt
                hT = hTr.next()
                self.prologue(sd, stl, hT.ap[:, :, 0:n], hT.b)
                self.proj_tok(hT, n, w2[:, :, 0:512], 512, PS[1])
                self.proj_tok(hT, n, w2[:, :, 512:1024], 512, PS[2])
                self.proj_tok(hT, n, w2[:, :, 1024:1536], 512, PS[3])
                cosd = self.cosD.ap[0:n, stl.tt, :]
                sind = self.sinD.ap[0:n, stl.tt, :]
                kf = kfr.next()
                S.add("act", "copy", dict(out=kf.ap[0:n, :], in_=PS[2].ap[0:n, :]), reads=[PS[2].b], writes=[kf.b])
                k4 = PS[2].ap[0:n, :].rearrange("p (g d) -> p g d", d=64)[:, :, 0:16]
                kf4 = kf.ap[0:n, :].rearrange("p (g d) -> p g d", d=64)[:, :, 0:16]
                self.rope("dve", k4, kf4, n, cosd, sind, 8, 8, rtmp, [PS[2].b], [kf.b])
                S.dma("pool", "dma_start", dict(out=sd.o_k[stl.row0:stl.row0 + stl.n, :], in_=kf.ap[0:stl.n, :]), self.dr_o["k"].next(), reads=[kf.b])
                S.add("pool", "tensor_copy", dict(out=kb.ap[0:n, :], in_=kf.ap[0:n, :]), reads=[kf.b], writes=[kb.b])
                k_transposes(kb.ap, kb.b, kt, n)
                vf = vfr.next()
                S.add("act", "copy", dict(out=vf.ap[0:n, :], in_=PS[3].ap[0:n, :]), reads=[PS[3].b], writes=[vf.b])
                S.dma("pool", "dma_start", dict(out=sd.o_v[stl.row0:stl.row0 + stl.n, :], in_=vf.ap[0:stl.n, :]), self.dr_o["v"].next(), reads=[vf.b])
                S.add("dve", "tensor_copy", dict(out=VD[0:n, kt, :], in_=PS[3].ap[0:n, :]), reads=[PS[3].b], writes=[vdb[kt]])
                S.add("act", "copy", dict(out=qb.ap[0:n, :], in_=PS[1].ap[0:n, :]), reads=[PS[1].b], writes=[qb.b])
                q4 = PS[1].ap[0:n, :].rearrange("p (g d) -> p g d", d=64)[:, :, 0:16]
                qb4 = qb.ap[0:n, :].rearrange("p (g d) -> p g d", d=64)[:, :, 0:16]
                self.rope("dve", q4, qb4, n, cosd, sind, 8, 8, rtmp, [PS[1].b], [qb.b])
                pb = PS[5]
                pbf = pb.ap[:, :].bitcast(BF16).rearrange("p (c t) -> p c t", t=128)
                for h in range(4):
                    S.add("pe", "transpose", dict(out=pbf[:, h, 0:n], in_=qb.ap[0:n, h * 128:(h + 1) * 128], identity=self.ident.ap[0:n, 0:n]),
                          reads=[qb.b, self.ident.b], writes=[pb.b])
                S.add("act", "copy", dict(out=QT.ap[0:64, :, 0, j * 128:j * 128 + n], in_=pbf[0:64, 0:4, 0:n]), reads=[pb.b], writes=[QT.b])
                S.add("dve", "tensor_copy", dict(out=QT.ap[64:128, :, 1, j * 128:j * 128 + n], in_=pbf[64:128, 0:4, 0:n]), reads=[pb.b], writes=[QT.b])
            tiles = self.key_tiles(sd, blk, bi)
            units = [(h, c, ti) for h in range(4) for c in range(2) for ti in range(len(tiles))]
            accs = Ring([(PS[3], PS[4]), (PS[5], PS[6])])
            sring = Ring([PS[0], PS[1], PS[2]])
            pend = []
            cur_acc = {}
            tt = {0: t0, 1: t1}

            def issue_S(u):
                h, c, ti = u
                kt, nk, q0, mask = tiles[ti]
                sb_ = sring.next()
                S.add("pe", "matmul", dict(out=sb_.ap[0:nk, q0:nq], lhsT=KT[:, h, kt * 128:kt * 128 + nk], rhs=QT.ap[:, h, c, q0:nq],
                                               start=True, stop=True), reads=[ktb[kt], QT.b], writes=[sb_.b])
                pT = pTr.next()
                S.add("act", "activation", dict(out=pT.ap[0:nk, q0:nq], in_=sb_.ap[0:nk, q0:nq], func=AF.Exp, scale=SC_DA), reads=[sb_.b], writes=[pT.b])
                if mask:
                    S.add("pool", "memset", dict(ap=pT.ap[64:128, q0:q0 + 64], constant=0.0), writes=[pT.b])
                return pT

            def issue_AV(u, pT):
                h, c, ti = u
                kt, nk, q0, mask = tiles[ti]
                if ti == 0:
                    cur_acc[(h, c)] = accs.next()
                au, asum = cur_acc[(h, c)]
                first, last = (ti == 0), (ti == len(tiles) - 1)
                S.add("pe", "matmul", dict(out=au.ap[:, q0:nq], lhsT=VD[0:nk, kt, h * 128:(h + 1) * 128], rhs=pT.ap[0:nk, q0:nq], start=first, stop=last),
                      reads=[vdb[kt], pT.b], writes=[au.b])
                S.add("pe", "matmul", dict(out=asum.ap[:, q0:nq], lhsT=self.ones.ap[0:nk, :], rhs=pT.ap[0:nk, q0:nq], start=first, stop=last),
                      reads=[self.ones.b, pT.b], writes=[asum.b])
                if last:
                    t = tt[c]
                    if c == 0:
                        S.add("dve", "reciprocal", dict(out=t.ap[:, 0:nq], in_=asum.ap[:, 0:nq]), reads=[asum.b], writes=[t.b])
                    else:
                        S.add("act", "activation", dict(out=t.ap[:, 0:nq], in_=asum.ap[:, 0:nq], func=AF.Ln), reads=[asum.b], writes=[t.b])
                        S.add("act", "activation", dict(out=t.ap[:, 0:nq], in_=t.ap[:, 0:nq], func=AF.Exp, scale=-1.0), reads=[t.b], writes=[t.b])
                    S.add("dve", "tensor_tensor", dict(out=t.ap[:, 0:nq], in0=t.ap[:, 0:nq], in1=au.ap[:, 0:nq], op=ALU.mult), reads=[au.b, t.b], writes=[t.b])
                    if c == 1:
                        fin(h)

            deferred = []

            def defer(n, fn):
                deferred.append([n, fn])

            def tick(flush=False):
                while True:
                    due = [d for d in deferred if flush or d[0] <= 0]
                    if not due:
                        break
                    for d in due:
                        deferred.remove(d)
                    for d in due:
                        d[1]()
                for d in deferred:
                    d[0] -= 1

            def fin(h):
                v = self.vec
                S.add("dve", "scalar_tensor_tensor", dict(out=oraw.ap[:, 0:nq], in0=t1.ap[:, 0:nq], scalar=v.ap[:, 0:1], in1=t0.ap[:, 0:nq], op0=ALU.mult, op1=ALU.add),
                      reads=[t0.b, t1.b, v.b], writes=[oraw.b])
                S.add("act", "activation", dict(out=sq.ap[:, 0:nq], in_=oraw.ap[:, 0:nq], func=AF.Square), reads=[oraw.b], writes=[sq.b])

                def stage_b():
                    S.add("pe", "matmul", dict(out=PS[7].ap[:, 0:nq], lhsT=self.ones.ap[:, :], rhs=sq.ap[:, 0:nq], start=True, stop=True), reads=[sq.b, self.ones.b], writes=[PS[7].b])
                    S.add("act", "activation", dict(out=lnr.ap[:, 0:nq], in_=PS[7].ap[:, 0:nq], func=AF.Ln, scale=1.0 / 128, bias=EPS), reads=[PS[7].b], writes=[lnr.b])
                    S.add("act", "activation", dict(out=lnr.ap[:, 0:nq], in_=lnr.ap[:, 0:nq], func=AF.Exp, scale=-0.5), reads=[lnr.b], writes=[lnr.b])
                    S.add("dve", "scalar_tensor_tensor", dict(out=OA[:, h, c0:c0 + nq], in0=oraw.ap[:, 0:nq], scalar=v.ap[:, 1:2], in1=lnr.ap[:, 0:nq], op0=ALU.mult, op1=ALU.mult),
                          reads=[oraw.b, lnr.b, v.b], writes=[self.OAb[sd.obi[bi]]])
                defer(5, stage_b)

            LOOK = 2
            for i, u in enumerate(units):
                pend.append((u, issue_S(u)))
                if len(pend) > LOOK:
                    issue_AV(*pend.pop(0))
                    tick()
            while pend:
                issue_AV(*pend.pop(0))
                tick()
            tick(flush=True)

    def fin_pass(self, sd, reload=True):
        S = self.S
        SL = self.SL
        self.new_arena([self.U[:, 40960:self.U_N]] if self.U_N - 40960 > 1024 else [])
        ar = self.ar
        if reload:
            self.wb = Buf("w3")
        U = self.U
        wz = U[:, 0:4096].rearrange("p (c n) -> p c n", n=512)
        wg = U[:, 4096:24576].rearrange("p (c n) -> p c n", n=2560)
        wba = U[:, 24576:28672].rearrange("p (c n) -> p c n", n=1024)
        wbb = U[:, 28672:32768].rearrange("p (c n) -> p c n", n=1024)
        wout = U[:, 32768:40960].rearrange("p (c n) -> p c n", n=1024)
        if reload:
            self.load_w(wz, self.w_in, 8, C_DAZ, C_DAZ + 512)
            self.load_w(wg, self.w_in, 8, C_MZ, IN_COLS)
            self.load_w(wba, self.i_wba, 4, 0, 1024)
            self.load_w(wbb, self.i_wbb, 4, 0, 1024)
            self.load_w(wout, self.i_wout, 8, 0, 1024)
        gf = ar.alloc([D], F32, "gf")
        S.dma("sp", "dma_start", dict(out=gf.ap[:], in_=self.i_gf.partition_broadcast(128)), self.ds_const, writes=[gf.b])
        hT = ar.alloc([8, 512], BF16, "hT")
        oz = ar.alloc([8, 512], BF16, "oz")
        mT = ar.alloc([8, 512], BF16, "mT")
        sg = Ring([ar.alloc([512], F32, f"sg{i}") for i in range(2)])
        tg = Ring([ar.alloc([512], F32, f"tg{i}") for i in range(2)])
        PS = self.PS
        OA = sd.OA
        OB = sd.OB
        zb = Ring([PS[i] for i in range(1, 8)])
        ojunk = [None]
        pro_x = Ring([self.xring.items[0]])
        out_x = Ring([self.xring.items[1]])
        for bi, blk in enumerate(sd.blocks):
            nq = sum(s.n for s in blk)
            c0 = blk[0].c0
            self.xring = pro_x
            if sd.prompt and bi >= 1:
                c0p = sd.blocks[bi - 1][0].c0
                hch = [OA[:, c, c0p:c0p + 512] for c in range(4)] + [OB[:, c, c0p:c0p + 512] for c in range(4)]
                hbufs = [self.OAb[bi - 1], self.OBb[bi - 1]]
                for j, stl in enumerate(blk):
                    self.prologue(sd, stl, None, None, outs=[(OA[:, :, c0p + j * 128:c0p + j * 128 + stl.n], 0, 4, [self.OAb[bi - 1]]),
                                                             (OB[:, :, c0p + j * 128:c0p + j * 128 + stl.n], 4, 8, [self.OBb[bi - 1]])])
            else:
                hch = [hT.ap[:, c, :] for c in range(8)]
                hbufs = [hT.b]
                for j, stl in enumerate(blk):
                    self.prologue(sd, stl, hT.ap[:, :, j * 128:j * 128 + stl.n], hT.b)
            for m in range(8):
                bank = zb.next()
                wsrc = wz[:, :, m * 128:(m + 1) * 128] if m < 4 else wg[:, :, (m - 4) * 128:(m - 3) * 128]
                for c in range(8):
                    S.add("pe", "matmul", dict(out=bank.ap[:, 0:nq], lhsT=wsrc[:, c, :], rhs=hch[c][:, 0:nq], start=(c == 0), stop=(c == 7)),
                          reads=hbufs + [self.wb], writes=[bank.b])
                s_ = sg.next()
                S.add("act", "activation", dict(out=s_.ap[:, 0:nq], in_=bank.ap[:, 0:nq], func=AF.Silu), reads=[bank.b], writes=[s_.b])
                osrc = OA[:, m, c0:c0 + nq] if m < 4 else OB[:, m - 4, c0:c0 + nq]
                ob = [self.OAb[sd.obi[bi]]] if m < 4 else [self.OBb[sd.obi[bi]]]
                S.add("dve", "tensor_tensor", dict(out=oz.ap[:, m, 0:nq], in0=s_.ap[:, 0:nq], in1=osrc, op=ALU.mult), reads=[s_.b] + ob, writes=[oz.b])
            for m in range(8):
                bya, byb, bga, bgb = zb.next(), zb.next(), zb.next(), zb.next()
                for c in range(4):
                    S.add("pe", "matmul", dict(out=bya.ap[:, 0:nq], lhsT=wba[:, c, m * 128:(m + 1) * 128], rhs=oz.ap[:, c, 0:nq], start=(c == 0), stop=(c == 3)),
                          reads=[oz.b, self.wb], writes=[bya.b])
                for c in range(4):
                    S.add("pe", "matmul", dict(out=byb.ap[:, 0:nq], lhsT=wbb[:, c, m * 128:(m + 1) * 128], rhs=oz.ap[:, 4 + c, 0:nq], start=(c == 0), stop=(c == 3)),
                          reads=[oz.b, self.wb], writes=[byb.b])
                for c in range(8):
                    S.add("pe", "matmul", dict(out=bga.ap[:, 0:nq], lhsT=wg[:, c, 512 + m * 128:512 + (m + 1) * 128], rhs=hch[c][:, 0:nq], start=(c == 0), stop=(c == 7)),
                          reads=hbufs + [self.wb], writes=[bga.b])
                for c in range(8):
                    S.add("pe", "matmul", dict(out=bgb.ap[:, 0:nq], lhsT=wg[:, c, 1536 + m * 128:1536 + (m + 1) * 128], rhs=hch[c][:, 0:nq], start=(c == 0), stop=(c == 7)),
                          reads=hbufs + [self.wb], writes=[bgb.b])
                ga, gb_ = sg.next(), sg.next()
                S.add("act", "activation", dict(out=ga.ap[:, 0:nq], in_=bga.ap[:, 0:nq], func=AF.Sigmoid, bias=self.gateb.ap[:, m:m + 1]), reads=[bga.b, self.gateb.b], writes=[ga.b])
                S.add("act", "activation", dict(out=gb_.ap[:, 0:nq], in_=bgb.ap[:, 0:nq], func=AF.Sigmoid, bias=self.gateb.ap[:, 8 + m:9 + m]), reads=[bgb.b, self.gateb.b], writes=[gb_.b])
                ta, tb = tg.next(), tg.next()
                S.add("dve", "tensor_tensor", dict(out=ta.ap[:, 0:nq], in0=ga.ap[:, 0:nq], in1=bya.ap[:, 0:nq], op=ALU.mult), reads=[ga.b, bya.b], writes=[ta.b])
                S.add("dve", "tensor_tensor", dict(out=tb.ap[:, 0:nq], in0=gb_.ap[:, 0:nq], in1=byb.ap[:, 0:nq], op=ALU.mult), reads=[gb_.b, byb.b], writes=[tb.b])
                S.add("pool", "tensor_tensor", dict(out=mT.ap[:, m, 0:nq], in0=ta.ap[:, 0:nq], in1=tb.ap[:, 0:nq], op=ALU.add), reads=[ta.b, tb.b], writes=[mT.b])
            if sd.prompt and bi == 0 and len(sd.blocks) > 1:
                x2 = R(hT.ap[:, 0:4, :].rearrange("p a b -> p (a b)").bitcast(F32), "xt2")
                x2.b.w = hT.b.w
                x2.b.rs = list(hT.b.rs)
                out_x.items.append(x2)
                jk = R(hT.ap[:, 4:6, :].rearrange("p a b -> p (a b)"), "ojunk")
                jk.b.w = hT.b.w
                jk.b.rs = list(hT.b.rs)
                ojunk[0] = jk
            self.xring = out_x
            for j, stl in enumerate(blk):
                n = stl.n
                xt = self.load_x(sd, stl, self.dr_x2)
                for hf in range(2):
                    bank = zb.next()
                    for c in range(8):
                        S.add("pe", "matmul", dict(out=bank.ap[0:n, :], lhsT=mT.ap[:, c, j * 128:j * 128 + n], rhs=wout[:, c, hf * 512:(hf + 1) * 512],
                                                                                  start=(c == 0), stop=(c == 7)), reads=[mT.b, self.wb], writes=[bank.b])
                    S.add("dve", "tensor_tensor", dict(out=xt.ap[0:n, hf * 512:(hf + 1) * 512], in0=xt.ap[0:n, hf * 512:(hf + 1) * 512], in1=bank.ap[0:n, :], op=ALU.add),
                          reads=[bank.b, xt.b], writes=[xt.b])
                stt = self.rstd(xt.ap[0:n, :], [xt.b], n, D, ojunk[0] if ojunk[0] is not None else self.hbring.next())
                S.add("dve", "scalar_tensor_tensor", dict(out=xt.ap[0:n, :], in0=xt.ap[0:n, :], scalar=stt.ap[0:n, 2:3], in1=gf.ap[0:n, :], op0=ALU.mult, op1=ALU.mult),
                      reads=[xt.b, stt.b, gf.b], writes=[xt.b])
                S.dma("pool", "dma_start", dict(out=sd.o_y[stl.row0:stl.row0 + stl.n, :], in_=xt.ap[0:stl.n, :]), self.dr_o["y"].next(), reads=[xt.b])


def rope_tables(SL, sample):
    ntt = SL // 128 + 1
    pos = np.zeros((128, ntt), np.float64)
    for t in range(ntt - 1):
        pos[:, t] = t * 128 + np.arange(128)
    pos[:, ntt - 1] = 1024 + np.arange(128)
    out = {}
    for name, rot in (("D", 16), ("M", 32)):
        half = rot // 2
        inv = (np.float32(500000.0) ** (-np.arange(half, dtype=np.float32) * np.float32(2.0) / np.float32(rot))).astype(np.float32)
        ang = (pos.astype(np.float32)[:, :, None] * inv[None, None, :]).astype(np.float32)
        out["cos" + name] = np.cos(ang.astype(np.float64)).astype(np.float32).reshape(128, ntt * half)
        out["sin" + name] = np.sin(ang.astype(np.float64)).astype(np.float32).reshape(128, ntt * half)
    return out


_CACHE = {}


def get_nc(NSEQ, SL, SAMPLE, parts=("mla", "da", "fin")):
    key = (NSEQ, SL, SAMPLE, parts)
    if key not in _CACHE:
        b = Builder(NSEQ, SL, SAMPLE, parts)
        nc = b.build()
        _CACHE[key] = (nc, b)
    return _CACHE[key]


def shared_inputs(inp, SL):
    f = lambda a: np.ascontiguousarray(np.asarray(a, dtype=np.float32))
    sh = {
        "w_in": f(inp["w_in"][0]),
        "norm_g": f(inp["norm_g"][0]).reshape(1, D),
        "gate_bT": f(np.asarray(inp["gate_b"][0]).reshape(16, 128).T),
        "da_lambda": f(inp["da_lambda"][0]).reshape(1, 256),
        "hng": f(inp["da_head_norm_g"][0]).reshape(128, 1),
        "gq": f(inp["mla_q_norm_g"][0]).reshape(1, 384),
        "gkv": f(inp["mla_kv_norm_g"][0]).reshape(1, 256),
        "w_uq": f(inp["mla_w_uq"][0]),
        "w_uk": f(inp["mla_w_uk"][0]),
        "w_uv": f(np.asarray(inp["mla_w_uv"][0]).reshape(256, 8, 64).transpose(0, 2, 1).reshape(256, 512)),
        "w_ba": f(inp["w_branch_a"][0]),
        "w_bb": f(inp["w_branch_b"][0]),
        "w_out": f(inp["w_out"][0]),
        "gf": f(inp["final_norm_g"]).reshape(1, D),
        "ident": np.eye(128, dtype=np.float32),
        "shiftm": np.eye(128, k=64, dtype=np.float32),
    }
    sh.update(rope_tables(SL, True))
    return sh


def kernel(**inputs):
    NCORES = 8
    xp = np.asarray(inputs["x_prompt"], dtype=np.float32)
    xs = np.asarray(inputs["x_sample"], dtype=np.float32)
    B, SL, _ = xp.shape
    NSEQ = B // NCORES
    nc, _ = get_nc(NSEQ, SL, True)
    sh = shared_inputs(inputs, SL)
    cdk = np.asarray(inputs["cache_da_k"], dtype=np.float32)[0]
    cdv = np.asarray(inputs["cache_da_v"], dtype=np.float32)[0]
    clat = np.asarray(inputs["cache_mla_latent"], dtype=np.float32)[0]
    ckr = np.asarray(inputs["cache_mla_krope"], dtype=np.float32)[0]
    in_maps = []
    for c in range(NCORES):
        m = dict(sh)
        m["xp"] = np.ascontiguousarray(xp[c * NSEQ:(c + 1) * NSEQ].reshape(NSEQ * SL, D))
        m["xs"] = np.ascontiguousarray(xs[c])
        m["cdk"] = np.ascontiguousarray(cdk[c].reshape(1024, 512))
        m["cdv"] = np.ascontiguousarray(cdv[c].reshape(1024, 512))
        m["clat"] = np.ascontiguousarray(clat[c])
        m["ckr"] = np.ascontiguousarray(ckr[c])
        in_maps.append(m)
    res = run_bass_kernel_spmd(nc, in_maps, core_ids=list(range(NCORES))).results
    cat = lambda k: np.concatenate([np.asarray(r[k]) for r in res], axis=0)
    y_p = cat("yp").reshape(B, SL, D)
    y_s = cat("ys").reshape(NCORES, 64, D)
    k_p = cat("kp").reshape(1, B, SL, 4, 128)
    v_p = cat("vp").reshape(1, B, SL, 4, 128)
    lat_p = cat("latp").reshape(1, B, SL, 256)
    kr_p = cat("krp").reshape(1, B, SL, 32)
    k_s = cat("ks").reshape(1, NCORES, 64, 4, 128)
    v_s = cat("vs").reshape(1, NCORES, 64, 4, 128)
    lat_s = cat("lats").reshape(1, NCORES, 64, 256)
    kr_s = cat("krs").reshape(1, NCORES, 64, 32)
    return tuple(np.ascontiguousarray(a, dtype=np.float32) for a in (y_p, y_s, k_p, v_p, lat_p, kr_p, k_s, v_s, lat_s, kr_s))
```

```python
import math
import numpy as np
from contextlib import ExitStack
import concourse.bass as bass
import concourse.mybir as mybir
from concourse.bass_utils import run_bass_kernel_spmd

F32 = mybir.dt.float32
BF16 = mybir.dt.bfloat16
AF = mybir.ActivationFunctionType
ALU = mybir.AluOpType

D = 1024
SEM_ROT = 12000
SCL = {}
EPS = 1e-6
import os
STOP = int(os.environ.get('KSTOP', '99'))
SUB = int(os.environ.get('KSUB', '99'))
C_DAQ, C_DAK, C_DAV, C_DAZ, C_CQ, C_CKV, C_KR, C_MZ, C_G = 0, 512, 1024, 1536, 2048, 2432, 2688, 2720, 3232
IN_COLS = 5280
LAM_INIT = 0.8 - 0.6 * math.exp(-0.3 * 0)
SC_DA = 64 ** -0.5
SC_MLA = 96 ** -0.5


class Buf:
    __slots__ = ("name", "w", "rs", "excl")

    def __init__(self, name="", excl=False):
        self.name = name
        self.w = None
        self.rs = []
        self.excl = excl


class DmaSem:
    def __init__(self, sem):
        self.sem = sem
        self.count = 0
        self.last_group = None


class DmaGroup:
    def __init__(self, ds):
        self.ds = ds
        self.final = None
        self.last_op = None


class Op:
    __slots__ = ("eng", "name", "kw", "deps", "signal", "sem", "val", "idx", "group", "is_dma", "gidx", "region", "fin")

    def __init__(self, eng, name, kw):
        self.eng = eng
        self.name = name
        self.kw = kw
        self.deps = []
        self.signal = False
        self.sem = None
        self.val = None
        self.idx = None
        self.group = None
        self.is_dma = False


class Sched:
    ENGS = ("pe", "act", "dve", "pool", "sp")

    def __init__(self, nc, stack):
        self.nc = nc
        self.ops = {e: [] for e in self.ENGS}
        self.dma_sems = []
        self._stack = stack
        self.bar_deps = []
        self.bar_seen = {e: True for e in self.ENGS}
        self.all_ops = []
        self.region = 0

    def new_sem(self, name):
        return self._stack.enter_context(self.nc.semaphore(name))

    def dma_sem(self, name):
        ds = DmaSem(self.new_sem(name))
        self.dma_sems.append(ds)
        return ds

    def dma_ring(self, name, n):
        return DmaRing([self.dma_sem(f"{name}{i}") for i in range(n)])

    def barrier(self):
        self.region += 1

    def _collect(self, op, reads, writes, extra):
        deps = []
        for b in reads:
            if b.w is not None:
                deps.append(b.w)
            if b.excl:
                for r in b.rs:
                    if r.eng != op.eng:
                        deps.append(r)
        for b in writes:
            if b.w is not None:
                deps.append(b.w)
            deps.extend(b.rs)
        deps.extend(extra)
        seen = set()
        out = []
        for d in deps:
            if d is op or id(d) in seen:
                continue
            seen.add(id(d))
            out.append(d)
        op.deps = out
        op.gidx = len(self.all_ops)
        op.region = self.region
        self.all_ops.append(op)
        for b in writes:
            b.w = op
            b.rs = []
        for b in reads:
            b.rs.append(op)

    def add(self, eng, name, kw, reads=(), writes=(), extra=()):
        op = Op(eng, name, kw)
        self._collect(op, reads, writes, extra)
        self.ops[eng].append(op)
        return op

    def dma(self, eng, name, kw, ds, reads=(), writes=(), extra=()):
        op = Op(eng, name, kw)
        op.is_dma = True
        extra = list(extra)
        g = DmaGroup(ds)
        if ds.last_group is not None:
            extra.append(ds.last_group.last_op)
        ds.last_group = g
        ds.count += 1
        g.final = 16 * ds.count
        g.last_op = op
        op.group = g
        op.sem = ds.sem
        op.signal = True
        self._collect(op, reads, writes, extra)
        self.ops[eng].append(op)
        return op

    @staticmethod
    def _fsize(ap):
        n = 1
        for x in ap.shape[1:]:
            n *= x
        return n

    def _dur(self, op):
        return self._dur0(op) * SCL.get("dma" if op.is_dma else op.eng, 1.0)

    def _dur0(self, op):
        kw = op.kw
        if op.is_dma:
            o = kw["out"]
            nbytes = self._fsize(o) * o.shape[0] * (4 if o.dtype == F32 else 2)
            return 2200.0 + nbytes / 120.0
        if op.eng == "pe":
            if op.name == "transpose":
                return 70.0
            return 8.0 + 0.41 * self._fsize(kw["rhs"])
        a = kw.get("in_", kw.get("in0", kw.get("out", kw.get("ap"))))
        f = self._fsize(a)
        if op.eng == "act":
            if os.environ.get("KEXP2") and f == 512 and kw.get("func") == AF.Exp:
                return (190.0 + 1024 / 1.2) / 2
            return 190.0 + f / 1.2 + (100.0 if "accum_out" in kw else 0.0)
        if op.eng == "dve":
            if op.name == "reciprocal":
                return 80.0 + 6.6 * f
            return 100.0 + f / 0.85
        return 200.0 + f / 0.48

    def schedule(self, dry=False, beta=None):
        import heapq
        if beta is None:
            beta = float(os.environ.get("KBETA", "0.3"))
        regions = {}
        for op in self.all_ops:
            regions.setdefault(op.region, []).append(op)
        new_ops = {e: [] for e in self.ENGS}
        t0 = 0.0
        tail = []
        for r in sorted(regions):
            ops = regions[r]
            reg_first = {}
            reg_last = {}
            dma_last = {}
            inreg = set(id(o) for o in ops)
            succ = {}
            indeg = {}
            ready = {}
            for o in ops:
                cnt = 0
                for d in o.deps:
                    if id(d) in inreg:
                        cnt += 1
                        succ.setdefault(id(d), []).append(o)
                indeg[id(o)] = cnt
                ready[id(o)] = t0
            bl = {}
            if beta:
                for o in reversed(ops):
                    m = 0.0
                    for sc in succ.get(id(o), ()):
                        v = bl[id(sc)]
                        if v > m:
                            m = v
                    bl[id(o)] = m + self._dur(o)
            heap = [(t0 - beta * bl.get(id(o), 0.0), o.gidx, o) for o in ops if indeg[id(o)] == 0]
            heapq.heapify(heap)
            free = {e: t0 for e in self.ENGS}
            tmax = t0
            while heap:
                _, _, o = heapq.heappop(heap)
                rt = ready[id(o)]
                st = max(rt, free[o.eng])
                if o.is_dma:
                    issue = 1000.0 if o.eng == "pool" else 80.0
                    free[o.eng] = st + issue
                    fin = st + issue + self._dur(o)
                else:
                    fin = st + self._dur(o)
                    free[o.eng] = fin
                o.fin = fin
                tmax = max(tmax, fin)
                new_ops[o.eng].append(o)
                if o.eng not in reg_first:
                    reg_first[o.eng] = o
                if o.is_dma:
                    cur = dma_last.get(id(o.sem))
                    if cur is None or o.group.final > cur.group.final:
                        dma_last[id(o.sem)] = o
                else:
                    reg_last[o.eng] = o
                for sc in succ.get(id(o), ()):
                    if o.eng == "pe" and sc.eng == "pe" and not sc.is_dma and not o.is_dma:
                        lat = 0.0
                    elif o.eng == sc.eng and not o.is_dma:
                        lat = 60.0
                    else:
                        lat = 150.0
                    lat *= SCL.get("lat", 1.0)
                    ready[id(sc)] = max(ready[id(sc)], fin + lat)
                    indeg[id(sc)] -= 1
                    if indeg[id(sc)] == 0:
                        heapq.heappush(heap, (ready[id(sc)] - beta * bl.get(id(sc), 0.0), sc.gidx, sc))
            if not dry:
                for e, o in reg_first.items():
                    have = set(id(d) for d in o.deps)
                    o.deps = list(o.deps) + [d for d in tail if id(d) not in have and d is not o]
            tail = list(reg_last.values()) + list(dma_last.values())
            if os.environ.get("KDEBUG"):
                busy = {}
                for o in ops:
                    if not o.is_dma:
                        busy[o.eng] = busy.get(o.eng, 0.0) + self._dur(o)
                print("region", r, "ops", len(ops), "dur us", round((tmax - t0) / 1000, 1), {k: round(v / 1000) for k, v in busy.items()})
            t0 = tmax
        assert sum(len(v) for v in new_ops.values()) == len(self.all_ops)
        if dry:
            return t0
        self.ops = new_ops
        self.est_ns = t0

    def finalize(self):
        if os.environ.get("KSCHED", "1") == "1":
            self.schedule()
        for e in self.ENGS:
            for i, op in enumerate(self.ops[e]):
                op.idx = i
        for e in self.ENGS:
            for op in self.ops[e]:
                best = {}
                out = []
                seen_groups = set()
                for d in op.deps:
                    if d.is_dma:
                        if id(d.group) not in seen_groups:
                            seen_groups.add(id(d.group))
                            out.append(d)
                    else:
                        if d.eng == "pe" and op.eng == "pe" and not op.is_dma:
                            continue
                        cur = best.get(d.eng)
                        if cur is None or d.idx > cur.idx:
                            best[d.eng] = d
                out.extend(best.values())
                for d in out:
                    d.signal = True
                op.deps = out
        for e in ("pe", "act", "dve", "pool"):
            nsem = 0
            cnt = 0
            cur = None
            for op in self.ops[e]:
                if op.is_dma or not op.signal:
                    continue
                if cur is None or cnt >= SEM_ROT:
                    cur = self.new_sem(f"s_{e}{nsem}")
                    nsem += 1
                    cnt = 0
                cnt += 1
                op.sem = cur
                op.val = cnt
        for e in self.ENGS:
            for op in self.ops[e]:
                if op.is_dma:
                    op.val = op.group.final

    def emit(self, block):
        self.finalize()
        stats = {}

        def run(e):
            def body(eng):
                seen = {}
                nw = 0
                for op in self.ops[e]:
                    for d in op.deps:
                        k = id(d.sem)
                        if seen.get(k, 0) >= d.val:
                            continue
                        seen[k] = d.val
                        eng.wait_ge(d.sem, d.val)
                        nw += 1
                    ins = getattr(eng, op.name)(**op.kw)
                    if op.signal:
                        ins.then_inc(op.sem, 16 if op.is_dma else 1)
                if e == "sp":
                    for ds in self.dma_sems:
                        if ds.count:
                            eng.wait_ge(ds.sem, 16 * ds.count)
                stats[e] = (len(self.ops[e]), nw)
            return body

        block.tensor(run("pe"))
        block.scalar(run("act"))
        block.vector(run("dve"))
        block.gpsimd(run("pool"))
        block.sync(run("sp"))
        return stats


class DmaRing:
    def __init__(self, sems):
        self.sems = sems
        self.i = 0

    def next(self):
        s = self.sems[self.i % len(self.sems)]
        self.i += 1
        return s


class R:
    __slots__ = ("ap", "b")

    def __init__(self, ap, name=""):
        self.ap = ap
        self.b = Buf(name)


class Ring:
    def __init__(self, items):
        self.items = items
        self.i = 0

    def next(self):
        r = self.items[self.i % len(self.items)]
        self.i += 1
        return r


class Arena:
    def __init__(self, regions):
        self.regions = regions
        self.reset()

    def reset(self):
        self.off = [0 for _ in self.regions]

    def alloc(self, shape, dtype, name=""):
        n = 1
        for s in shape:
            n *= s
        esz = 4 if dtype == F32 else 2
        nel = n * esz // 2
        for i, reg in enumerate(self.regions):
            o = (self.off[i] + 1) // 2 * 2
            if o + nel <= reg.shape[1]:
                self.off[i] = o + nel
                ap = reg[:, o:o + nel]
                if dtype != BF16:
                    ap = ap.bitcast(dtype)
                if len(shape) == 2:
                    ap = ap.rearrange("p (a b) -> p a b", b=shape[1])
                elif len(shape) == 3:
                    ap = ap.rearrange("p (a b c) -> p a b c", b=shape[1], c=shape[2])
                return R(ap, name)
        raise RuntimeError(f"arena overflow allocating {name} {shape}; offs={self.off}")


def run_skewed(stages, items):
    n, ns = len(items), len(stages)
    for step in range(n + ns - 1):
        for si in range(ns):
            j = step - si
            if 0 <= j < n:
                stages[si](items[j])


class SubTile:
    def __init__(self, row0, n, kt, tt, c0):
        self.row0, self.n, self.kt, self.tt, self.c0 = row0, n, kt, tt, c0


class SeqDesc:
    pass


class Builder:
    def __init__(self, NSEQ, SL, SAMPLE, parts=("mla", "da", "fin")):
        self.NSEQ, self.SL, self.SAMPLE = NSEQ, SL, SAMPLE
        self.parts = parts
        self.TK = max(SL, 1152 if SAMPLE else 0)
        self.W = {}
        self.NTT = SL // 128 + 1

    def dram(self, name, shape, kind="ExternalInput"):
        return self.nc.dram_tensor(name, list(shape), F32, kind=kind).ap()

    def sb(self, name, shape, dtype):
        return self.st.enter_context(self.nc.sbuf_tensor("sb_" + name, list(shape), dtype))

    def build(self):
        nc = self.nc = bass.Bass("TRN2", target_bir_lowering=False)
        NSEQ, SL, TK = self.NSEQ, self.SL, self.TK
        NP = NSEQ * SL
        dr = self.dram
        self.xp = dr("xp", [NP, D])
        self.w_in = dr("w_in", [D, IN_COLS])
        self.i_normg = dr("norm_g", [1, D])
        self.i_gateb = dr("gate_bT", [128, 16])
        self.i_lam = dr("da_lambda", [1, 256])
        self.i_hng = dr("hng", [128, 1])
        self.i_gq = dr("gq", [1, 384])
        self.i_gkv = dr("gkv", [1, 256])
        self.i_wuq = dr("w_uq", [384, 768])
        self.i_wuk = dr("w_uk", [256, 512])
        self.i_wuv = dr("w_uv", [256, 512])
        self.i_wba = dr("w_ba", [512, D])
        self.i_wbb = dr("w_bb", [512, D])
        self.i_wout = dr("w_out", [D, D])
        self.i_gf = dr("gf", [1, D])
        self.i_ident = dr("ident", [128, 128])
        self.i_shift = dr("shiftm", [128, 128])
        NTT = self.NTT
        self.i_cosD = dr("cosD", [128, NTT * 8])
        self.i_sinD = dr("sinD", [128, NTT * 8])
        self.i_cosM = dr("cosM", [128, NTT * 16])
        self.i_sinM = dr("sinM", [128, NTT * 16])
        o = lambda n, s: dr(n, s, kind="ExternalOutput")
        self.o_y = o("yp", [NP, D])
        self.o_k = o("kp", [NP, 512])
        self.o_v = o("vp", [NP, 512])
        self.o_lat = o("latp", [NP, 256])
        self.o_kr = o("krp", [NP, 32])
        if self.SAMPLE:
            self.xs = dr("xs", [64, D])
            self.cdk = dr("cdk", [1024, 512])
            self.cdv = dr("cdv", [1024, 512])
            self.clat = dr("clat", [1024, 256])
            self.ckr = dr("ckr", [1024, 32])
            self.o_ys = o("ys", [64, D])
            self.o_ks = o("ks", [64, 512])
            self.o_vs = o("vs", [64, 512])
            self.o_lats = o("lats", [64, 256])
            self.o_krs = o("krs", [64, 32])

        with ExitStack() as st:
            self.st = st
            S = self.S = Sched(nc, st)
            sb = self.sb
            self.U_N = max(8 * TK + (TK // 128 + 1) * 520, 8 * TK + 12288, 40960)
            self.U = sb("U", [128, self.U_N], BF16)
            self.SLX = SL
            SLX = self.SLX
            self.OB = sb("OB", [128, 4, SLX], BF16)
            self.OA_N = max(4 * SLX, 9728)
            self.OA = sb("OA", [128, self.OA_N], BF16)
            self.OBb = [Buf(f"OB{i}") for i in range(SL // 512 + 1)]
            self.OAb = [Buf(f"OA{i}") for i in range(SL // 512 + 1)]
            self.Ub = Buf("U")
            self.ident = R(sb("ident", [128, 128], BF16))
            self.ones = R(sb("ones", [128, 128], BF16))
            self.shiftm = R(sb("shiftm", [128, 128], BF16))
            self.sel64 = R(sb("sel64", [128, 64], BF16))
            self.cosD = R(sb("cosD", [128, NTT, 8], F32))
            self.sinD = R(sb("sinD", [128, NTT, 8], F32))
            self.cosM = R(sb("cosM", [128, NTT, 16], F32))
            self.sinM = R(sb("sinM", [128, NTT, 16], F32))
            self.g_in = R(sb("g_in", [128, D], F32))
            self.gateb = R(sb("gateb", [128, 16], F32))
            self.vec = R(sb("vec", [128, 16], F32))
            self.lamt = R(sb("lamt", [128, 256], F32))
            self.stats = Ring([R(sb(f"stat{i}", [128, 4], F32)) for i in range(6)])
            self.cstage = R(sb("cstage", [128, 128], F32))
            rem = nc.sbuf_bytes_remaining
            tn = (rem - 64) // 2 // 2 * 2
            self.Tt = sb("T", [128, tn], BF16)
            self.PS = [R(st.enter_context(nc.psum_tensor(f"ps{i}", [128, 512], F32)), f"ps{i}") for i in range(8)]
            for r_ in self.PS:
                r_.b.excl = True
            self.ds_const = S.dma_sem("dconst")
            self.dr_w = S.dma_ring("dw", 4)
            self.dr_x = S.dma_ring("dx", 2)
            self.dr_x2 = S.dma_ring("dxo", 2)
            self.dr_o = {k: S.dma_ring("do" + k, 2) for k in ("k", "v", "lat", "kr", "y")}
            self.ds_cache = S.dma_sem("dcache")

            self.consts()
            seqs = []
            for s in range(NSEQ):
                sd = SeqDesc()
                sd.x = self.xp
                sd.prompt = True
                sd.ncache = 0
                sd.blocks = []
                for b in range(SL // 512):
                    sd.blocks.append([SubTile(s * SL + (4 * b + j) * 128, 128, 4 * b + j, 4 * b + j, (4 * b + j) * 128) for j in range(4)])
                sd.o_y, sd.o_k, sd.o_v, sd.o_lat, sd.o_kr = self.o_y, self.o_k, self.o_v, self.o_lat, self.o_kr
                sd.obi = list(range(SL // 512))
                sd.OB = self.OB
                sd.OA = self.OA[:, 0:4 * SL].rearrange("p (h t) -> p h t", t=SL)
                seqs.append(sd)
            if self.SAMPLE:
                sd = SeqDesc()
                sd.x = self.xs
                sd.prompt = False
                sd.ncache = 8
                sd.blocks = [[SubTile(0, 64, 8, NTT - 1, 0)]]
                sd.obi = [SL // 512]
                lb = self.lamt.ap[:, :].bitcast(BF16)
                sd.OB = lb[:, 0:256].rearrange("p (h t) -> p h t", t=64)
                sd.OA = lb[:, 256:512].rearrange("p (h t) -> p h t", t=64)
                sd.o_y, sd.o_k, sd.o_v, sd.o_lat, sd.o_kr = self.o_ys, self.o_ks, self.o_vs, self.o_lats, self.o_krs
                seqs.append(sd)
            prompts = [q for q in seqs if q.prompt]
            smp = [q for q in seqs if not q.prompt]
            groups = [[q] for q in prompts[:-1]] + [prompts[-1:] + smp]
            for grp in groups:
                for pname, fn in (("mla", self.mla_pass), ("da", self.da_pass), ("fin", self.fin_pass)):
                    if pname in self.parts:
                        for gi, sd in enumerate(grp):
                            fn(sd, reload=(gi == 0))
            with nc.allow_low_precision(reason="bf16 matmul operands by design"), nc.Block() as block:
                self.stats_out = S.emit(block)
        return nc

    def consts(self):
        S = self.S
        ds = self.ds_const
        cs = self.cstage
        for src, dst in ((self.i_ident, self.ident), (self.i_shift, self.shiftm)):
            S.dma("sp", "dma_start", dict(out=cs.ap[:], in_=src), ds, writes=[cs.b])
            S.add("dve", "tensor_copy", dict(out=dst.ap[:], in_=cs.ap[:]), reads=[cs.b], writes=[dst.b])
        S.add("pool", "memset", dict(ap=self.ones.ap[:], constant=1.0), writes=[self.ones.b])
        S.add("pool", "memset", dict(ap=self.sel64.ap[:], constant=0.0), writes=[self.sel64.b])
        S.add("pool", "memset", dict(ap=self.sel64.ap[64:65, :], constant=1.0), writes=[self.sel64.b])
        NTT = self.NTT
        for src, dst, k in ((self.i_cosD, self.cosD, 8), (self.i_sinD, self.sinD, 8), (self.i_cosM, self.cosM, 16), (self.i_sinM, self.sinM, 16)):
            S.dma("sp", "dma_start", dict(out=dst.ap[:], in_=src.rearrange("p (t k) -> p t k", k=k)), ds, writes=[dst.b])
        S.dma("sp", "dma_start", dict(out=self.g_in.ap[:], in_=self.i_normg.partition_broadcast(128)), ds, writes=[self.g_in.b])
        S.dma("sp", "dma_start", dict(out=self.gateb.ap[:], in_=self.i_gateb), ds, writes=[self.gateb.b])
        S.dma("sp", "dma_start", dict(out=self.lamt.ap[:], in_=self.i_lam.partition_broadcast(128)), ds, writes=[self.lamt.b])
        v = self.vec
        S.dma("sp", "dma_start", dict(out=v.ap[:, 1:2], in_=self.i_hng), ds, writes=[v.b])
        lt = self.lamt
        S.add("dve", "tensor_tensor", dict(out=lt.ap[:, 0:64], in0=lt.ap[:, 0:64], in1=lt.ap[:, 64:128], op=ALU.mult), reads=[lt.b], writes=[lt.b])
        S.add("dve", "tensor_tensor", dict(out=lt.ap[:, 128:192], in0=lt.ap[:, 128:192], in1=lt.ap[:, 192:256], op=ALU.mult), reads=[lt.b], writes=[lt.b])
        S.add("dve", "tensor_reduce", dict(out=v.ap[:, 2:3], in_=lt.ap[:, 0:64], op=ALU.add, axis=mybir.AxisListType.X), reads=[lt.b], writes=[v.b])
        S.add("dve", "tensor_reduce", dict(out=v.ap[:, 3:4], in_=lt.ap[:, 128:192], op=ALU.add, axis=mybir.AxisListType.X), reads=[lt.b], writes=[v.b])
        S.add("act", "activation", dict(out=v.ap[:, 4:6], in_=v.ap[:, 2:4], func=AF.Exp), reads=[v.b], writes=[v.b])
        S.add("dve", "scalar_tensor_tensor", dict(out=v.ap[:, 0:1], in0=v.ap[:, 5:6], scalar=-LAM_INIT, in1=v.ap[:, 4:5], op0=ALU.add, op1=ALU.subtract), reads=[v.b], writes=[v.b])
        S.add("dve", "tensor_scalar", dict(out=v.ap[:, 1:2], in0=v.ap[:, 1:2], scalar1=(1.0 - LAM_INIT), scalar2=None, op0=ALU.mult), reads=[v.b], writes=[v.b])

    def new_arena(self, extra_regions=()):
        self.S.barrier()
        self.ar = Arena([self.Tt[:, :]] + list(extra_regions))
        ar = self.ar
        self.xring = Ring([ar.alloc([D], F32, f"xt{i}") for i in range(2)])
        self.hbring = Ring([ar.alloc([D], BF16, f"hb{i}") for i in range(2)])

    def rstd(self, src_ap, src_bufs, n, F, junk):
        S = self.S
        stt = self.stats.next()
        S.add("act", "activation", dict(out=junk.ap[0:n, 0:F], in_=src_ap, func=AF.Square, accum_out=stt.ap[0:n, 0:1]),
              reads=src_bufs, writes=[junk.b, stt.b])
        S.add("act", "activation", dict(out=stt.ap[0:n, 1:2], in_=stt.ap[0:n, 0:1], func=AF.Ln, scale=1.0 / F, bias=EPS),
              reads=[stt.b], writes=[stt.b])
        S.add("act", "activation", dict(out=stt.ap[0:n, 2:3], in_=stt.ap[0:n, 1:2], func=AF.Exp, scale=-0.5),
              reads=[stt.b], writes=[stt.b])
        return stt

    def load_x(self, sd, stl, ring=None):
        S = self.S
        xt = self.xring.next()
        n = stl.n
        S.dma("sp", "dma_start", dict(out=xt.ap[0:n, :], in_=sd.x[stl.row0:stl.row0 + n, :]), (ring or self.dr_x).next(), writes=[xt.b])
        return xt

    def prologue(self, sd, stl, hT_ap, hT_buf, copy_eng="act", outs=None):
        S = self.S
        n = stl.n
        xt = self.load_x(sd, stl)
        hb = self.hbring.next()
        stt = self.rstd(xt.ap[0:n, :], [xt.b], n, D, hb)
        S.add("dve", "scalar_tensor_tensor", dict(out=hb.ap[0:n, :], in0=xt.ap[0:n, :], scalar=stt.ap[0:n, 2:3], in1=self.g_in.ap[0:n, :],
                                                       op0=ALU.mult, op1=ALU.mult), reads=[xt.b, stt.b, self.g_in.b], writes=[hb.b])
        pb = self.PS[0]
        pbf = pb.ap[:, :].bitcast(BF16).rearrange("p (c t) -> p c t", t=128)
        for c in range(8):
            S.add("pe", "transpose", dict(out=pbf[:, c, 0:n], in_=hb.ap[0:n, c * 128:(c + 1) * 128], identity=self.ident.ap[0:n, 0:n]),
                  reads=[hb.b, self.ident.b], writes=[pb.b])
        if outs is None:
            outs = [(hT_ap, 0, 8, [hT_buf])]
        for ap_, lo, hi, bufs in outs:
            if copy_eng == "act":
                S.add("act", "copy", dict(out=ap_, in_=pbf[:, lo:hi, 0:n]), reads=[pb.b], writes=bufs)
            else:
                S.add("dve", "tensor_copy", dict(out=ap_, in_=pbf[:, lo:hi, 0:n]), reads=[pb.b], writes=bufs)
        return xt

    def proj_tok(self, hT, n, w_ap, ncols, bank, wb):
        S = self.S
        for c in range(8):
            S.add("pe", "matmul", dict(out=bank.ap[0:n, 0:ncols], lhsT=hT.ap[:, c, 0:n], rhs=w_ap[:, c, :], start=(c == 0), stop=(c == 7)),
                  reads=[hT.b] + wb, writes=[bank.b])

    def load_w(self, dst_ap, src_ap, K, c0, c1, name=None):
        S = self.S
        pieces = []
        a = c0
        while a < c1:
            b = min(a + 1024, c1)
            pb_ = Buf(f"w_{name}_{a}")
            S.dma("pool", "dma_start", dict(out=dst_ap[:, :, a - c0:b - c0], in_=src_ap[:, a:b].rearrange("(c p) n -> p c n", p=128)),
                  self.dr_w.next(), writes=[pb_])
            pieces.append((a - c0, b - c0, pb_))
            a = b
        self.W[name] = pieces

    def wbufs(self, name, a=None, b=None):
        out = []
        for lo, hi, pb_ in self.W[name]:
            if a is None or (lo < b and a < hi):
                out.append(pb_)
        return out

    def rope(self, eng, src4, dst4, n, cos_ap, sin_ap, G, half, tmp, src_bufs, dst_bufs):
        S = self.S
        t4 = tmp.ap[0:n, 0:G * 2 * half].rearrange("p (g k) -> p g k", k=2 * half)
        cb = cos_ap.unsqueeze(1).broadcast_to([n, G, half])
        sbb = sin_ap.unsqueeze(1).broadcast_to([n, G, half])
        rb = src_bufs + [self.cosM.b, self.sinM.b, self.cosD.b, self.sinD.b]
        S.add(eng, "scalar_tensor_tensor", dict(out=t4[:, :, 0:half], in0=src4[:, :, half:2 * half], scalar=-1.0, in1=sbb, op0=ALU.mult, op1=ALU.mult),
              reads=rb, writes=[tmp.b])
        S.add(eng, "tensor_tensor", dict(out=t4[:, :, half:2 * half], in0=src4[:, :, 0:half], in1=sbb, op=ALU.mult), reads=rb, writes=[tmp.b])
        S.add(eng, "tensor_tensor", dict(out=dst4[:, :, 0:half], in0=src4[:, :, 0:half], in1=cb, op=ALU.mult), reads=rb, writes=dst_bufs)
        S.add(eng, "tensor_tensor", dict(out=dst4[:, :, half:2 * half], in0=src4[:, :, half:2 * half], in1=cb, op=ALU.mult), reads=rb, writes=dst_bufs)
        S.add(eng, "tensor_tensor", dict(out=dst4, in0=dst4, in1=t4, op=ALU.add), reads=[tmp.b] + dst_bufs, writes=dst_bufs)

    def key_tiles(self, sd, blk, bi):
        if sd.prompt:
            out = [(kt, 128, 0, False) for kt in range(4 * bi)]
            for j in range(4):
                out.append((4 * bi + j, 128, 128 * j, True))
            return out
        return [(kt, 128, 0, False) for kt in range(sd.ncache)] + [(sd.ncache, 64, 0, False)]

    def mla_pass(self, sd, reload=True):
        S = self.S
        TK = self.TK
        NKT = TK // 128
        w1 = self.OA[:, 0:9728]
        regs = [self.OA[:, 9728:self.OA_N]] if self.OA_N > 9728 + 1024 else []
        if not sd.prompt and self.SL >= 2048:
            regs.append(self.U[:, 8 * TK + 10 * 520:8 * TK + NKT * 520])
        self.new_arena(regs)
        ar = self.ar
        w1in = w1[:, 0:5376].rearrange("p (c n) -> p c n", n=672)
        wuq = w1[:, 5376:7680].rearrange("p (c n) -> p c n", n=768)
        wuk = w1[:, 7680:8704].rearrange("p (c n) -> p c n", n=512)
        wuv = w1[:, 8704:9728].rearrange("p (c n) -> p c n", n=512)
        if reload:
            self.load_w(w1in, self.w_in, 8, C_CQ, C_MZ, "w1in")
            self.load_w(wuq, self.i_wuq, 3, 0, 768, "wuq")
            self.load_w(wuk, self.i_wuk, 2, 0, 512, "wuk")
            self.load_w(wuv, self.i_wuv, 2, 0, 512, "wuv")
        KT = self.U[:, 0:8 * TK].rearrange("p (h t) -> p h t", t=TK)
        VM = self.U[:, 8 * TK:8 * TK + NKT * 520].rearrange("p (k c) -> p k c", c=520)
        VMf = self.U[:, 8 * TK:8 * TK + (NKT + 1) * 520]
        ktb = [Buf(f"ktm{i}") for i in range(NKT)]
        vmb = [Buf(f"vm{i}") for i in range(NKT)]
        gq = ar.alloc([384], F32, "gq")
        gkv = ar.alloc([256], F32, "gkv")
        S.dma("sp", "dma_start", dict(out=gq.ap[:], in_=self.i_gq.partition_broadcast(128)), self.ds_const, writes=[gq.b])
        S.dma("sp", "dma_start", dict(out=gkv.ap[:], in_=self.i_gkv.partition_broadcast(128)), self.ds_const, writes=[gkv.b])
        nkc = (sd.ncache + 1) * 128 if not sd.prompt else TK
        nvz = (nkc // 128 + 1) * 520
        S.add("dve", "memset", dict(ap=VMf[:, 0:nvz // 2], constant=0.0), writes=vmb)
        S.add("pool", "memset", dict(ap=VMf[:, nvz // 2:nvz], constant=0.0), writes=vmb)
        S.add("pool", "memset", dict(ap=VM[:, 0:nkc // 128, 512:520], constant=1.0), writes=vmb)
        S.add("pool", "memset", dict(ap=KT[96:128, 0:4, 0:nkc], constant=0.0), writes=ktb)
        S.add("dve", "memset", dict(ap=KT[96:128, 4:8, 0:nkc], constant=0.0), writes=ktb)
        if STOP <= 1:
            return
        hTr = Ring([ar.alloc([8, 128], BF16, f"hTs{i}") for i in range(2)])
        pTr = Ring([ar.alloc([512], BF16, f"pT{i}") for i in range(3)])
        QT = ar.alloc([8, 512], BF16, "QT")
        cfr = Ring([ar.alloc([256], F32, f"cf{i}") for i in range(1)])
        krfr = Ring([ar.alloc([32], F32, f"krf{i}") for i in range(2)])
        cbf = ar.alloc([256], BF16, "cbf")
        cqbf = ar.alloc([384], BF16, "cqbf")
        ccT = ar.alloc([5, 128], BF16, "ccT")
        kfull = ar.alloc([8, 96], BF16, "kfull")
        qtok = ar.alloc([8, 96], BF16, "qtok")
        rtmp = ar.alloc([8 * 32], F32, "rtmp")
        u_sb = ar.alloc([512], F32, "u_sb")
        on_sb = ar.alloc([512], BF16, "on_sb")
        rr = ar.alloc([512], BF16, "rr")
        S.add("pool", "memset", dict(ap=rr.ap[:, :], constant=0.0), writes=[rr.b])
        rr32 = R(u_sb.ap, "rr32")
        junk = ar.alloc([384], BF16, "junk")
        S.add("pool", "memset", dict(ap=QT.ap[96:128, :, :], constant=0.0), writes=[QT.b])
        S.add("pool", "memset", dict(ap=on_sb.ap[:, :], constant=0.0), writes=[on_sb.b])
        PS = self.PS

        ccTr = Ring([ccT, ar.alloc([5, 128], BF16, "ccT1")])
        kfr_ = Ring([kfull, ar.alloc([8, 96], BF16, "kfull1")])
        rtmpC = ar.alloc([4 * 32], F32, "rtmpC")
        if os.environ.get("KDEBUG"):
            print("MLA arena", ar.off, [r.shape for r in ar.regions])

        def ctrans(cc, cb_ap, cb_b, n):
            pb = PS[4]
            pbf = pb.ap[:, :].bitcast(BF16).rearrange("p (c t) -> p c t", t=128)
            for c in range(2):
                S.add("pe", "transpose", dict(out=pbf[:, c, 0:n], in_=cb_ap[0:n, c * 128:(c + 1) * 128], identity=self.ident.ap[0:n, 0:n]),
                      reads=[cb_b, self.ident.b], writes=[pb.b])
            S.add("dve", "tensor_copy", dict(out=cc.ap[:, 0:2, 0:n], in_=pbf[:, 0:2, 0:n]), reads=[pb.b], writes=[cc.b])

        def kside2(cc, kf_, kt, n):
            for c in range(2):
                S.add("pe", "matmul", dict(out=PS[5].ap[0:n, :], lhsT=cc.ap[:, c, 0:n], rhs=wuk[:, c, :], start=(c == 0), stop=(c == 1)),
                      reads=[cc.b] + self.wbufs("wuk"), writes=[PS[5].b])
            for c in range(2):
                S.add("pe", "matmul", dict(out=PS[6].ap[0:n, :], lhsT=cc.ap[:, c, 0:n], rhs=wuv[:, c, :], start=(c == 0), stop=(c == 1)),
                      reads=[cc.b] + self.wbufs("wuv"), writes=[PS[6].b])
            S.add("dve", "tensor_copy", dict(out=kf_.ap[0:n, :, 0:64], in_=PS[5].ap[0:n, :].rearrange("p (h d) -> p h d", d=64)),
                  reads=[PS[5].b], writes=[kf_.b])
            S.add("dve", "tensor_copy", dict(out=VM[0:n, kt, 0:512], in_=PS[6].ap[0:n, :]), reads=[PS[6].b], writes=[vmb[kt]])
            pb2 = PS[7]
            pbf2 = pb2.ap[:, :].bitcast(BF16).rearrange("p (c t) -> p c t", t=128)
            for h in range(8):
                S.add("pe", "transpose", dict(out=pbf2[0:96, h, 0:n], in_=kf_.ap[0:n, h, :], identity=self.ident.ap[0:n, 0:n]),
                      reads=[kf_.b, self.ident.b], writes=[pb2.b])
            S.add("dve", "tensor_copy", dict(out=KT[0:96, :, kt * 128:kt * 128 + n], in_=pbf2[0:96, :, 0:n]), reads=[pb2.b], writes=[ktb[kt]])

        if sd.ncache:
            cst = ar.alloc([8, 256], BF16, "cst")
            kst = ar.alloc([8, 32], BF16, "kst")
            S.dma("pool", "dma_start", dict(out=cst.ap[:], in_=self.clat.rearrange("(t p) c -> p t c", p=128)), self.ds_cache, writes=[cst.b])
            S.dma("pool", "dma_start", dict(out=kst.ap[:], in_=self.ckr.rearrange("(t p) c -> p t c", p=128)), self.ds_cache, writes=[kst.b])

            def cB(t):
                cc, kf_ = ccTr.next(), kfr_.next()
                S.add("pool", "tensor_copy", dict(out=kf_.ap[:, :, 64:96], in_=kst.ap[:, t, :].unsqueeze(1).broadcast_to([128, 8, 32])),
                      reads=[kst.b], writes=[kf_.b])
                ctrans(cc, cst.ap[:, t, :], cst.b, 128)
                cctx[t] = (cc, kf_)

            def cC(t):
                cc, kf_ = cctx[t]
                kside2(cc, kf_, t, 128)
            cctx = {}
            run_skewed([cB, cC], list(range(sd.ncache)))

        def stA(c):
            c["hT"] = hTr.next()
            self.prologue(sd, c["stl"], c["hT"].ap[:, :, 0:c["stl"].n], c["hT"].b, copy_eng="dve")

        def stB(c):
            stl, hT = c["stl"], c["hT"]
            n = stl.n
            self.proj_tok(hT, n, w1in[:, :, 0:384], 384, PS[1], self.wbufs("w1in", 0, 384))
            self.proj_tok(hT, n, w1in[:, :, 384:672], 288, PS[2], self.wbufs("w1in", 384, 672))
            cc, kf_ = ccTr.next(), kfr_.next()
            c["cc"], c["kf"] = cc, kf_
            stt = self.rstd(PS[2].ap[0:n, 0:256], [PS[2].b], n, 256, junk)
            cf = cfr.next()
            S.add("dve", "scalar_tensor_tensor", dict(out=cf.ap[0:n, :], in0=PS[2].ap[0:n, 0:256], scalar=stt.ap[0:n, 2:3], in1=gkv.ap[0:n, :],
                                                      op0=ALU.mult, op1=ALU.mult), reads=[PS[2].b, stt.b, gkv.b], writes=[cf.b])
            S.dma("pool", "dma_start", dict(out=sd.o_lat[stl.row0:stl.row0 + n, :], in_=cf.ap[0:n, :]), self.dr_o["lat"].next(), reads=[cf.b])
            S.add("pool", "tensor_copy", dict(out=cbf.ap[0:n, :], in_=cf.ap[0:n, :]), reads=[cf.b], writes=[cbf.b])
            krf = krfr.next()
            src = PS[2].ap[0:n, 256:288].unsqueeze(1)
            dst = krf.ap[0:n, :].unsqueeze(1)
            self.rope("dve", src, dst, n, self.cosM.ap[0:n, stl.tt, :], self.sinM.ap[0:n, stl.tt, :], 1, 16, rtmp, [PS[2].b], [krf.b])
            S.dma("pool", "dma_start", dict(out=sd.o_kr[stl.row0:stl.row0 + n, :], in_=krf.ap[0:n, :]), self.dr_o["kr"].next(), reads=[krf.b])
            S.add("pool", "tensor_copy", dict(out=kf_.ap[0:n, :, 64:96], in_=krf.ap[0:n, :].unsqueeze(1).broadcast_to([n, 8, 32])),
                  reads=[krf.b], writes=[kf_.b])
            stq = self.rstd(PS[1].ap[0:n, 0:384], [PS[1].b], n, 384, junk)
            S.add("dve", "scalar_tensor_tensor", dict(out=cqbf.ap[0:n, :], in0=PS[1].ap[0:n, 0:384], scalar=stq.ap[0:n, 2:3], in1=gq.ap[0:n, :],
                                                      op0=ALU.mult, op1=ALU.mult), reads=[PS[1].b, stq.b, gq.b], writes=[cqbf.b])
            pb = PS[3]
            pbf = pb.ap[:, :].bitcast(BF16).rearrange("p (c t) -> p c t", t=128)
            for ch in range(3):
                S.add("pe", "transpose", dict(out=pbf[:, ch, 0:n], in_=cqbf.ap[0:n, ch * 128:(ch + 1) * 128], identity=self.ident.ap[0:n, 0:n]),
                      reads=[cqbf.b, self.ident.b], writes=[pb.b])
            S.add("dve", "tensor_copy", dict(out=cc.ap[:, 2:5, 0:n], in_=pbf[:, 0:3, 0:n]), reads=[pb.b], writes=[cc.b])
            ctrans(cc, cbf.ap, cbf.b, n)

        def stC(c):
            stl, cc, kf_, j = c["stl"], c["cc"], c["kf"], c["j"]
            n = stl.n
            kside2(cc, kf_, stl.kt, n)
            for hf in range(2):
                bank = PS[5 + hf]
                for ch in range(3):
                    S.add("pe", "matmul", dict(out=bank.ap[0:n, 0:384], lhsT=cc.ap[:, 2 + ch, 0:n], rhs=wuq[:, ch, hf * 384:(hf + 1) * 384],
                                               start=(ch == 0), stop=(ch == 2)), reads=[cc.b] + self.wbufs("wuq", hf * 384, (hf + 1) * 384), writes=[bank.b])
                q3 = bank.ap[0:n, 0:384].rearrange("p (h d) -> p h d", d=96)
                S.add("dve", "tensor_copy", dict(out=qtok.ap[0:n, 4 * hf:4 * hf + 4, 0:64], in_=q3[:, :, 0:64]), reads=[bank.b], writes=[qtok.b])
                self.rope("dve", q3[:, :, 64:96], qtok.ap[0:n, 4 * hf:4 * hf + 4, 64:96], n, self.cosM.ap[0:n, stl.tt, :], self.sinM.ap[0:n, stl.tt, :],
                          4, 16, rtmpC, [bank.b], [qtok.b])
            pb2 = PS[7]
            pbf2 = pb2.ap[:, :].bitcast(BF16).rearrange("p (c t) -> p c t", t=128)
            for h in range(8):
                S.add("pe", "transpose", dict(out=pbf2[0:96, h, 0:n], in_=qtok.ap[0:n, h, :], identity=self.ident.ap[0:n, 0:n]),
                      reads=[qtok.b, self.ident.b], writes=[pb2.b])
            S.add("act", "copy", dict(out=QT.ap[0:96, :, j * 128:j * 128 + n], in_=pbf2[0:96, :, 0:n]), reads=[pb2.b], writes=[QT.b])

        for bi, blk in enumerate(sd.blocks):
            nq = sum(s.n for s in blk)
            run_skewed([stA, stB, stC], [dict(stl=stl, j=j) for j, stl in enumerate(blk)])
            if STOP <= 7:
                continue
            tiles = self.key_tiles(sd, blk, bi)
            c0 = blk[0].c0
            units = [(h, ti) for h in range(8) for ti in range(len(tiles))]
            accs = Ring([PS[3], PS[4], PS[5]])
            sring = Ring([PS[0], PS[1], PS[2]])
            pend = []
            cur_acc = {}

            def issue_S(u):
                h, ti = u
                kt, nk, q0, mask = tiles[ti]
                sb_ = sring.next()
                S.add("pe", "matmul", dict(out=sb_.ap[0:nk, q0:nq], lhsT=KT[:, h, kt * 128:kt * 128 + nk], rhs=QT.ap[:, h, q0:nq], start=True, stop=True),
                      reads=[ktb[kt], QT.b], writes=[sb_.b])
                pT = pTr.next()
                S.add("act", "activation", dict(out=pT.ap[0:nk, q0:nq], in_=sb_.ap[0:nk, q0:nq], func=AF.Exp, scale=SC_MLA), reads=[sb_.b], writes=[pT.b])
                if mask:
                    S.add("pool", "memset", dict(ap=pT.ap[64:128, q0:q0 + 64], constant=0.0), writes=[pT.b])
                return pT

            def issue_AV(u, pT):
                h, ti = u
                kt, nk, q0, mask = tiles[ti]
                if ti == 0:
                    cur_acc[h] = accs.next()
                acc = cur_acc[h]
                S.add("pe", "matmul", dict(out=acc.ap[:, q0:nq], lhsT=VMf[0:nk, kt * 520 + h:kt * 520 + h + 1017:8], rhs=pT.ap[0:nk, q0:nq], start=(ti == 0), stop=(ti == len(tiles) - 1)),
                      reads=[vmb[kt], pT.b], writes=[acc.b])
                if ti == len(tiles) - 1:
                    fin(h, acc)

            deferred = []

            def defer(n, fn):
                deferred.append([n, fn])

            def tick(flush=False):
                while True:
                    due = [d for d in deferred if flush or d[0] <= 0]
                    if not due:
                        break
                    for d in due:
                        deferred.remove(d)
                    for d in due:
                        d[1]()
                for d in deferred:
                    d[0] -= 1

            def fin(h, acc):
                t, od = h // 2, h % 2
                S.add("act", "activation", dict(out=rr32.ap[64:65, 0:nq], in_=acc.ap[64:65, 0:nq], func=AF.Ln), reads=[acc.b], writes=[rr32.b])
                S.add("act", "activation", dict(out=rr.ap[64:65, 0:nq], in_=rr32.ap[64:65, 0:nq], func=AF.Exp, scale=-1.0), reads=[rr32.b], writes=[rr.b])
                S.add("dve", "tensor_copy", dict(out=u_sb.ap[0:64, 0:nq], in_=acc.ap[0:64, 0:nq]), reads=[acc.b], writes=[u_sb.b])

                def stage_b():
                    S.add("pe", "matmul", dict(out=PS[6].ap[0:64, 0:nq], lhsT=self.sel64.ap[:, 0:64], rhs=rr.ap[:, 0:nq], start=True, stop=True),
                          reads=[rr.b, self.sel64.b], writes=[PS[6].b])
                    if od == 0:
                        S.add("dve", "tensor_tensor", dict(out=sd.OB[0:64, t, c0:c0 + nq], in0=u_sb.ap[0:64, 0:nq], in1=PS[6].ap[0:64, 0:nq], op=ALU.mult),
                              reads=[u_sb.b, PS[6].b], writes=[self.OBb[sd.obi[bi]]])
                    else:
                        S.add("dve", "tensor_tensor", dict(out=on_sb.ap[0:64, 0:nq], in0=u_sb.ap[0:64, 0:nq], in1=PS[6].ap[0:64, 0:nq], op=ALU.mult),
                              reads=[u_sb.b, PS[6].b], writes=[on_sb.b])

                        def stage_c():
                            S.add("pe", "matmul", dict(out=PS[7].ap[:, 0:nq], lhsT=self.shiftm.ap[:, :], rhs=on_sb.ap[:, 0:nq], start=True, stop=True),
                                  reads=[on_sb.b, self.shiftm.b], writes=[PS[7].b])
                            S.add("dve", "tensor_copy", dict(out=sd.OB[64:128, t, c0:c0 + nq], in_=PS[7].ap[64:128, 0:nq]), reads=[PS[7].b], writes=[self.OBb[sd.obi[bi]]])
                        defer(2, stage_c)
                defer(3, stage_b)

            LOOK = 2
            for i, u in enumerate(units):
                pend.append((u, issue_S(u)))
                if len(pend) > LOOK:
                    issue_AV(*pend.pop(0))
                    tick()
            while pend:
                issue_AV(*pend.pop(0))
                tick()
            tick(flush=True)

    def da_pass(self, sd, reload=True):
        S = self.S
        TK = self.TK
        NKT = TK // 128
        regs = [self.U[:, 8 * TK + 12288:self.U_N]] if self.U_N - (8 * TK + 12288) > 1024 else []
        if not sd.prompt and self.SL >= 2048:
            regs.append(self.U[:, 4 * TK + 9 * 512:8 * TK])
        self.new_arena(regs)
        ar = self.ar
        w2 = self.U[:, 8 * TK:8 * TK + 12288].rearrange("p (c n) -> p c n", n=1536)
        if reload:
            self.load_w(w2, self.w_in, 8, 0, 1536, "w2")
        KT = self.U[:, 0:4 * TK].rearrange("p (h t) -> p h t", t=TK)
        VD = self.U[:, 4 * TK:8 * TK].rearrange("p (k c) -> p k c", c=512)
        ktb = [Buf(f"ktd{i}") for i in range(NKT)]
        vdb = [Buf(f"vd{i}") for i in range(NKT)]
        hTr = Ring([ar.alloc([8, 128], BF16, f"hTs{i}") for i in range(2)])
        pTr = Ring([ar.alloc([512], BF16, f"pT{i}") for i in range(4)])
        QT = ar.alloc([4, 2, 512], BF16, "QT")
        S.add("pool", "memset", dict(ap=QT.ap[64:128, :, 0, :], constant=0.0), writes=[QT.b])
        S.add("pool", "memset", dict(ap=QT.ap[0:64, :, 1, :], constant=0.0), writes=[QT.b])
        kfr = Ring([ar.alloc([512], F32, f"kf{i}") for i in range(1)])
        vfr = Ring([ar.alloc([512], F32, f"vf{i}") for i in range(1)])
        kb = ar.alloc([512], BF16, "kb")
        qb = ar.alloc([512], BF16, "qb")
        rtmp = ar.alloc([8 * 16], F32, "rtmp")
        t0 = ar.alloc([512], F32, "t0")
        t1 = ar.alloc([512], F32, "t1")
        oraw = ar.alloc([512], F32, "oraw")
        sq = kb
        lnr = ar.alloc([512], F32, "lnr")
        PS = self.PS
        OA = sd.OA

        def k_transposes(kb_ap, kb_b, kt, n):
            pb = PS[4]
            pbf = pb.ap[:, :].bitcast(BF16).rearrange("p (c t) -> p c t", t=128)
            for h in range(4):
                S.add("pe", "transpose", dict(out=pbf[:, h, 0:n], in_=kb_ap[0:n, h * 128:(h + 1) * 128], identity=self.ident.ap[0:n, 0:n]),
                      reads=[kb_b, self.ident.b], writes=[pb.b])
            S.add("act", "copy", dict(out=KT[:, :, kt * 128:kt * 128 + n], in_=pbf[:, 0:4, 0:n]), reads=[pb.b], writes=[ktb[kt]])

        if sd.ncache:
            kst = ar.alloc([8, 512], BF16, "kst")
            S.dma("pool", "dma_start", dict(out=kst.ap[:], in_=self.cdk.rearrange("(t p) c -> p t c", p=128)), self.ds_cache, writes=[kst.b])
            S.dma("pool", "dma_start", dict(out=VD[:, 0:8, :], in_=self.cdv.rearrange("(t p) c -> p t c", p=128)), self.ds_cache, writes=vdb[0:8])
            for t in range(sd.ncache):
                k_transposes(kst.ap[:, t, :], kst.b, t, 128)

        for bi, blk in enumerate(sd.blocks):
            nq = sum(s.n for s in blk)
            c0 = blk[0].c0
            for j, stl in enumerate(blk):
                n = stl.n
                kt = stl.kt
                hT = hTr.next()
                self.prologue(sd, stl, hT.ap[:, :, 0:n], hT.b)
                self.proj_tok(hT, n, w2[:, :, 0:512], 512, PS[1], self.wbufs("w2", 0, 512))
                self.proj_tok(hT, n, w2[:, :, 512:1024], 512, PS[2], self.wbufs("w2", 512, 1024))
                self.proj_tok(hT, n, w2[:, :, 1024:1536], 512, PS[3], self.wbufs("w2", 1024, 1536))
                cosd = self.cosD.ap[0:n, stl.tt, :]
                sind = self.sinD.ap[0:n, stl.tt, :]
                kf = kfr.next()
                S.add("act", "copy", dict(out=kf.ap[0:n, :], in_=PS[2].ap[0:n, :]), reads=[PS[2].b], writes=[kf.b])
                k4 = PS[2].ap[0:n, :].rearrange("p (g d) -> p g d", d=64)[:, :, 0:16]
                kf4 = kf.ap[0:n, :].rearrange("p (g d) -> p g d", d=64)[:, :, 0:16]
                self.rope("dve", k4, kf4, n, cosd, sind, 8, 8, rtmp, [PS[2].b], [kf.b])
                S.dma("pool", "dma_start", dict(out=sd.o_k[stl.row0:stl.row0 + stl.n, :], in_=kf.ap[0:stl.n, :]), self.dr_o["k"].next(), reads=[kf.b])
                S.add("pool", "tensor_copy", dict(out=kb.ap[0:n, :], in_=kf.ap[0:n, :]), reads=[kf.b], writes=[kb.b])
                k_transposes(kb.ap, kb.b, kt, n)
                vf = vfr.next()
                S.add("act", "copy", dict(out=vf.ap[0:n, :], in_=PS[3].ap[0:n, :]), reads=[PS[3].b], writes=[vf.b])
                S.dma("pool", "dma_start", dict(out=sd.o_v[stl.row0:stl.row0 + stl.n, :], in_=vf.ap[0:stl.n, :]), self.dr_o["v"].next(), reads=[vf.b])
                S.add("dve", "tensor_copy", dict(out=VD[0:n, kt, :], in_=PS[3].ap[0:n, :]), reads=[PS[3].b], writes=[vdb[kt]])
                S.add("act", "copy", dict(out=qb.ap[0:n, :], in_=PS[1].ap[0:n, :]), reads=[PS[1].b], writes=[qb.b])
                q4 = PS[1].ap[0:n, :].rearrange("p (g d) -> p g d", d=64)[:, :, 0:16]
                qb4 = qb.ap[0:n, :].rearrange("p (g d) -> p g d", d=64)[:, :, 0:16]
                self.rope("dve", q4, qb4, n, cosd, sind, 8, 8, rtmp, [PS[1].b], [qb.b])
                pb = PS[5]
                pbf = pb.ap[:, :].bitcast(BF16).rearrange("p (c t) -> p c t", t=128)
                for h in range(4):
                    S.add("pe", "transpose", dict(out=pbf[:, h, 0:n], in_=qb.ap[0:n, h * 128:(h + 1) * 128], identity=self.ident.ap[0:n, 0:n]),
                          reads=[qb.b, self.ident.b], writes=[pb.b])
                S.add("act", "copy", dict(out=QT.ap[0:64, :, 0, j * 128:j * 128 + n], in_=pbf[0:64, 0:4, 0:n]), reads=[pb.b], writes=[QT.b])
                S.add("dve", "tensor_copy", dict(out=QT.ap[64:128, :, 1, j * 128:j * 128 + n], in_=pbf[64:128, 0:4, 0:n]), reads=[pb.b], writes=[QT.b])
            tiles = self.key_tiles(sd, blk, bi)
            units = [(h, c, ti) for h in range(4) for c in range(2) for ti in range(len(tiles))]
            accs = Ring([(PS[3], PS[4]), (PS[5], PS[6])])
            sring = Ring([PS[0], PS[1], PS[2]])
            pend = []
            cur_acc = {}
            tt = {0: t0, 1: t1}

            def issue_S(u):
                h, c, ti = u
                kt, nk, q0, mask = tiles[ti]
                sb_ = sring.next()
                S.add("pe", "matmul", dict(out=sb_.ap[0:nk, q0:nq], lhsT=KT[:, h, kt * 128:kt * 128 + nk], rhs=QT.ap[:, h, c, q0:nq],
                                               start=True, stop=True), reads=[ktb[kt], QT.b], writes=[sb_.b])
                pT = pTr.next()
                S.add("act", "activation", dict(out=pT.ap[0:nk, q0:nq], in_=sb_.ap[0:nk, q0:nq], func=AF.Exp, scale=SC_DA), reads=[sb_.b], writes=[pT.b])
                if mask:
                    S.add("pool", "memset", dict(ap=pT.ap[64:128, q0:q0 + 64], constant=0.0), writes=[pT.b])
                return pT

            def issue_AV(u, pT):
                h, c, ti = u
                kt, nk, q0, mask = tiles[ti]
                if ti == 0:
                    cur_acc[(h, c)] = accs.next()
                au, asum = cur_acc[(h, c)]
                first, last = (ti == 0), (ti == len(tiles) - 1)
                S.add("pe", "matmul", dict(out=au.ap[:, q0:nq], lhsT=VD[0:nk, kt, h * 128:(h + 1) * 128], rhs=pT.ap[0:nk, q0:nq], start=first, stop=last),
                      reads=[vdb[kt], pT.b], writes=[au.b])
                S.add("pe", "matmul", dict(out=asum.ap[:, q0:nq], lhsT=self.ones.ap[0:nk, :], rhs=pT.ap[0:nk, q0:nq], start=first, stop=last),
                      reads=[self.ones.b, pT.b], writes=[asum.b])
                if last:
                    t = tt[c]
                    if c == 0:
                        S.add("dve", "reciprocal", dict(out=t.ap[:, 0:nq], in_=asum.ap[:, 0:nq]), reads=[asum.b], writes=[t.b])
                    else:
                        S.add("act", "activation", dict(out=t.ap[:, 0:nq], in_=asum.ap[:, 0:nq], func=AF.Ln), reads=[asum.b], writes=[t.b])
                        S.add("act", "activation", dict(out=t.ap[:, 0:nq], in_=t.ap[:, 0:nq], func=AF.Exp, scale=-1.0), reads=[t.b], writes=[t.b])
                    S.add("dve", "tensor_tensor", dict(out=t.ap[:, 0:nq], in0=t.ap[:, 0:nq], in1=au.ap[:, 0:nq], op=ALU.mult), reads=[au.b, t.b], writes=[t.b])
                    if c == 1:
                        fin(h)

            deferred = []

            def defer(n, fn):
                deferred.append([n, fn])

            def tick(flush=False):
                while True:
                    due = [d for d in deferred if flush or d[0] <= 0]
                    if not due:
                        break
                    for d in due:
                        deferred.remove(d)
                    for d in due:
                        d[1]()
                for d in deferred:
                    d[0] -= 1

            def fin(h):
                v = self.vec
                S.add("dve", "scalar_tensor_tensor", dict(out=oraw.ap[:, 0:nq], in0=t1.ap[:, 0:nq], scalar=v.ap[:, 0:1], in1=t0.ap[:, 0:nq], op0=ALU.mult, op1=ALU.add),
                      reads=[t0.b, t1.b, v.b], writes=[oraw.b])
                S.add("act", "activation", dict(out=sq.ap[:, 0:nq], in_=oraw.ap[:, 0:nq], func=AF.Square), reads=[oraw.b], writes=[sq.b])

                def stage_b():
                    S.add("pe", "matmul", dict(out=PS[7].ap[:, 0:nq], lhsT=self.ones.ap[:, :], rhs=sq.ap[:, 0:nq], start=True, stop=True), reads=[sq.b, self.ones.b], writes=[PS[7].b])
                    S.add("act", "activation", dict(out=lnr.ap[:, 0:nq], in_=PS[7].ap[:, 0:nq], func=AF.Ln, scale=1.0 / 128, bias=EPS), reads=[PS[7].b], writes=[lnr.b])
                    S.add("act", "activation", dict(out=lnr.ap[:, 0:nq], in_=lnr.ap[:, 0:nq], func=AF.Exp, scale=-0.5), reads=[lnr.b], writes=[lnr.b])
                    S.add("dve", "scalar_tensor_tensor", dict(out=OA[:, h, c0:c0 + nq], in0=oraw.ap[:, 0:nq], scalar=v.ap[:, 1:2], in1=lnr.ap[:, 0:nq], op0=ALU.mult, op1=ALU.mult),
                          reads=[oraw.b, lnr.b, v.b], writes=[self.OAb[sd.obi[bi]]])
                defer(5, stage_b)

            LOOK = 2
            for i, u in enumerate(units):
                pend.append((u, issue_S(u)))
                if len(pend) > LOOK:
                    issue_AV(*pend.pop(0))
                    tick()
            while pend:
                issue_AV(*pend.pop(0))
                tick()
            tick(flush=True)

    def fin_pass(self, sd, reload=True):
        S = self.S
        SL = self.SL
        self.new_arena([self.U[:, 40960:self.U_N]] if self.U_N - 40960 > 1024 else [])
        ar = self.ar
        U = self.U
        wz = U[:, 0:4096].rearrange("p (c n) -> p c n", n=512)
        wg = U[:, 4096:24576].rearrange("p (c n) -> p c n", n=2560)
        wba = U[:, 24576:28672].rearrange("p (c n) -> p c n", n=1024)
        wbb = U[:, 28672:32768].rearrange("p (c n) -> p c n", n=1024)
        wout = U[:, 32768:40960].rearrange("p (c n) -> p c n", n=1024)
        if reload:
            self.load_w(wz, self.w_in, 8, C_DAZ, C_DAZ + 512, "wz")
            self.load_w(wg, self.w_in, 8, C_MZ, IN_COLS, "wg")
            self.load_w(wba, self.i_wba, 4, 0, 1024, "wba")
            self.load_w(wbb, self.i_wbb, 4, 0, 1024, "wbb")
            self.load_w(wout, self.i_wout, 8, 0, 1024, "wout")
        gf = ar.alloc([D], F32, "gf")
        S.dma("sp", "dma_start", dict(out=gf.ap[:], in_=self.i_gf.partition_broadcast(128)), self.ds_const, writes=[gf.b])
        hT = ar.alloc([8, 512], BF16, "hT")
        oz = ar.alloc([8, 512], BF16, "oz")
        mT = ar.alloc([8, 512], BF16, "mT")
        sg = Ring([ar.alloc([512], F32, f"sg{i}") for i in range(2)])
        tg = Ring([ar.alloc([512], F32, f"tg{i}") for i in range(2)])
        PS = self.PS
        OA = sd.OA
        OB = sd.OB
        zb = Ring([PS[i] for i in range(1, 8)])
        ojunk = [None]
        pro_x = Ring([self.xring.items[0]])
        out_x = Ring([self.xring.items[1]])
        for bi, blk in enumerate(sd.blocks):
            nq = sum(s.n for s in blk)
            c0 = blk[0].c0
            self.xring = pro_x
            if sd.prompt and bi >= 1:
                c0p = sd.blocks[bi - 1][0].c0
                hch = [OA[:, c, c0p:c0p + 512] for c in range(4)] + [OB[:, c, c0p:c0p + 512] for c in range(4)]
                hbufs = [self.OAb[bi - 1], self.OBb[bi - 1]]
                for j, stl in enumerate(blk):
                    self.prologue(sd, stl, None, None, outs=[(OA[:, :, c0p + j * 128:c0p + j * 128 + stl.n], 0, 4, [self.OAb[bi - 1]]),
                                                             (OB[:, :, c0p + j * 128:c0p + j * 128 + stl.n], 4, 8, [self.OBb[bi - 1]])])
            else:
                hch = [hT.ap[:, c, :] for c in range(8)]
                hbufs = [hT.b]
                for j, stl in enumerate(blk):
                    self.prologue(sd, stl, hT.ap[:, :, j * 128:j * 128 + stl.n], hT.b)
            for m in range(8):
                bank = zb.next()
                wsrc = wz[:, :, m * 128:(m + 1) * 128] if m < 4 else wg[:, :, (m - 4) * 128:(m - 3) * 128]
                wzb = self.wbufs("wz") if m < 4 else self.wbufs("wg", (m - 4) * 128, (m - 3) * 128)
                for c in range(8):
                    S.add("pe", "matmul", dict(out=bank.ap[:, 0:nq], lhsT=wsrc[:, c, :], rhs=hch[c][:, 0:nq], start=(c == 0), stop=(c == 7)),
                          reads=hbufs + wzb, writes=[bank.b])
                s_ = sg.next()
                S.add("act", "activation", dict(out=s_.ap[:, 0:nq], in_=bank.ap[:, 0:nq], func=AF.Silu), reads=[bank.b], writes=[s_.b])
                osrc = OA[:, m, c0:c0 + nq] if m < 4 else OB[:, m - 4, c0:c0 + nq]
                ob = [self.OAb[sd.obi[bi]]] if m < 4 else [self.OBb[sd.obi[bi]]]
                S.add("dve", "tensor_tensor", dict(out=oz.ap[:, m, 0:nq], in0=s_.ap[:, 0:nq], in1=osrc, op=ALU.mult), reads=[s_.b] + ob, writes=[oz.b])
            for m in range(8):
                bya, byb, bga, bgb = zb.next(), zb.next(), zb.next(), zb.next()
                for c in range(4):
                    S.add("pe", "matmul", dict(out=bya.ap[:, 0:nq], lhsT=wba[:, c, m * 128:(m + 1) * 128], rhs=oz.ap[:, c, 0:nq], start=(c == 0), stop=(c == 3)),
                          reads=[oz.b] + self.wbufs("wba"), writes=[bya.b])
                for c in range(4):
                    S.add("pe", "matmul", dict(out=byb.ap[:, 0:nq], lhsT=wbb[:, c, m * 128:(m + 1) * 128], rhs=oz.ap[:, 4 + c, 0:nq], start=(c == 0), stop=(c == 3)),
                          reads=[oz.b] + self.wbufs("wbb"), writes=[byb.b])
                for c in range(8):
                    S.add("pe", "matmul", dict(out=bga.ap[:, 0:nq], lhsT=wg[:, c, 512 + m * 128:512 + (m + 1) * 128], rhs=hch[c][:, 0:nq], start=(c == 0), stop=(c == 7)),
                          reads=hbufs + self.wbufs("wg", 512 + m * 128, 512 + (m + 1) * 128), writes=[bga.b])
                for c in range(8):
                    S.add("pe", "matmul", dict(out=bgb.ap[:, 0:nq], lhsT=wg[:, c, 1536 + m * 128:1536 + (m + 1) * 128], rhs=hch[c][:, 0:nq], start=(c == 0), stop=(c == 7)),
                          reads=hbufs + self.wbufs("wg", 1536 + m * 128, 1536 + (m + 1) * 128), writes=[bgb.b])
                ga, gb_ = sg.next(), sg.next()
                S.add("act", "activation", dict(out=ga.ap[:, 0:nq], in_=bga.ap[:, 0:nq], func=AF.Sigmoid, bias=self.gateb.ap[:, m:m + 1]), reads=[bga.b, self.gateb.b], writes=[ga.b])
                S.add("act", "activation", dict(out=gb_.ap[:, 0:nq], in_=bgb.ap[:, 0:nq], func=AF.Sigmoid, bias=self.gateb.ap[:, 8 + m:9 + m]), reads=[bgb.b, self.gateb.b], writes=[gb_.b])
                ta, tb = tg.next(), tg.next()
                S.add("dve", "tensor_tensor", dict(out=ta.ap[:, 0:nq], in0=ga.ap[:, 0:nq], in1=bya.ap[:, 0:nq], op=ALU.mult), reads=[ga.b, bya.b], writes=[ta.b])
                S.add("dve", "tensor_tensor", dict(out=tb.ap[:, 0:nq], in0=gb_.ap[:, 0:nq], in1=byb.ap[:, 0:nq], op=ALU.mult), reads=[gb_.b, byb.b], writes=[tb.b])
                S.add("pool", "tensor_tensor", dict(out=mT.ap[:, m, 0:nq], in0=ta.ap[:, 0:nq], in1=tb.ap[:, 0:nq], op=ALU.add), reads=[ta.b, tb.b], writes=[mT.b])
            if sd.prompt and bi == 0 and len(sd.blocks) > 1:
                x2 = R(hT.ap[:, 0:4, :].rearrange("p a b -> p (a b)").bitcast(F32), "xt2")
                x2.b.w = hT.b.w
                x2.b.rs = list(hT.b.rs)
                out_x.items.append(x2)
                jk = R(hT.ap[:, 4:6, :].rearrange("p a b -> p (a b)"), "ojunk")
                jk.b.w = hT.b.w
                jk.b.rs = list(hT.b.rs)
                ojunk[0] = jk
            self.xring = out_x
            for j, stl in enumerate(blk):
                n = stl.n
                xt = self.load_x(sd, stl, self.dr_x2)
                for hf in range(2):
                    bank = zb.next()
                    for c in range(8):
                        S.add("pe", "matmul", dict(out=bank.ap[0:n, :], lhsT=mT.ap[:, c, j * 128:j * 128 + n], rhs=wout[:, c, hf * 512:(hf + 1) * 512],
                                                                                  start=(c == 0), stop=(c == 7)), reads=[mT.b] + self.wbufs("wout", hf * 512, (hf + 1) * 512), writes=[bank.b])
                    S.add("dve", "tensor_tensor", dict(out=xt.ap[0:n, hf * 512:(hf + 1) * 512], in0=xt.ap[0:n, hf * 512:(hf + 1) * 512], in1=bank.ap[0:n, :], op=ALU.add),
                          reads=[bank.b, xt.b], writes=[xt.b])
                stt = self.rstd(xt.ap[0:n, :], [xt.b], n, D, ojunk[0] if ojunk[0] is not None else self.hbring.next())
                S.add("dve", "scalar_tensor_tensor", dict(out=xt.ap[0:n, :], in0=xt.ap[0:n, :], scalar=stt.ap[0:n, 2:3], in1=gf.ap[0:n, :], op0=ALU.mult, op1=ALU.mult),
                      reads=[xt.b, stt.b, gf.b], writes=[xt.b])
                S.dma("pool", "dma_start", dict(out=sd.o_y[stl.row0:stl.row0 + stl.n, :], in_=xt.ap[0:stl.n, :]), self.dr_o["y"].next(), reads=[xt.b])


def rope_tables(SL, sample):
    ntt = SL // 128 + 1
    pos = np.zeros((128, ntt), np.float64)
    for t in range(ntt - 1):
        pos[:, t] = t * 128 + np.arange(128)
    pos[:, ntt - 1] = 1024 + np.arange(128)
    out = {}
    for name, rot in (("D", 16), ("M", 32)):
        half = rot // 2
        inv = (np.float32(500000.0) ** (-np.arange(half, dtype=np.float32) * np.float32(2.0) / np.float32(rot))).astype(np.float32)
        ang = (pos.astype(np.float32)[:, :, None] * inv[None, None, :]).astype(np.float32)
        out["cos" + name] = np.cos(ang.astype(np.float64)).astype(np.float32).reshape(128, ntt * half)
        out["sin" + name] = np.sin(ang.astype(np.float64)).astype(np.float32).reshape(128, ntt * half)
    return out


_CACHE = {}


def get_nc(NSEQ, SL, SAMPLE, parts=("mla", "da", "fin")):
    key = (NSEQ, SL, SAMPLE, parts)
    if key not in _CACHE:
        b = Builder(NSEQ, SL, SAMPLE, parts)
        nc = b.build()
        _CACHE[key] = (nc, b)
    return _CACHE[key]


def shared_inputs(inp, SL):
    f = lambda a: np.ascontiguousarray(np.asarray(a, dtype=np.float32))
    sh = {
        "w_in": f(inp["w_in"][0]),
        "norm_g": f(inp["norm_g"][0]).reshape(1, D),
        "gate_bT": f(np.asarray(inp["gate_b"][0]).reshape(16, 128).T),
        "da_lambda": f(inp["da_lambda"][0]).reshape(1, 256),
        "hng": f(inp["da_head_norm_g"][0]).reshape(128, 1),
        "gq": f(inp["mla_q_norm_g"][0]).reshape(1, 384),
        "gkv": f(inp["mla_kv_norm_g"][0]).reshape(1, 256),
        "w_uq": f(inp["mla_w_uq"][0]),
        "w_uk": f(inp["mla_w_uk"][0]),
        "w_uv": f(np.asarray(inp["mla_w_uv"][0]).reshape(256, 8, 64).transpose(0, 2, 1).reshape(256, 512)),
        "w_ba": f(inp["w_branch_a"][0]),
        "w_bb": f(inp["w_branch_b"][0]),
        "w_out": f(inp["w_out"][0]),
        "gf": f(inp["final_norm_g"]).reshape(1, D),
        "ident": np.eye(128, dtype=np.float32),
        "shiftm": np.eye(128, k=64, dtype=np.float32),
    }
    sh.update(rope_tables(SL, True))
    return sh


def kernel(**inputs):
    NCORES = 8
    xp = np.asarray(inputs["x_prompt"], dtype=np.float32)
    xs = np.asarray(inputs["x_sample"], dtype=np.float32)
    B, SL, _ = xp.shape
    NSEQ = B // NCORES
    nc, _ = get_nc(NSEQ, SL, True)
    sh = shared_inputs(inputs, SL)
    cdk = np.asarray(inputs["cache_da_k"], dtype=np.float32)[0]
    cdv = np.asarray(inputs["cache_da_v"], dtype=np.float32)[0]
    clat = np.asarray(inputs["cache_mla_latent"], dtype=np.float32)[0]
    ckr = np.asarray(inputs["cache_mla_krope"], dtype=np.float32)[0]
    in_maps = []
    for c in range(NCORES):
        m = dict(sh)
        m["xp"] = np.ascontiguousarray(xp[c * NSEQ:(c + 1) * NSEQ].reshape(NSEQ * SL, D))
        m["xs"] = np.ascontiguousarray(xs[c])
        m["cdk"] = np.ascontiguousarray(cdk[c].reshape(1024, 512))
        m["cdv"] = np.ascontiguousarray(cdv[c].reshape(1024, 512))
        m["clat"] = np.ascontiguousarray(clat[c])
        m["ckr"] = np.ascontiguousarray(ckr[c])
        in_maps.append(m)
    res = run_bass_kernel_spmd(nc, in_maps, core_ids=list(range(NCORES))).results
    cat = lambda k: np.concatenate([np.asarray(r[k]) for r in res], axis=0)
    y_p = cat("yp").reshape(B, SL, D)
    y_s = cat("ys").reshape(NCORES, 64, D)
    k_p = cat("kp").reshape(1, B, SL, 4, 128)
    v_p = cat("vp").reshape(1, B, SL, 4, 128)
    lat_p = cat("latp").reshape(1, B, SL, 256)
    kr_p = cat("krp").reshape(1, B, SL, 32)
    k_s = cat("ks").reshape(1, NCORES, 64, 4, 128)
    v_s = cat("vs").reshape(1, NCORES, 64, 4, 128)
    lat_s = cat("lats").reshape(1, NCORES, 64, 256)
    kr_s = cat("krs").reshape(1, NCORES, 64, 32)
    return tuple(np.ascontiguousarray(a, dtype=np.float32) for a in (y_p, y_s, k_p, v_p, lat_p, kr_p, k_s, v_s, lat_s, kr_s))
```

```python
import math
import numpy as np
from contextlib import ExitStack
import concourse.bass as bass
import concourse.mybir as mybir
from concourse.bass_utils import run_bass_kernel_spmd

F32 = mybir.dt.float32
BF16 = mybir.dt.bfloat16
AF = mybir.ActivationFunctionType
ALU = mybir.AluOpType

D = 1024
SEM_ROT = 12000
SCL = {}
EPS = 1e-6
import os
STOP = int(os.environ.get('KSTOP', '99'))
SUB = int(os.environ.get('KSUB', '99'))
C_DAQ, C_DAK, C_DAV, C_DAZ, C_CQ, C_CKV, C_KR, C_MZ, C_G = 0, 512, 1024, 1536, 2048, 2432, 2688, 2720, 3232
IN_COLS = 5280
LAM_INIT = 0.8 - 0.6 * math.exp(-0.3 * 0)
SC_DA = 64 ** -0.5
SC_MLA = 96 ** -0.5


class Buf:
    __slots__ = ("name", "w", "rs", "excl")

    def __init__(self, name="", excl=False):
        self.name = name
        self.w = None
        self.rs = []
        self.excl = excl


class DmaSem:
    def __init__(self, sem):
        self.sem = sem
        self.count = 0
        self.last_group = None


class DmaGroup:
    def __init__(self, ds):
        self.ds = ds
        self.final = None
        self.last_op = None


class Op:
    __slots__ = ("eng", "name", "kw", "deps", "signal", "sem", "val", "idx", "group", "is_dma", "gidx", "region", "fin")

    def __init__(self, eng, name, kw):
        self.eng = eng
        self.name = name
        self.kw = kw
        self.deps = []
        self.signal = False
        self.sem = None
        self.val = None
        self.idx = None
        self.group = None
        self.is_dma = False


class Sched:
    ENGS = ("pe", "act", "dve", "pool", "sp")

    def __init__(self, nc, stack):
        self.nc = nc
        self.ops = {e: [] for e in self.ENGS}
        self.dma_sems = []
        self._stack = stack
        self.bar_deps = []
        self.bar_seen = {e: True for e in self.ENGS}
        self.all_ops = []
        self.region = 0

    def new_sem(self, name):
        return self._stack.enter_context(self.nc.semaphore(name))

    def dma_sem(self, name):
        ds = DmaSem(self.new_sem(name))
        self.dma_sems.append(ds)
        return ds

    def dma_ring(self, name, n):
        return DmaRing([self.dma_sem(f"{name}{i}") for i in range(n)])

    def barrier(self):
        self.region += 1

    def _collect(self, op, reads, writes, extra):
        deps = []
        for b in reads:
            if b.w is not None:
                deps.append(b.w)
            if b.excl:
                for r in b.rs:
                    if r.eng != op.eng:
                        deps.append(r)
        for b in writes:
            if b.w is not None:
                deps.append(b.w)
            deps.extend(b.rs)
        deps.extend(extra)
        seen = set()
        out = []
        for d in deps:
            if d is op or id(d) in seen:
                continue
            seen.add(id(d))
            out.append(d)
        op.deps = out
        op.gidx = len(self.all_ops)
        op.region = self.region
        self.all_ops.append(op)
        for b in writes:
            b.w = op
            b.rs = []
        for b in reads:
            b.rs.append(op)

    def add(self, eng, name, kw, reads=(), writes=(), extra=()):
        op = Op(eng, name, kw)
        self._collect(op, reads, writes, extra)
        self.ops[eng].append(op)
        return op

    def dma(self, eng, name, kw, ds, reads=(), writes=(), extra=()):
        op = Op(eng, name, kw)
        op.is_dma = True
        extra = list(extra)
        g = DmaGroup(ds)
        if ds.last_group is not None:
            extra.append(ds.last_group.last_op)
        ds.last_group = g
        ds.count += 1
        g.final = 16 * ds.count
        g.last_op = op
        op.group = g
        op.sem = ds.sem
        op.signal = True
        self._collect(op, reads, writes, extra)
        self.ops[eng].append(op)
        return op

    @staticmethod
    def _fsize(ap):
        n = 1
        for x in ap.shape[1:]:
            n *= x
        return n

    def _dur(self, op):
        return self._dur0(op) * SCL.get("dma" if op.is_dma else op.eng, 1.0)

    def _dur0(self, op):
        kw = op.kw
        if op.is_dma:
            o = kw["out"]
            nbytes = self._fsize(o) * o.shape[0] * (4 if o.dtype == F32 else 2)
            return 2200.0 + nbytes / 120.0
        if op.eng == "pe":
            if op.name == "transpose":
                return 70.0
            return 8.0 + 0.41 * self._fsize(kw["rhs"])
        a = kw.get("in_", kw.get("in0", kw.get("out", kw.get("ap"))))
        f = self._fsize(a)
        if op.eng == "act":
            if os.environ.get("KEXP2") and f == 512 and kw.get("func") == AF.Exp:
                return (190.0 + 1024 / 1.2) / 2
            return 190.0 + f / 1.2 + (100.0 if "accum_out" in kw else 0.0)
        if op.eng == "dve":
            if op.name == "reciprocal":
                return 80.0 + 6.6 * f
            return 100.0 + f / 0.85
        return 200.0 + f / 0.48

    def schedule(self, dry=False, beta=None):
        import heapq
        if beta is None:
            beta = float(os.environ.get("KBETA", "0.5"))
        regions = {}
        for op in self.all_ops:
            regions.setdefault(op.region, []).append(op)
        new_ops = {e: [] for e in self.ENGS}
        t0 = 0.0
        tail = []
        for r in sorted(regions):
            ops = regions[r]
            reg_first = {}
            reg_last = {}
            dma_last = {}
            inreg = set(id(o) for o in ops)
            succ = {}
            indeg = {}
            ready = {}
            for o in ops:
                cnt = 0
                for d in o.deps:
                    if id(d) in inreg:
                        cnt += 1
                        succ.setdefault(id(d), []).append(o)
                indeg[id(o)] = cnt
                ready[id(o)] = t0
            bl = {}
            if beta:
                for o in reversed(ops):
                    m = 0.0
                    for sc in succ.get(id(o), ()):
                        v = bl[id(sc)]
                        if v > m:
                            m = v
                    bl[id(o)] = m + self._dur(o)
            heap = [(t0 - beta * bl.get(id(o), 0.0), o.gidx, o) for o in ops if indeg[id(o)] == 0]
            heapq.heapify(heap)
            free = {e: t0 for e in self.ENGS}
            tmax = t0
            while heap:
                _, _, o = heapq.heappop(heap)
                rt = ready[id(o)]
                st = max(rt, free[o.eng])
                if o.is_dma:
                    issue = 1000.0 if o.eng == "pool" else 80.0
                    free[o.eng] = st + issue
                    fin = st + issue + self._dur(o)
                else:
                    fin = st + self._dur(o)
                    free[o.eng] = fin
                o.fin = fin
                tmax = max(tmax, fin)
                new_ops[o.eng].append(o)
                if o.eng not in reg_first:
                    reg_first[o.eng] = o
                if o.is_dma:
                    cur = dma_last.get(id(o.sem))
                    if cur is None or o.group.final > cur.group.final:
                        dma_last[id(o.sem)] = o
                else:
                    reg_last[o.eng] = o
                for sc in succ.get(id(o), ()):
                    if o.eng == "pe" and sc.eng == "pe" and not sc.is_dma and not o.is_dma:
                        lat = 0.0
                    elif o.eng == sc.eng and not o.is_dma:
                        lat = 60.0
                    else:
                        lat = 150.0
                    lat *= SCL.get("lat", 1.0)
                    ready[id(sc)] = max(ready[id(sc)], fin + lat)
                    indeg[id(sc)] -= 1
                    if indeg[id(sc)] == 0:
                        heapq.heappush(heap, (ready[id(sc)] - beta * bl.get(id(sc), 0.0), sc.gidx, sc))
            if not dry:
                for e, o in reg_first.items():
                    have = set(id(d) for d in o.deps)
                    o.deps = list(o.deps) + [d for d in tail if id(d) not in have and d is not o]
            tail = list(reg_last.values()) + list(dma_last.values())
            if os.environ.get("KDEBUG"):
                busy = {}
                for o in ops:
                    if not o.is_dma:
                        busy[o.eng] = busy.get(o.eng, 0.0) + self._dur(o)
                print("region", r, "ops", len(ops), "dur us", round((tmax - t0) / 1000, 1), {k: round(v / 1000) for k, v in busy.items()})
            t0 = tmax
        assert sum(len(v) for v in new_ops.values()) == len(self.all_ops)
        if dry:
            return t0
        self.ops = new_ops
        self.est_ns = t0

    def finalize(self):
        if os.environ.get("KSCHED", "1") == "1":
            self.schedule()
        for e in self.ENGS:
            for i, op in enumerate(self.ops[e]):
                op.idx = i
        for e in self.ENGS:
            for op in self.ops[e]:
                best = {}
                out = []
                seen_groups = set()
                for d in op.deps:
                    if d.is_dma:
                        if id(d.group) not in seen_groups:
                            seen_groups.add(id(d.group))
                            out.append(d)
                    else:
                        if d.eng == "pe" and op.eng == "pe" and not op.is_dma:
                            continue
                        cur = best.get(d.eng)
                        if cur is None or d.idx > cur.idx:
                            best[d.eng] = d
                out.extend(best.values())
                for d in out:
                    d.signal = True
                op.deps = out
        for e in ("pe", "act", "dve", "pool"):
            nsem = 0
            cnt = 0
            cur = None
            for op in self.ops[e]:
                if op.is_dma or not op.signal:
                    continue
                if cur is None or cnt >= SEM_ROT:
                    cur = self.new_sem(f"s_{e}{nsem}")
                    nsem += 1
                    cnt = 0
                cnt += 1
                op.sem = cur
                op.val = cnt
        for e in self.ENGS:
            for op in self.ops[e]:
                if op.is_dma:
                    op.val = op.group.final

    def emit(self, block):
        self.finalize()
        stats = {}

        def run(e):
            def body(eng):
                seen = {}
                nw = 0
                for op in self.ops[e]:
                    for d in op.deps:
                        k = id(d.sem)
                        if seen.get(k, 0) >= d.val:
                            continue
                        seen[k] = d.val
                        eng.wait_ge(d.sem, d.val)
                        nw += 1
                    ins = getattr(eng, op.name)(**op.kw)
                    if op.signal:
                        ins.then_inc(op.sem, 16 if op.is_dma else 1)
                if e == "sp":
                    for ds in self.dma_sems:
                        if ds.count:
                            eng.wait_ge(ds.sem, 16 * ds.count)
                stats[e] = (len(self.ops[e]), nw)
            return body

        block.tensor(run("pe"))
        block.scalar(run("act"))
        block.vector(run("dve"))
        block.gpsimd(run("pool"))
        block.sync(run("sp"))
        return stats


class DmaRing:
    def __init__(self, sems):
        self.sems = sems
        self.i = 0

    def next(self):
        s = self.sems[self.i % len(self.sems)]
        self.i += 1
        return s


class R:
    __slots__ = ("ap", "b")

    def __init__(self, ap, name=""):
        self.ap = ap
        self.b = Buf(name)


class Ring:
    def __init__(self, items):
        self.items = items
        self.i = 0

    def next(self):
        r = self.items[self.i % len(self.items)]
        self.i += 1
        return r


class Arena:
    def __init__(self, regions):
        self.regions = regions
        self.reset()

    def reset(self):
        self.off = [0 for _ in self.regions]

    def alloc(self, shape, dtype, name=""):
        n = 1
        for s in shape:
            n *= s
        esz = 4 if dtype == F32 else 2
        nel = n * esz // 2
        for i, reg in enumerate(self.regions):
            o = (self.off[i] + 1) // 2 * 2
            if o + nel <= reg.shape[1]:
                self.off[i] = o + nel
                ap = reg[:, o:o + nel]
                if dtype != BF16:
                    ap = ap.bitcast(dtype)
                if len(shape) == 2:
                    ap = ap.rearrange("p (a b) -> p a b", b=shape[1])
                elif len(shape) == 3:
                    ap = ap.rearrange("p (a b c) -> p a b c", b=shape[1], c=shape[2])
                return R(ap, name)
        raise RuntimeError(f"arena overflow allocating {name} {shape}; offs={self.off}")


def run_skewed(stages, items):
    n, ns = len(items), len(stages)
    for step in range(n + ns - 1):
        for si in range(ns):
            j = step - si
            if 0 <= j < n:
                stages[si](items[j])


class SubTile:
    def __init__(self, row0, n, kt, tt, c0):
        self.row0, self.n, self.kt, self.tt, self.c0 = row0, n, kt, tt, c0


class SeqDesc:
    pass


class Builder:
    def __init__(self, NSEQ, SL, SAMPLE, parts=("mla", "da", "fin")):
        self.NSEQ, self.SL, self.SAMPLE = NSEQ, SL, SAMPLE
        self.parts = parts
        self.TK = max(SL, 1152 if SAMPLE else 0)
        self.W = {}
        self.NTT = SL // 128 + 1

    def dram(self, name, shape, kind="ExternalInput"):
        return self.nc.dram_tensor(name, list(shape), F32, kind=kind).ap()

    def sb(self, name, shape, dtype):
        return self.st.enter_context(self.nc.sbuf_tensor("sb_" + name, list(shape), dtype))

    def build(self):
        nc = self.nc = bass.Bass("TRN2", target_bir_lowering=False)
        NSEQ, SL, TK = self.NSEQ, self.SL, self.TK
        NP = NSEQ * SL
        dr = self.dram
        self.xp = dr("xp", [NP, D])
        self.w_in = dr("w_in", [D, IN_COLS])
        self.i_normg = dr("norm_g", [1, D])
        self.i_gateb = dr("gate_bT", [128, 16])
        self.i_lam = dr("da_lambda", [1, 256])
        self.i_hng = dr("hng", [128, 1])
        self.i_gq = dr("gq", [1, 384])
        self.i_gkv = dr("gkv", [1, 256])
        self.i_wuq = dr("w_uq", [384, 768])
        self.i_wuk = dr("w_uk", [256, 512])
        self.i_wuv = dr("w_uv", [256, 512])
        self.i_wba = dr("w_ba", [512, D])
        self.i_wbb = dr("w_bb", [512, D])
        self.i_wout = dr("w_out", [D, D])
        self.i_gf = dr("gf", [1, D])
        self.i_ident = dr("ident", [128, 128])
        self.i_shift = dr("shiftm", [128, 128])
        NTT = self.NTT
        self.i_cosD = dr("cosD", [128, NTT * 8])
        self.i_sinD = dr("sinD", [128, NTT * 8])
        self.i_cosM = dr("cosM", [128, NTT * 16])
        self.i_sinM = dr("sinM", [128, NTT * 16])
        o = lambda n, s: dr(n, s, kind="ExternalOutput")
        self.o_y = o("yp", [NP, D])
        self.o_k = o("kp", [NP, 512])
        self.o_v = o("vp", [NP, 512])
        self.o_lat = o("latp", [NP, 256])
        self.o_kr = o("krp", [NP, 32])
        if self.SAMPLE:
            self.xs = dr("xs", [64, D])
            self.cdk = dr("cdk", [1024, 512])
            self.cdv = dr("cdv", [1024, 512])
            self.clat = dr("clat", [1024, 256])
            self.ckr = dr("ckr", [1024, 32])
            self.o_ys = o("ys", [64, D])
            self.o_ks = o("ks", [64, 512])
            self.o_vs = o("vs", [64, 512])
            self.o_lats = o("lats", [64, 256])
            self.o_krs = o("krs", [64, 32])

        with ExitStack() as st:
            self.st = st
            S = self.S = Sched(nc, st)
            sb = self.sb
            self.U_N = max(8 * TK + (TK // 128 + 1) * 520, 8 * TK + 12288, 40960)
            self.U = sb("U", [128, self.U_N], BF16)
            self.SLX = SL
            SLX = self.SLX
            self.OB = sb("OB", [128, 4, SLX], BF16)
            self.OA_N = max(4 * SLX, 9728)
            self.OA = sb("OA", [128, self.OA_N], BF16)
            self.OBb = [Buf(f"OB{i}") for i in range(SL // 512 + 1)]
            self.OAb = [Buf(f"OA{i}") for i in range(SL // 512 + 1)]
            self.Ub = Buf("U")
            self.ident = R(sb("ident", [128, 128], BF16))
            self.ones = R(sb("ones", [128, 128], BF16))
            self.shiftm = R(sb("shiftm", [128, 128], BF16))
            self.sel64 = R(sb("sel64", [128, 64], BF16))
            self.cosD = R(sb("cosD", [128, NTT, 8], F32))
            self.sinD = R(sb("sinD", [128, NTT, 8], F32))
            self.cosM = R(sb("cosM", [128, NTT, 16], F32))
            self.sinM = R(sb("sinM", [128, NTT, 16], F32))
            self.g_in = R(sb("g_in", [128, D], F32))
            self.gateb = R(sb("gateb", [128, 16], F32))
            self.vec = R(sb("vec", [128, 16], F32))
            self.lamt = R(sb("lamt", [128, 256], F32))
            self.stats = Ring([R(sb(f"stat{i}", [128, 4], F32)) for i in range(6)])
            self.cstage = R(sb("cstage", [128, 128], F32))
            rem = nc.sbuf_bytes_remaining
            tn = (rem - 64) // 2 // 2 * 2
            self.Tt = sb("T", [128, tn], BF16)
            self.PS = [R(st.enter_context(nc.psum_tensor(f"ps{i}", [128, 512], F32)), f"ps{i}") for i in range(8)]
            for r_ in self.PS:
                r_.b.excl = True
            self.ds_const = S.dma_sem("dconst")
            self.dr_w = S.dma_ring("dw", 4)
            self.dr_x = S.dma_ring("dx", 2)
            self.dr_x2 = S.dma_ring("dxo", 2)
            self.dr_o = {k: S.dma_ring("do" + k, 2) for k in ("k", "v", "lat", "kr", "y")}
            self.ds_cache = S.dma_sem("dcache")

            self.consts()
            seqs = []
            for s in range(NSEQ):
                sd = SeqDesc()
                sd.x = self.xp
                sd.prompt = True
                sd.ncache = 0
                sd.blocks = []
                for b in range(SL // 512):
                    sd.blocks.append([SubTile(s * SL + (4 * b + j) * 128, 128, 4 * b + j, 4 * b + j, (4 * b + j) * 128) for j in range(4)])
                sd.o_y, sd.o_k, sd.o_v, sd.o_lat, sd.o_kr = self.o_y, self.o_k, self.o_v, self.o_lat, self.o_kr
                sd.obi = list(range(SL // 512))
                sd.OB = self.OB
                sd.OA = self.OA[:, 0:4 * SL].rearrange("p (h t) -> p h t", t=SL)
                seqs.append(sd)
            if self.SAMPLE:
                sd = SeqDesc()
                sd.x = self.xs
                sd.prompt = False
                sd.ncache = 8
                sd.blocks = [[SubTile(0, 64, 8, NTT - 1, 0)]]
                sd.obi = [SL // 512]
                lb = self.lamt.ap[:, :].bitcast(BF16)
                sd.OB = lb[:, 0:256].rearrange("p (h t) -> p h t", t=64)
                sd.OA = lb[:, 256:512].rearrange("p (h t) -> p h t", t=64)
                sd.o_y, sd.o_k, sd.o_v, sd.o_lat, sd.o_kr = self.o_ys, self.o_ks, self.o_vs, self.o_lats, self.o_krs
                seqs.append(sd)
            prompts = [q for q in seqs if q.prompt]
            smp = [q for q in seqs if not q.prompt]
            groups = [[q] for q in prompts[:-1]] + [prompts[-1:] + smp]
            for grp in groups:
                for pname, fn in (("mla", self.mla_pass), ("da", self.da_pass), ("fin", self.fin_pass)):
                    if pname in self.parts:
                        for gi, sd in enumerate(grp):
                            fn(sd, reload=(gi == 0))
            with nc.allow_low_precision(reason="bf16 matmul operands by design"), nc.Block() as block:
                self.stats_out = S.emit(block)
        return nc

    def consts(self):
        S = self.S
        ds = self.ds_const
        cs = self.cstage
        for src, dst in ((self.i_ident, self.ident), (self.i_shift, self.shiftm)):
            S.dma("sp", "dma_start", dict(out=cs.ap[:], in_=src), ds, writes=[cs.b])
            S.add("dve", "tensor_copy", dict(out=dst.ap[:], in_=cs.ap[:]), reads=[cs.b], writes=[dst.b])
        S.add("pool", "memset", dict(ap=self.ones.ap[:], constant=1.0), writes=[self.ones.b])
        S.add("pool", "memset", dict(ap=self.sel64.ap[:], constant=0.0), writes=[self.sel64.b])
        S.add("pool", "memset", dict(ap=self.sel64.ap[64:65, :], constant=1.0), writes=[self.sel64.b])
        NTT = self.NTT
        for src, dst, k in ((self.i_cosD, self.cosD, 8), (self.i_sinD, self.sinD, 8), (self.i_cosM, self.cosM, 16), (self.i_sinM, self.sinM, 16)):
            S.dma("sp", "dma_start", dict(out=dst.ap[:], in_=src.rearrange("p (t k) -> p t k", k=k)), ds, writes=[dst.b])
        S.dma("sp", "dma_start", dict(out=self.g_in.ap[:], in_=self.i_normg.partition_broadcast(128)), ds, writes=[self.g_in.b])
        S.dma("sp", "dma_start", dict(out=self.gateb.ap[:], in_=self.i_gateb), ds, writes=[self.gateb.b])
        S.dma("sp", "dma_start", dict(out=self.lamt.ap[:], in_=self.i_lam.partition_broadcast(128)), ds, writes=[self.lamt.b])
        v = self.vec
        S.dma("sp", "dma_start", dict(out=v.ap[:, 1:2], in_=self.i_hng), ds, writes=[v.b])
        lt = self.lamt
        S.add("dve", "tensor_tensor", dict(out=lt.ap[:, 0:64], in0=lt.ap[:, 0:64], in1=lt.ap[:, 64:128], op=ALU.mult), reads=[lt.b], writes=[lt.b])
        S.add("dve", "tensor_tensor", dict(out=lt.ap[:, 128:192], in0=lt.ap[:, 128:192], in1=lt.ap[:, 192:256], op=ALU.mult), reads=[lt.b], writes=[lt.b])
        S.add("dve", "tensor_reduce", dict(out=v.ap[:, 2:3], in_=lt.ap[:, 0:64], op=ALU.add, axis=mybir.AxisListType.X), reads=[lt.b], writes=[v.b])
        S.add("dve", "tensor_reduce", dict(out=v.ap[:, 3:4], in_=lt.ap[:, 128:192], op=ALU.add, axis=mybir.AxisListType.X), reads=[lt.b], writes=[v.b])
        S.add("act", "activation", dict(out=v.ap[:, 4:6], in_=v.ap[:, 2:4], func=AF.Exp), reads=[v.b], writes=[v.b])
        S.add("dve", "scalar_tensor_tensor", dict(out=v.ap[:, 0:1], in0=v.ap[:, 5:6], scalar=-LAM_INIT, in1=v.ap[:, 4:5], op0=ALU.add, op1=ALU.subtract), reads=[v.b], writes=[v.b])
        S.add("dve", "tensor_scalar", dict(out=v.ap[:, 1:2], in0=v.ap[:, 1:2], scalar1=(1.0 - LAM_INIT), scalar2=None, op0=ALU.mult), reads=[v.b], writes=[v.b])

    def new_arena(self, extra_regions=()):
        self.S.barrier()
        self.ar = Arena([self.Tt[:, :]] + list(extra_regions))
        ar = self.ar
        self.xring = Ring([ar.alloc([D], F32, f"xt{i}") for i in range(2)])
        self.hbring = Ring([ar.alloc([D], BF16, f"hb{i}") for i in range(2)])

    def rstd(self, src_ap, src_bufs, n, F, junk):
        S = self.S
        stt = self.stats.next()
        S.add("act", "activation", dict(out=junk.ap[0:n, 0:F], in_=src_ap, func=AF.Square, accum_out=stt.ap[0:n, 0:1]),
              reads=src_bufs, writes=[junk.b, stt.b])
        S.add("act", "activation", dict(out=stt.ap[0:n, 1:2], in_=stt.ap[0:n, 0:1], func=AF.Ln, scale=1.0 / F, bias=EPS),
              reads=[stt.b], writes=[stt.b])
        S.add("act", "activation", dict(out=stt.ap[0:n, 2:3], in_=stt.ap[0:n, 1:2], func=AF.Exp, scale=-0.5),
              reads=[stt.b], writes=[stt.b])
        return stt

    def load_x(self, sd, stl, ring=None):
        S = self.S
        xt = self.xring.next()
        n = stl.n
        S.dma("sp", "dma_start", dict(out=xt.ap[0:n, :], in_=sd.x[stl.row0:stl.row0 + n, :]), (ring or self.dr_x).next(), writes=[xt.b])
        return xt

    def prologue(self, sd, stl, hT_ap, hT_buf, copy_eng="act", outs=None):
        S = self.S
        n = stl.n
        xt = self.load_x(sd, stl)
        hb = self.hbring.next()
        stt = self.rstd(xt.ap[0:n, :], [xt.b], n, D, hb)
        S.add("dve", "scalar_tensor_tensor", dict(out=hb.ap[0:n, :], in0=xt.ap[0:n, :], scalar=stt.ap[0:n, 2:3], in1=self.g_in.ap[0:n, :],
                                                       op0=ALU.mult, op1=ALU.mult), reads=[xt.b, stt.b, self.g_in.b], writes=[hb.b])
        pb = self.PS[0]
        pbf = pb.ap[:, :].bitcast(BF16).rearrange("p (c t) -> p c t", t=128)
        for c in range(8):
            S.add("pe", "transpose", dict(out=pbf[:, c, 0:n], in_=hb.ap[0:n, c * 128:(c + 1) * 128], identity=self.ident.ap[0:n, 0:n]),
                  reads=[hb.b, self.ident.b], writes=[pb.b])
        if outs is None:
            outs = [(hT_ap, 0, 8, [hT_buf])]
        for ap_, lo, hi, bufs in outs:
            if copy_eng == "act":
                S.add("act", "copy", dict(out=ap_, in_=pbf[:, lo:hi, 0:n]), reads=[pb.b], writes=bufs)
            else:
                S.add("dve", "tensor_copy", dict(out=ap_, in_=pbf[:, lo:hi, 0:n]), reads=[pb.b], writes=bufs)
        return xt

    def proj_tok(self, hT, n, w_ap, ncols, bank, wb):
        S = self.S
        for c in range(8):
            S.add("pe", "matmul", dict(out=bank.ap[0:n, 0:ncols], lhsT=hT.ap[:, c, 0:n], rhs=w_ap[:, c, :], start=(c == 0), stop=(c == 7)),
                  reads=[hT.b] + wb, writes=[bank.b])

    def load_w(self, dst_ap, src_ap, K, c0, c1, name=None):
        S = self.S
        pieces = []
        a = c0
        while a < c1:
            b = min(a + 1024, c1)
            pb_ = Buf(f"w_{name}_{a}")
            S.dma("pool", "dma_start", dict(out=dst_ap[:, :, a - c0:b - c0], in_=src_ap[:, a:b].rearrange("(c p) n -> p c n", p=128)),
                  self.dr_w.next(), writes=[pb_])
            pieces.append((a - c0, b - c0, pb_))
            a = b
        self.W[name] = pieces

    def wbufs(self, name, a=None, b=None):
        out = []
        for lo, hi, pb_ in self.W[name]:
            if a is None or (lo < b and a < hi):
                out.append(pb_)
        return out

    def rope(self, eng, src4, dst4, n, cos_ap, sin_ap, G, half, tmp, src_bufs, dst_bufs):
        S = self.S
        t4 = tmp.ap[0:n, 0:G * 2 * half].rearrange("p (g k) -> p g k", k=2 * half)
        cb = cos_ap.unsqueeze(1).broadcast_to([n, G, half])
        sbb = sin_ap.unsqueeze(1).broadcast_to([n, G, half])
        rb = src_bufs + [self.cosM.b, self.sinM.b, self.cosD.b, self.sinD.b]
        S.add(eng, "scalar_tensor_tensor", dict(out=t4[:, :, 0:half], in0=src4[:, :, half:2 * half], scalar=-1.0, in1=sbb, op0=ALU.mult, op1=ALU.mult),
              reads=rb, writes=[tmp.b])
        S.add(eng, "tensor_tensor", dict(out=t4[:, :, half:2 * half], in0=src4[:, :, 0:half], in1=sbb, op=ALU.mult), reads=rb, writes=[tmp.b])
        S.add(eng, "tensor_tensor", dict(out=dst4[:, :, 0:half], in0=src4[:, :, 0:half], in1=cb, op=ALU.mult), reads=rb, writes=dst_bufs)
        S.add(eng, "tensor_tensor", dict(out=dst4[:, :, half:2 * half], in0=src4[:, :, half:2 * half], in1=cb, op=ALU.mult), reads=rb, writes=dst_bufs)
        S.add(eng, "tensor_tensor", dict(out=dst4, in0=dst4, in1=t4, op=ALU.add), reads=[tmp.b] + dst_bufs, writes=dst_bufs)

    def key_tiles(self, sd, blk, bi):
        if sd.prompt:
            out = [(kt, 128, 0, False) for kt in range(4 * bi)]
            for j in range(4):
                out.append((4 * bi + j, 128, 128 * j, True))
            return out
        return [(kt, 128, 0, False) for kt in range(sd.ncache)] + [(sd.ncache, 64, 0, False)]

    def mla_pass(self, sd, reload=True):
        S = self.S
        TK = self.TK
        NKT = TK // 128
        w1 = self.OA[:, 0:9728]
        regs = [self.OA[:, 9728:self.OA_N]] if self.OA_N > 9728 + 1024 else []
        if not sd.prompt and self.SL >= 2048:
            regs.append(self.U[:, 8 * TK + 10 * 520:8 * TK + NKT * 520])
        self.new_arena(regs)
        ar = self.ar
        w1in = w1[:, 0:5376].rearrange("p (c n) -> p c n", n=672)
        wuq = w1[:, 5376:7680].rearrange("p (c n) -> p c n", n=768)
        wuk = w1[:, 7680:8704].rearrange("p (c n) -> p c n", n=512)
        wuv = w1[:, 8704:9728].rearrange("p (c n) -> p c n", n=512)
        if reload:
            self.load_w(w1in, self.w_in, 8, C_CQ, C_MZ, "w1in")
            self.load_w(wuq, self.i_wuq, 3, 0, 768, "wuq")
            self.load_w(wuk, self.i_wuk, 2, 0, 512, "wuk")
            self.load_w(wuv, self.i_wuv, 2, 0, 512, "wuv")
        KT = self.U[:, 0:8 * TK].rearrange("p (h t) -> p h t", t=TK)
        VM = self.U[:, 8 * TK:8 * TK + NKT * 520].rearrange("p (k c) -> p k c", c=520)
        VMf = self.U[:, 8 * TK:8 * TK + (NKT + 1) * 520]
        ktb = [Buf(f"ktm{i}") for i in range(NKT)]
        vmb = [Buf(f"vm{i}") for i in range(NKT)]
        gq = ar.alloc([384], F32, "gq")
        gkv = ar.alloc([256], F32, "gkv")
        S.dma("sp", "dma_start", dict(out=gq.ap[:], in_=self.i_gq.partition_broadcast(128)), self.ds_const, writes=[gq.b])
        S.dma("sp", "dma_start", dict(out=gkv.ap[:], in_=self.i_gkv.partition_broadcast(128)), self.ds_const, writes=[gkv.b])
        nkc = (sd.ncache + 1) * 128 if not sd.prompt else TK
        nvz = (nkc // 128 + 1) * 520
        ng = nkc // 512
        for g in range(ng):
            t_lo, t_hi = 4 * g, (4 * g + 4 if g < ng - 1 else nkc // 128 + 1)
            bufs_g = vmb[t_lo:min(t_hi, NKT)]
            S.add("dve" if g % 2 == 0 else "pool", "memset", dict(ap=VMf[:, t_lo * 520:t_hi * 520], constant=0.0), writes=bufs_g)
            S.add("pool", "memset", dict(ap=VM[:, t_lo:min(t_hi, nkc // 128), 512:520], constant=1.0), writes=bufs_g)
        for g in range(ng):
            c_hi = 512 * (g + 1) if g < ng - 1 else nkc
            S.add("pool" if g % 2 == 0 else "dve", "memset", dict(ap=KT[96:128, :, 512 * g:c_hi], constant=0.0), writes=ktb[4 * g:c_hi // 128])
        if STOP <= 1:
            return
        hTr = Ring([ar.alloc([8, 128], BF16, f"hTs{i}") for i in range(2)])
        pTr = Ring([ar.alloc([512], BF16, f"pT{i}") for i in range(3)])
        QT = ar.alloc([8, 512], BF16, "QT")
        cfr = Ring([ar.alloc([256], F32, f"cf{i}") for i in range(1)])
        krfr = Ring([ar.alloc([32], F32, f"krf{i}") for i in range(2)])
        cbf = ar.alloc([256], BF16, "cbf")
        cqbf = ar.alloc([384], BF16, "cqbf")
        ccT = ar.alloc([5, 128], BF16, "ccT")
        kfull = ar.alloc([8, 96], BF16, "kfull")
        qtok = ar.alloc([8, 96], BF16, "qtok")
        rtmp = ar.alloc([8 * 32], F32, "rtmp")
        u_sb = ar.alloc([512], F32, "u_sb")
        on_sb = ar.alloc([512], BF16, "on_sb")
        rr = ar.alloc([512], BF16, "rr")
        S.add("pool", "memset", dict(ap=rr.ap[:, :], constant=0.0), writes=[rr.b])
        rr32 = R(u_sb.ap, "rr32")
        junk = ar.alloc([384], BF16, "junk")
        S.add("pool", "memset", dict(ap=QT.ap[96:128, :, :], constant=0.0), writes=[QT.b])
        S.add("pool", "memset", dict(ap=on_sb.ap[:, :], constant=0.0), writes=[on_sb.b])
        PS = self.PS

        ccTr = Ring([ccT, ar.alloc([5, 128], BF16, "ccT1")])
        kfr_ = Ring([kfull, ar.alloc([8, 96], BF16, "kfull1")])
        rtmpC = ar.alloc([4 * 32], F32, "rtmpC")
        if os.environ.get("KDEBUG"):
            print("MLA arena", ar.off, [r.shape for r in ar.regions])

        def ctrans(cc, cb_ap, cb_b, n):
            pb = PS[4]
            pbf = pb.ap[:, :].bitcast(BF16).rearrange("p (c t) -> p c t", t=128)
            for c in range(2):
                S.add("pe", "transpose", dict(out=pbf[:, c, 0:n], in_=cb_ap[0:n, c * 128:(c + 1) * 128], identity=self.ident.ap[0:n, 0:n]),
                      reads=[cb_b, self.ident.b], writes=[pb.b])
            S.add("dve", "tensor_copy", dict(out=cc.ap[:, 0:2, 0:n], in_=pbf[:, 0:2, 0:n]), reads=[pb.b], writes=[cc.b])

        def kside2(cc, kf_, kt, n):
            for c in range(2):
                S.add("pe", "matmul", dict(out=PS[5].ap[0:n, :], lhsT=cc.ap[:, c, 0:n], rhs=wuk[:, c, :], start=(c == 0), stop=(c == 1)),
                      reads=[cc.b] + self.wbufs("wuk"), writes=[PS[5].b])
            for c in range(2):
                S.add("pe", "matmul", dict(out=PS[6].ap[0:n, :], lhsT=cc.ap[:, c, 0:n], rhs=wuv[:, c, :], start=(c == 0), stop=(c == 1)),
                      reads=[cc.b] + self.wbufs("wuv"), writes=[PS[6].b])
            S.add("dve", "tensor_copy", dict(out=kf_.ap[0:n, :, 0:64], in_=PS[5].ap[0:n, :].rearrange("p (h d) -> p h d", d=64)),
                  reads=[PS[5].b], writes=[kf_.b])
            S.add("dve", "tensor_copy", dict(out=VM[0:n, kt, 0:512], in_=PS[6].ap[0:n, :]), reads=[PS[6].b], writes=[vmb[kt]])
            pb2 = PS[7]
            pbf2 = pb2.ap[:, :].bitcast(BF16).rearrange("p (c t) -> p c t", t=128)
            for h in range(8):
                S.add("pe", "transpose", dict(out=pbf2[0:96, h, 0:n], in_=kf_.ap[0:n, h, :], identity=self.ident.ap[0:n, 0:n]),
                      reads=[kf_.b, self.ident.b], writes=[pb2.b])
            S.add("dve", "tensor_copy", dict(out=KT[0:96, :, kt * 128:kt * 128 + n], in_=pbf2[0:96, :, 0:n]), reads=[pb2.b], writes=[ktb[kt]])

        if sd.ncache:
            cst = ar.alloc([8, 256], BF16, "cst")
            kst = ar.alloc([8, 32], BF16, "kst")
            S.dma("pool", "dma_start", dict(out=cst.ap[:], in_=self.clat.rearrange("(t p) c -> p t c", p=128)), self.ds_cache, writes=[cst.b])
            S.dma("pool", "dma_start", dict(out=kst.ap[:], in_=self.ckr.rearrange("(t p) c -> p t c", p=128)), self.ds_cache, writes=[kst.b])

            def cB(t):
                cc, kf_ = ccTr.next(), kfr_.next()
                S.add("pool", "tensor_copy", dict(out=kf_.ap[:, :, 64:96], in_=kst.ap[:, t, :].unsqueeze(1).broadcast_to([128, 8, 32])),
                      reads=[kst.b], writes=[kf_.b])
                ctrans(cc, cst.ap[:, t, :], cst.b, 128)
                cctx[t] = (cc, kf_)

            def cC(t):
                cc, kf_ = cctx[t]
                kside2(cc, kf_, t, 128)
            cctx = {}
            run_skewed([cB, cC], list(range(sd.ncache)))

        def stA(c):
            c["hT"] = hTr.next()
            self.prologue(sd, c["stl"], c["hT"].ap[:, :, 0:c["stl"].n], c["hT"].b, copy_eng="dve")

        def stB(c):
            stl, hT = c["stl"], c["hT"]
            n = stl.n
            self.proj_tok(hT, n, w1in[:, :, 0:384], 384, PS[1], self.wbufs("w1in", 0, 384))
            self.proj_tok(hT, n, w1in[:, :, 384:672], 288, PS[2], self.wbufs("w1in", 384, 672))
            cc, kf_ = ccTr.next(), kfr_.next()
            c["cc"], c["kf"] = cc, kf_
            stt = self.rstd(PS[2].ap[0:n, 0:256], [PS[2].b], n, 256, junk)
            cf = cfr.next()
            S.add("dve", "scalar_tensor_tensor", dict(out=cf.ap[0:n, :], in0=PS[2].ap[0:n, 0:256], scalar=stt.ap[0:n, 2:3], in1=gkv.ap[0:n, :],
                                                      op0=ALU.mult, op1=ALU.mult), reads=[PS[2].b, stt.b, gkv.b], writes=[cf.b])
            S.dma("pool", "dma_start", dict(out=sd.o_lat[stl.row0:stl.row0 + n, :], in_=cf.ap[0:n, :]), self.dr_o["lat"].next(), reads=[cf.b])
            S.add("pool", "tensor_copy", dict(out=cbf.ap[0:n, :], in_=cf.ap[0:n, :]), reads=[cf.b], writes=[cbf.b])
            krf = krfr.next()
            src = PS[2].ap[0:n, 256:288].unsqueeze(1)
            dst = krf.ap[0:n, :].unsqueeze(1)
            self.rope("dve", src, dst, n, self.cosM.ap[0:n, stl.tt, :], self.sinM.ap[0:n, stl.tt, :], 1, 16, rtmp, [PS[2].b], [krf.b])
            S.dma("pool", "dma_start", dict(out=sd.o_kr[stl.row0:stl.row0 + n, :], in_=krf.ap[0:n, :]), self.dr_o["kr"].next(), reads=[krf.b])
            S.add("pool", "tensor_copy", dict(out=kf_.ap[0:n, :, 64:96], in_=krf.ap[0:n, :].unsqueeze(1).broadcast_to([n, 8, 32])),
                  reads=[krf.b], writes=[kf_.b])
            stq = self.rstd(PS[1].ap[0:n, 0:384], [PS[1].b], n, 384, junk)
            S.add("dve", "scalar_tensor_tensor", dict(out=cqbf.ap[0:n, :], in0=PS[1].ap[0:n, 0:384], scalar=stq.ap[0:n, 2:3], in1=gq.ap[0:n, :],
                                                      op0=ALU.mult, op1=ALU.mult), reads=[PS[1].b, stq.b, gq.b], writes=[cqbf.b])
            pb = PS[3]
            pbf = pb.ap[:, :].bitcast(BF16).rearrange("p (c t) -> p c t", t=128)
            for ch in range(3):
                S.add("pe", "transpose", dict(out=pbf[:, ch, 0:n], in_=cqbf.ap[0:n, ch * 128:(ch + 1) * 128], identity=self.ident.ap[0:n, 0:n]),
                      reads=[cqbf.b, self.ident.b], writes=[pb.b])
            S.add("dve", "tensor_copy", dict(out=cc.ap[:, 2:5, 0:n], in_=pbf[:, 0:3, 0:n]), reads=[pb.b], writes=[cc.b])
            ctrans(cc, cbf.ap, cbf.b, n)

        def stC(c):
            stl, cc, kf_, j = c["stl"], c["cc"], c["kf"], c["j"]
            n = stl.n
            kside2(cc, kf_, stl.kt, n)
            for hf in range(2):
                bank = PS[5 + hf]
                for ch in range(3):
                    S.add("pe", "matmul", dict(out=bank.ap[0:n, 0:384], lhsT=cc.ap[:, 2 + ch, 0:n], rhs=wuq[:, ch, hf * 384:(hf + 1) * 384],
                                               start=(ch == 0), stop=(ch == 2)), reads=[cc.b] + self.wbufs("wuq", hf * 384, (hf + 1) * 384), writes=[bank.b])
                q3 = bank.ap[0:n, 0:384].rearrange("p (h d) -> p h d", d=96)
                S.add("dve", "tensor_copy", dict(out=qtok.ap[0:n, 4 * hf:4 * hf + 4, 0:64], in_=q3[:, :, 0:64]), reads=[bank.b], writes=[qtok.b])
                self.rope("dve", q3[:, :, 64:96], qtok.ap[0:n, 4 * hf:4 * hf + 4, 64:96], n, self.cosM.ap[0:n, stl.tt, :], self.sinM.ap[0:n, stl.tt, :],
                          4, 16, rtmpC, [bank.b], [qtok.b])
            pb2 = PS[7]
            pbf2 = pb2.ap[:, :].bitcast(BF16).rearrange("p (c t) -> p c t", t=128)
            for h in range(8):
                S.add("pe", "transpose", dict(out=pbf2[0:96, h, 0:n], in_=qtok.ap[0:n, h, :], identity=self.ident.ap[0:n, 0:n]),
                      reads=[qtok.b, self.ident.b], writes=[pb2.b])
            S.add("act", "copy", dict(out=QT.ap[0:96, :, j * 128:j * 128 + n], in_=pbf2[0:96, :, 0:n]), reads=[pb2.b], writes=[QT.b])

        for bi, blk in enumerate(sd.blocks):
            nq = sum(s.n for s in blk)
            run_skewed([stA, stB, stC], [dict(stl=stl, j=j) for j, stl in enumerate(blk)])
            if STOP <= 7:
                continue
            tiles = self.key_tiles(sd, blk, bi)
            c0 = blk[0].c0
            units = [(h, ti) for h in range(8) for ti in range(len(tiles))]
            accs = Ring([PS[3], PS[4], PS[5]])
            sring = Ring([PS[0], PS[1], PS[2]])
            pend = []
            cur_acc = {}

            def issue_S(u):
                h, ti = u
                kt, nk, q0, mask = tiles[ti]
                sb_ = sring.next()
                S.add("pe", "matmul", dict(out=sb_.ap[0:nk, q0:nq], lhsT=KT[:, h, kt * 128:kt * 128 + nk], rhs=QT.ap[:, h, q0:nq], start=True, stop=True),
                      reads=[ktb[kt], QT.b], writes=[sb_.b])
                pT = pTr.next()
                S.add("act", "activation", dict(out=pT.ap[0:nk, q0:nq], in_=sb_.ap[0:nk, q0:nq], func=AF.Exp, scale=SC_MLA), reads=[sb_.b], writes=[pT.b])
                if mask:
                    S.add("pool", "memset", dict(ap=pT.ap[64:128, q0:q0 + 64], constant=0.0), writes=[pT.b])
                return pT

            def issue_AV(u, pT):
                h, ti = u
                kt, nk, q0, mask = tiles[ti]
                if ti == 0:
                    cur_acc[h] = accs.next()
                acc = cur_acc[h]
                S.add("pe", "matmul", dict(out=acc.ap[:, q0:nq], lhsT=VMf[0:nk, kt * 520 + h:kt * 520 + h + 1017:8], rhs=pT.ap[0:nk, q0:nq], start=(ti == 0), stop=(ti == len(tiles) - 1)),
                      reads=[vmb[kt], pT.b] + ([vmb[kt + 1]] if kt + 1 < NKT else []), writes=[acc.b])
                if ti == len(tiles) - 1:
                    fin(h, acc)

            deferred = []

            def defer(n, fn):
                deferred.append([n, fn])

            def tick(flush=False):
                while True:
                    due = [d for d in deferred if flush or d[0] <= 0]
                    if not due:
                        break
                    for d in due:
                        deferred.remove(d)
                    for d in due:
                        d[1]()
                for d in deferred:
                    d[0] -= 1

            def fin(h, acc):
                t, od = h // 2, h % 2
                S.add("act", "activation", dict(out=rr32.ap[64:65, 0:nq], in_=acc.ap[64:65, 0:nq], func=AF.Ln), reads=[acc.b], writes=[rr32.b])
                S.add("act", "activation", dict(out=rr.ap[64:65, 0:nq], in_=rr32.ap[64:65, 0:nq], func=AF.Exp, scale=-1.0), reads=[rr32.b], writes=[rr.b])
                S.add("dve", "tensor_copy", dict(out=u_sb.ap[0:64, 0:nq], in_=acc.ap[0:64, 0:nq]), reads=[acc.b], writes=[u_sb.b])

                def stage_b():
                    S.add("pe", "matmul", dict(out=PS[6].ap[0:64, 0:nq], lhsT=self.sel64.ap[:, 0:64], rhs=rr.ap[:, 0:nq], start=True, stop=True),
                          reads=[rr.b, self.sel64.b], writes=[PS[6].b])
                    if od == 0:
                        S.add("dve", "tensor_tensor", dict(out=sd.OB[0:64, t, c0:c0 + nq], in0=u_sb.ap[0:64, 0:nq], in1=PS[6].ap[0:64, 0:nq], op=ALU.mult),
                              reads=[u_sb.b, PS[6].b], writes=[self.OBb[sd.obi[bi]]])
                    else:
                        S.add("dve", "tensor_tensor", dict(out=on_sb.ap[0:64, 0:nq], in0=u_sb.ap[0:64, 0:nq], in1=PS[6].ap[0:64, 0:nq], op=ALU.mult),
                              reads=[u_sb.b, PS[6].b], writes=[on_sb.b])

                        def stage_c():
                            S.add("pe", "matmul", dict(out=PS[7].ap[:, 0:nq], lhsT=self.shiftm.ap[:, :], rhs=on_sb.ap[:, 0:nq], start=True, stop=True),
                                  reads=[on_sb.b, self.shiftm.b], writes=[PS[7].b])
                            S.add("dve", "tensor_copy", dict(out=sd.OB[64:128, t, c0:c0 + nq], in_=PS[7].ap[64:128, 0:nq]), reads=[PS[7].b], writes=[self.OBb[sd.obi[bi]]])
                        defer(2, stage_c)
                defer(3, stage_b)

            LOOK = 2
            for i, u in enumerate(units):
                pend.append((u, issue_S(u)))
                if len(pend) > LOOK:
                    issue_AV(*pend.pop(0))
                    tick()
            while pend:
                issue_AV(*pend.pop(0))
                tick()
            tick(flush=True)

    def da_pass(self, sd, reload=True):
        S = self.S
        TK = self.TK
        NKT = TK // 128
        regs = [self.U[:, 8 * TK + 12288:self.U_N]] if self.U_N - (8 * TK + 12288) > 1024 else []
        if not sd.prompt and self.SL >= 2048:
            regs.append(self.U[:, 4 * TK + 9 * 512:8 * TK])
        self.new_arena(regs)
        ar = self.ar
        w2 = self.U[:, 8 * TK:8 * TK + 12288].rearrange("p (c n) -> p c n", n=1536)
        if reload:
            self.load_w(w2, self.w_in, 8, 0, 1536, "w2")
        KT = self.U[:, 0:4 * TK].rearrange("p (h t) -> p h t", t=TK)
        VD = self.U[:, 4 * TK:8 * TK].rearrange("p (k c) -> p k c", c=512)
        ktb = [Buf(f"ktd{i}") for i in range(NKT)]
        vdb = [Buf(f"vd{i}") for i in range(NKT)]
        hTr = Ring([ar.alloc([8, 128], BF16, f"hTs{i}") for i in range(2)])
        pTr = Ring([ar.alloc([512], BF16, f"pT{i}") for i in range(4)])
        QT = ar.alloc([4, 2, 512], BF16, "QT")
        S.add("pool", "memset", dict(ap=QT.ap[64:128, :, 0, :], constant=0.0), writes=[QT.b])
        S.add("pool", "memset", dict(ap=QT.ap[0:64, :, 1, :], constant=0.0), writes=[QT.b])
        kfr = Ring([ar.alloc([512], F32, f"kf{i}") for i in range(1)])
        vfr = Ring([ar.alloc([512], F32, f"vf{i}") for i in range(1)])
        kb = ar.alloc([512], BF16, "kb")
        qb = ar.alloc([512], BF16, "qb")
        rtmp = ar.alloc([8 * 16], F32, "rtmp")
        t0 = ar.alloc([512], F32, "t0")
        t1 = ar.alloc([512], F32, "t1")
        oraw = ar.alloc([512], F32, "oraw")
        sq = kb
        lnr = ar.alloc([512], F32, "lnr")
        PS = self.PS
        OA = sd.OA

        def k_transposes(kb_ap, kb_b, kt, n):
            pb = PS[4]
            pbf = pb.ap[:, :].bitcast(BF16).rearrange("p (c t) -> p c t", t=128)
            for h in range(4):
                S.add("pe", "transpose", dict(out=pbf[:, h, 0:n], in_=kb_ap[0:n, h * 128:(h + 1) * 128], identity=self.ident.ap[0:n, 0:n]),
                      reads=[kb_b, self.ident.b], writes=[pb.b])
            S.add("act", "copy", dict(out=KT[:, :, kt * 128:kt * 128 + n], in_=pbf[:, 0:4, 0:n]), reads=[pb.b], writes=[ktb[kt]])

        if sd.ncache:
            kst = ar.alloc([8, 512], BF16, "kst")
            S.dma("pool", "dma_start", dict(out=kst.ap[:], in_=self.cdk.rearrange("(t p) c -> p t c", p=128)), self.ds_cache, writes=[kst.b])
            S.dma("pool", "dma_start", dict(out=VD[:, 0:8, :], in_=self.cdv.rearrange("(t p) c -> p t c", p=128)), self.ds_cache, writes=vdb[0:8])
            for t in range(sd.ncache):
                k_transposes(kst.ap[:, t, :], kst.b, t, 128)

        for bi, blk in enumerate(sd.blocks):
            nq = sum(s.n for s in blk)
            c0 = blk[0].c0
            for j, stl in enumerate(blk):
                n = stl.n
                kt = stl.kt
                hT = hTr.next()
                self.prologue(sd, stl, hT.ap[:, :, 0:n], hT.b)
                self.proj_tok(hT, n, w2[:, :, 0:512], 512, PS[1], self.wbufs("w2", 0, 512))
                self.proj_tok(hT, n, w2[:, :, 512:1024], 512, PS[2], self.wbufs("w2", 512, 1024))
                self.proj_tok(hT, n, w2[:, :, 1024:1536], 512, PS[3], self.wbufs("w2", 1024, 1536))
                cosd = self.cosD.ap[0:n, stl.tt, :]
                sind = self.sinD.ap[0:n, stl.tt, :]
                kf = kfr.next()
                S.add("act", "copy", dict(out=kf.ap[0:n, :], in_=PS[2].ap[0:n, :]), reads=[PS[2].b], writes=[kf.b])
                k4 = PS[2].ap[0:n, :].rearrange("p (g d) -> p g d", d=64)[:, :, 0:16]
                kf4 = kf.ap[0:n, :].rearrange("p (g d) -> p g d", d=64)[:, :, 0:16]
                self.rope("dve", k4, kf4, n, cosd, sind, 8, 8, rtmp, [PS[2].b], [kf.b])
                S.dma("pool", "dma_start", dict(out=sd.o_k[stl.row0:stl.row0 + stl.n, :], in_=kf.ap[0:stl.n, :]), self.dr_o["k"].next(), reads=[kf.b])
                S.add("pool", "tensor_copy", dict(out=kb.ap[0:n, :], in_=kf.ap[0:n, :]), reads=[kf.b], writes=[kb.b])
                k_transposes(kb.ap, kb.b, kt, n)
                vf = vfr.next()
                S.add("act", "copy", dict(out=vf.ap[0:n, :], in_=PS[3].ap[0:n, :]), reads=[PS[3].b], writes=[vf.b])
                S.dma("pool", "dma_start", dict(out=sd.o_v[stl.row0:stl.row0 + stl.n, :], in_=vf.ap[0:stl.n, :]), self.dr_o["v"].next(), reads=[vf.b])
                S.add("dve", "tensor_copy", dict(out=VD[0:n, kt, :], in_=PS[3].ap[0:n, :]), reads=[PS[3].b], writes=[vdb[kt]])
                S.add("act", "copy", dict(out=qb.ap[0:n, :], in_=PS[1].ap[0:n, :]), reads=[PS[1].b], writes=[qb.b])
                q4 = PS[1].ap[0:n, :].rearrange("p (g d) -> p g d", d=64)[:, :, 0:16]
                qb4 = qb.ap[0:n, :].rearrange("p (g d) -> p g d", d=64)[:, :, 0:16]
                self.rope("dve", q4, qb4, n, cosd, sind, 8, 8, rtmp, [PS[1].b], [qb.b])
                pb = PS[5]
                pbf = pb.ap[:, :].bitcast(BF16).rearrange("p (c t) -> p c t", t=128)
                for h in range(4):
                    S.add("pe", "transpose", dict(out=pbf[:, h, 0:n], in_=qb.ap[0:n, h * 128:(h + 1) * 128], identity=self.ident.ap[0:n, 0:n]),
                          reads=[qb.b, self.ident.b], writes=[pb.b])
                S.add("act", "copy", dict(out=QT.ap[0:64, :, 0, j * 128:j * 128 + n], in_=pbf[0:64, 0:4, 0:n]), reads=[pb.b], writes=[QT.b])
                S.add("dve", "tensor_copy", dict(out=QT.ap[64:128, :, 1, j * 128:j * 128 + n], in_=pbf[64:128, 0:4, 0:n]), reads=[pb.b], writes=[QT.b])
            tiles = self.key_tiles(sd, blk, bi)
            units = [(h, c, ti) for h in range(4) for c in range(2) for ti in range(len(tiles))]
            accs = Ring([(PS[3], PS[4]), (PS[5], PS[6])])
            sring = Ring([PS[0], PS[1], PS[2]])
            pend = []
            cur_acc = {}
            tt = {0: t0, 1: t1}

            def issue_S(u):
                h, c, ti = u
                kt, nk, q0, mask = tiles[ti]
                sb_ = sring.next()
                S.add("pe", "matmul", dict(out=sb_.ap[0:nk, q0:nq], lhsT=KT[:, h, kt * 128:kt * 128 + nk], rhs=QT.ap[:, h, c, q0:nq],
                                               start=True, stop=True), reads=[ktb[kt], QT.b], writes=[sb_.b])
                pT = pTr.next()
                S.add("act", "activation", dict(out=pT.ap[0:nk, q0:nq], in_=sb_.ap[0:nk, q0:nq], func=AF.Exp, scale=SC_DA), reads=[sb_.b], writes=[pT.b])
                if mask:
                    S.add("pool", "memset", dict(ap=pT.ap[64:128, q0:q0 + 64], constant=0.0), writes=[pT.b])
                return pT

            def issue_AV(u, pT):
                h, c, ti = u
                kt, nk, q0, mask = tiles[ti]
                if ti == 0:
                    cur_acc[(h, c)] = accs.next()
                au, asum = cur_acc[(h, c)]
                first, last = (ti == 0), (ti == len(tiles) - 1)
                S.add("pe", "matmul", dict(out=au.ap[:, q0:nq], lhsT=VD[0:nk, kt, h * 128:(h + 1) * 128], rhs=pT.ap[0:nk, q0:nq], start=first, stop=last),
                      reads=[vdb[kt], pT.b], writes=[au.b])
                S.add("pe", "matmul", dict(out=asum.ap[:, q0:nq], lhsT=self.ones.ap[0:nk, :], rhs=pT.ap[0:nk, q0:nq], start=first, stop=last),
                      reads=[self.ones.b, pT.b], writes=[asum.b])
                if last:
                    t = tt[c]
                    if c == 0:
                        S.add("dve", "reciprocal", dict(out=t.ap[:, 0:nq], in_=asum.ap[:, 0:nq]), reads=[asum.b], writes=[t.b])
                    else:
                        S.add("act", "activation", dict(out=t.ap[:, 0:nq], in_=asum.ap[:, 0:nq], func=AF.Ln), reads=[asum.b], writes=[t.b])
                        S.add("act", "activation", dict(out=t.ap[:, 0:nq], in_=t.ap[:, 0:nq], func=AF.Exp, scale=-1.0), reads=[t.b], writes=[t.b])
                    S.add("dve", "tensor_tensor", dict(out=t.ap[:, 0:nq], in0=t.ap[:, 0:nq], in1=au.ap[:, 0:nq], op=ALU.mult), reads=[au.b, t.b], writes=[t.b])
                    if c == 1:
                        fin(h)

            deferred = []

            def defer(n, fn):
                deferred.append([n, fn])

            def tick(flush=False):
                while True:
                    due = [d for d in deferred if flush or d[0] <= 0]
                    if not due:
                        break
                    for d in due:
                        deferred.remove(d)
                    for d in due:
                        d[1]()
                for d in deferred:
                    d[0] -= 1

            def fin(h):
                v = self.vec
                S.add("dve", "scalar_tensor_tensor", dict(out=oraw.ap[:, 0:nq], in0=t1.ap[:, 0:nq], scalar=v.ap[:, 0:1], in1=t0.ap[:, 0:nq], op0=ALU.mult, op1=ALU.add),
                      reads=[t0.b, t1.b, v.b], writes=[oraw.b])
                S.add("act", "activation", dict(out=sq.ap[:, 0:nq], in_=oraw.ap[:, 0:nq], func=AF.Square), reads=[oraw.b], writes=[sq.b])

                def stage_b():
                    S.add("pe", "matmul", dict(out=PS[7].ap[:, 0:nq], lhsT=self.ones.ap[:, :], rhs=sq.ap[:, 0:nq], start=True, stop=True), reads=[sq.b, self.ones.b], writes=[PS[7].b])
                    S.add("act", "activation", dict(out=lnr.ap[:, 0:nq], in_=PS[7].ap[:, 0:nq], func=AF.Ln, scale=1.0 / 128, bias=EPS), reads=[PS[7].b], writes=[lnr.b])
                    S.add("act", "activation", dict(out=lnr.ap[:, 0:nq], in_=lnr.ap[:, 0:nq], func=AF.Exp, scale=-0.5), reads=[lnr.b], writes=[lnr.b])
                    S.add("dve", "scalar_tensor_tensor", dict(out=OA[:, h, c0:c0 + nq], in0=oraw.ap[:, 0:nq], scalar=v.ap[:, 1:2], in1=lnr.ap[:, 0:nq], op0=ALU.mult, op1=ALU.mult),
                          reads=[oraw.b, lnr.b, v.b], writes=[self.OAb[sd.obi[bi]]])
                defer(5, stage_b)

            LOOK = 2
            for i, u in enumerate(units):
                pend.append((u, issue_S(u)))
                if len(pend) > LOOK:
                    issue_AV(*pend.pop(0))
                    tick()
            while pend:
                issue_AV(*pend.pop(0))
                tick()
            tick(flush=True)

    def fin_pass(self, sd, reload=True):
        S = self.S
        SL = self.SL
        self.new_arena([self.U[:, 40960:self.U_N]] if self.U_N - 40960 > 1024 else [])
        ar = self.ar
        U = self.U
        wz = U[:, 0:4096].rearrange("p (c n) -> p c n", n=512)
        wg = U[:, 4096:24576].rearrange("p (c n) -> p c n", n=2560)
        wba = U[:, 24576:28672].rearrange("p (c n) -> p c n", n=1024)
        wbb = U[:, 28672:32768].rearrange("p (c n) -> p c n", n=1024)
        wout = U[:, 32768:40960].rearrange("p (c n) -> p c n", n=1024)
        if reload:
            self.load_w(wz, self.w_in, 8, C_DAZ, C_DAZ + 512, "wz")
            self.load_w(wg, self.w_in, 8, C_MZ, IN_COLS, "wg")
            self.load_w(wba, self.i_wba, 4, 0, 1024, "wba")
            self.load_w(wbb, self.i_wbb, 4, 0, 1024, "wbb")
            self.load_w(wout, self.i_wout, 8, 0, 1024, "wout")
        gf = ar.alloc([D], F32, "gf")
        S.dma("sp", "dma_start", dict(out=gf.ap[:], in_=self.i_gf.partition_broadcast(128)), self.ds_const, writes=[gf.b])
        hT = ar.alloc([8, 512], BF16, "hT")
        oz = ar.alloc([8, 512], BF16, "oz")
        mT = ar.alloc([8, 512], BF16, "mT")
        sg = Ring([ar.alloc([512], F32, f"sg{i}") for i in range(2)])
        tg = Ring([ar.alloc([512], F32, f"tg{i}") for i in range(2)])
        PS = self.PS
        OA = sd.OA
        OB = sd.OB
        zb = Ring([PS[i] for i in range(1, 8)])
        ojunk = [None]
        if os.environ.get("K_PX", "0") == "1":
            pro_x = Ring([self.xring.items[0], self.xring.items[1]])
            out_x = Ring([self.xring.items[1]])
        else:
            pro_x = Ring([self.xring.items[0]])
            out_x = Ring([self.xring.items[1]])
        for bi, blk in enumerate(sd.blocks):
            nq = sum(s.n for s in blk)
            c0 = blk[0].c0
            self.xring = pro_x
            if sd.prompt and bi >= 1:
                c0p = sd.blocks[bi - 1][0].c0
                hch = [OA[:, c, c0p:c0p + 512] for c in range(4)] + [OB[:, c, c0p:c0p + 512] for c in range(4)]
                hbufs = [self.OAb[bi - 1], self.OBb[bi - 1]]
                for j, stl in enumerate(blk):
                    self.prologue(sd, stl, None, None, outs=[(OA[:, :, c0p + j * 128:c0p + j * 128 + stl.n], 0, 4, [self.OAb[bi - 1]]),
                                                             (OB[:, :, c0p + j * 128:c0p + j * 128 + stl.n], 4, 8, [self.OBb[bi - 1]])])
            else:
                hch = [hT.ap[:, c, :] for c in range(8)]
                hbufs = [hT.b]
                for j, stl in enumerate(blk):
                    self.prologue(sd, stl, hT.ap[:, :, j * 128:j * 128 + stl.n], hT.b)
            for m in range(8):
                bank = zb.next()
                wsrc = wz[:, :, m * 128:(m + 1) * 128] if m < 4 else wg[:, :, (m - 4) * 128:(m - 3) * 128]
                wzb = self.wbufs("wz") if m < 4 else self.wbufs("wg", (m - 4) * 128, (m - 3) * 128)
                for c in range(8):
                    S.add("pe", "matmul", dict(out=bank.ap[:, 0:nq], lhsT=wsrc[:, c, :], rhs=hch[c][:, 0:nq], start=(c == 0), stop=(c == 7)),
                          reads=hbufs + wzb, writes=[bank.b])
                s_ = sg.next()
                S.add("act", "activation", dict(out=s_.ap[:, 0:nq], in_=bank.ap[:, 0:nq], func=AF.Silu), reads=[bank.b], writes=[s_.b])
                osrc = OA[:, m, c0:c0 + nq] if m < 4 else OB[:, m - 4, c0:c0 + nq]
                ob = [self.OAb[sd.obi[bi]]] if m < 4 else [self.OBb[sd.obi[bi]]]
                S.add("dve", "tensor_tensor", dict(out=oz.ap[:, m, 0:nq], in0=s_.ap[:, 0:nq], in1=osrc, op=ALU.mult), reads=[s_.b] + ob, writes=[oz.b])
            for m in range(8):
                bya, byb, bga, bgb = zb.next(), zb.next(), zb.next(), zb.next()
                for c in range(4):
                    S.add("pe", "matmul", dict(out=bya.ap[:, 0:nq], lhsT=wba[:, c, m * 128:(m + 1) * 128], rhs=oz.ap[:, c, 0:nq], start=(c == 0), stop=(c == 3)),
                          reads=[oz.b] + self.wbufs("wba"), writes=[bya.b])
                for c in range(4):
                    S.add("pe", "matmul", dict(out=byb.ap[:, 0:nq], lhsT=wbb[:, c, m * 128:(m + 1) * 128], rhs=oz.ap[:, 4 + c, 0:nq], start=(c == 0), stop=(c == 3)),
                          reads=[oz.b] + self.wbufs("wbb"), writes=[byb.b])
                for c in range(8):
                    S.add("pe", "matmul", dict(out=bga.ap[:, 0:nq], lhsT=wg[:, c, 512 + m * 128:512 + (m + 1) * 128], rhs=hch[c][:, 0:nq], start=(c == 0), stop=(c == 7)),
                          reads=hbufs + self.wbufs("wg", 512 + m * 128, 512 + (m + 1) * 128), writes=[bga.b])
                for c in range(8):
                    S.add("pe", "matmul", dict(out=bgb.ap[:, 0:nq], lhsT=wg[:, c, 1536 + m * 128:1536 + (m + 1) * 128], rhs=hch[c][:, 0:nq], start=(c == 0), stop=(c == 7)),
                          reads=hbufs + self.wbufs("wg", 1536 + m * 128, 1536 + (m + 1) * 128), writes=[bgb.b])
                ga, gb_ = sg.next(), sg.next()
                S.add("act", "activation", dict(out=ga.ap[:, 0:nq], in_=bga.ap[:, 0:nq], func=AF.Sigmoid, bias=self.gateb.ap[:, m:m + 1]), reads=[bga.b, self.gateb.b], writes=[ga.b])
                S.add("act", "activation", dict(out=gb_.ap[:, 0:nq], in_=bgb.ap[:, 0:nq], func=AF.Sigmoid, bias=self.gateb.ap[:, 8 + m:9 + m]), reads=[bgb.b, self.gateb.b], writes=[gb_.b])
                ta, tb = tg.next(), tg.next()
                S.add("dve", "tensor_tensor", dict(out=ta.ap[:, 0:nq], in0=ga.ap[:, 0:nq], in1=bya.ap[:, 0:nq], op=ALU.mult), reads=[ga.b, bya.b], writes=[ta.b])
                S.add("dve", "tensor_tensor", dict(out=tb.ap[:, 0:nq], in0=gb_.ap[:, 0:nq], in1=byb.ap[:, 0:nq], op=ALU.mult), reads=[gb_.b, byb.b], writes=[tb.b])
                S.add("pool", "tensor_tensor", dict(out=mT.ap[:, m, 0:nq], in0=ta.ap[:, 0:nq], in1=tb.ap[:, 0:nq], op=ALU.add), reads=[ta.b, tb.b], writes=[mT.b])
            if sd.prompt and bi == 0 and len(sd.blocks) > 1:
                x2 = R(hT.ap[:, 0:4, :].rearrange("p a b -> p (a b)").bitcast(F32), "xt2")
                x2.b.w = hT.b.w
                x2.b.rs = list(hT.b.rs)
                if os.environ.get("K_PX", "0") == "1":
                    out_x.items[:] = [x2]
                else:
                    out_x.items.append(x2)
                jk = R(hT.ap[:, 4:6, :].rearrange("p a b -> p (a b)"), "ojunk")
                jk.b.w = hT.b.w
                jk.b.rs = list(hT.b.rs)
                ojunk[0] = jk
            self.xring = out_x
            for j, stl in enumerate(blk):
                n = stl.n
                xt = self.load_x(sd, stl, self.dr_x2)
                for hf in range(2):
                    bank = zb.next()
                    for c in range(8):
                        S.add("pe", "matmul", dict(out=bank.ap[0:n, :], lhsT=mT.ap[:, c, j * 128:j * 128 + n], rhs=wout[:, c, hf * 512:(hf + 1) * 512],
                                                                                  start=(c == 0), stop=(c == 7)), reads=[mT.b] + self.wbufs("wout", hf * 512, (hf + 1) * 512), writes=[bank.b])
                    S.add("dve", "tensor_tensor", dict(out=xt.ap[0:n, hf * 512:(hf + 1) * 512], in0=xt.ap[0:n, hf * 512:(hf + 1) * 512], in1=bank.ap[0:n, :], op=ALU.add),
                          reads=[bank.b, xt.b], writes=[xt.b])
                stt = self.rstd(xt.ap[0:n, :], [xt.b], n, D, ojunk[0] if ojunk[0] is not None else self.hbring.next())
                S.add("dve", "scalar_tensor_tensor", dict(out=xt.ap[0:n, :], in0=xt.ap[0:n, :], scalar=stt.ap[0:n, 2:3], in1=gf.ap[0:n, :], op0=ALU.mult, op1=ALU.mult),
                      reads=[xt.b, stt.b, gf.b], writes=[xt.b])
                S.dma("pool", "dma_start", dict(out=sd.o_y[stl.row0:stl.row0 + stl.n, :], in_=xt.ap[0:stl.n, :]), self.dr_o["y"].next(), reads=[xt.b])


def rope_tables(SL, sample):
    ntt = SL // 128 + 1
    pos = np.zeros((128, ntt), np.float64)
    for t in range(ntt - 1):
        pos[:, t] = t * 128 + np.arange(128)
    pos[:, ntt - 1] = 1024 + np.arange(128)
    out = {}
    for name, rot in (("D", 16), ("M", 32)):
        half = rot // 2
        inv = (np.float32(500000.0) ** (-np.arange(half, dtype=np.float32) * np.float32(2.0) / np.float32(rot))).astype(np.float32)
        ang = (pos.astype(np.float32)[:, :, None] * inv[None, None, :]).astype(np.float32)
        out["cos" + name] = np.cos(ang.astype(np.float64)).astype(np.float32).reshape(128, ntt * half)
        out["sin" + name] = np.sin(ang.astype(np.float64)).astype(np.float32).reshape(128, ntt * half)
    return out


_CACHE = {}


def get_nc(NSEQ, SL, SAMPLE, parts=("mla", "da", "fin")):
    key = (NSEQ, SL, SAMPLE, parts)
    if key not in _CACHE:
        b = Builder(NSEQ, SL, SAMPLE, parts)
        nc = b.build()
        _CACHE[key] = (nc, b)
    return _CACHE[key]


def shared_inputs(inp, SL):
    f = lambda a: np.ascontiguousarray(np.asarray(a, dtype=np.float32))
    sh = {
        "w_in": f(inp["w_in"][0]),
        "norm_g": f(inp["norm_g"][0]).reshape(1, D),
        "gate_bT": f(np.asarray(inp["gate_b"][0]).reshape(16, 128).T),
        "da_lambda": f(inp["da_lambda"][0]).reshape(1, 256),
        "hng": f(inp["da_head_norm_g"][0]).reshape(128, 1),
        "gq": f(inp["mla_q_norm_g"][0]).reshape(1, 384),
        "gkv": f(inp["mla_kv_norm_g"][0]).reshape(1, 256),
        "w_uq": f(inp["mla_w_uq"][0]),
        "w_uk": f(inp["mla_w_uk"][0]),
        "w_uv": f(np.asarray(inp["mla_w_uv"][0]).reshape(256, 8, 64).transpose(0, 2, 1).reshape(256, 512)),
        "w_ba": f(inp["w_branch_a"][0]),
        "w_bb": f(inp["w_branch_b"][0]),
        "w_out": f(inp["w_out"][0]),
        "gf": f(inp["final_norm_g"]).reshape(1, D),
        "ident": np.eye(128, dtype=np.float32),
        "shiftm": np.eye(128, k=64, dtype=np.float32),
    }
    sh.update(rope_tables(SL, True))
    return sh


def kernel(**inputs):
    NCORES = 8
    xp = np.asarray(inputs["x_prompt"], dtype=np.float32)
    xs = np.asarray(inputs["x_sample"], dtype=np.float32)
    B, SL, _ = xp.shape
    NSEQ = B // NCORES
    nc, _ = get_nc(NSEQ, SL, True)
    sh = shared_inputs(inputs, SL)
    cdk = np.asarray(inputs["cache_da_k"], dtype=np.float32)[0]
    cdv = np.asarray(inputs["cache_da_v"], dtype=np.float32)[0]
    clat = np.asarray(inputs["cache_mla_latent"], dtype=np.float32)[0]
    ckr = np.asarray(inputs["cache_mla_krope"], dtype=np.float32)[0]
    in_maps = []
    for c in range(NCORES):
        m = dict(sh)
        m["xp"] = np.ascontiguousarray(xp[c * NSEQ:(c + 1) * NSEQ].reshape(NSEQ * SL, D))
        m["xs"] = np.ascontiguousarray(xs[c])
        m["cdk"] = np.ascontiguousarray(cdk[c].reshape(1024, 512))
        m["cdv"] = np.ascontiguousarray(cdv[c].reshape(1024, 512))
        m["clat"] = np.ascontiguousarray(clat[c])
        m["ckr"] = np.ascontiguousarray(ckr[c])
        in_maps.append(m)
    res = run_bass_kernel_spmd(nc, in_maps, core_ids=list(range(NCORES))).results
    cat = lambda k: np.concatenate([np.asarray(r[k]) for r in res], axis=0)
    y_p = cat("yp").reshape(B, SL, D)
    y_s = cat("ys").reshape(NCORES, 64, D)
    k_p = cat("kp").reshape(1, B, SL, 4, 128)
    v_p = cat("vp").reshape(1, B, SL, 4, 128)
    lat_p = cat("latp").reshape(1, B, SL, 256)
    kr_p = cat("krp").reshape(1, B, SL, 32)
    k_s = cat("ks").reshape(1, NCORES, 64, 4, 128)
    v_s = cat("vs").reshape(1, NCORES, 64, 4, 128)
    lat_s = cat("lats").reshape(1, NCORES, 64, 256)
    kr_s = cat("krs").reshape(1, NCORES, 64, 32)
    return tuple(np.ascontiguousarray(a, dtype=np.float32) for a in (y_p, y_s, k_p, v_p, lat_p, kr_p, k_s, v_s, lat_s, kr_s))
```

```python
import math
import numpy as np
from contextlib import ExitStack
import concourse.bass as bass
import concourse.mybir as mybir
from concourse.bass_utils import run_bass_kernel_spmd

F32 = mybir.dt.float32
BF16 = mybir.dt.bfloat16
AF = mybir.ActivationFunctionType
ALU = mybir.AluOpType

D = 1024
SEM_ROT = 12000
SCL = {}
EPS = 1e-6
import os
STOP = int(os.environ.get('KSTOP', '99'))
SUB = int(os.environ.get('KSUB', '99'))
C_DAQ, C_DAK, C_DAV, C_DAZ, C_CQ, C_CKV, C_KR, C_MZ, C_G = 0, 512, 1024, 1536, 2048, 2432, 2688, 2720, 3232
IN_COLS = 5280
LAM_INIT = 0.8 - 0.6 * math.exp(-0.3 * 0)
SC_DA = 64 ** -0.5
SC_MLA = 96 ** -0.5


class Buf:
    __slots__ = ("name", "w", "rs", "excl")

    def __init__(self, name="", excl=False):
        self.name = name
        self.w = None
        self.rs = []
        self.excl = excl


class DmaSem:
    def __init__(self, sem):
        self.sem = sem
        self.count = 0
        self.last_group = None


class DmaGroup:
    def __init__(self, ds):
        self.ds = ds
        self.final = None
        self.last_op = None


class Op:
    __slots__ = ("eng", "name", "kw", "deps", "signal", "sem", "val", "idx", "group", "is_dma", "gidx", "region", "fin")

    def __init__(self, eng, name, kw):
        self.eng = eng
        self.name = name
        self.kw = kw
        self.deps = []
        self.signal = False
        self.sem = None
        self.val = None
        self.idx = None
        self.group = None
        self.is_dma = False


class Sched:
    ENGS = ("pe", "act", "dve", "pool", "sp")

    def __init__(self, nc, stack):
        self.nc = nc
        self.ops = {e: [] for e in self.ENGS}
        self.dma_sems = []
        self._stack = stack
        self.bar_deps = []
        self.bar_seen = {e: True for e in self.ENGS}
        self.all_ops = []
        self.region = 0
        self.cpy_to_dve = False

    def new_sem(self, name):
        return self._stack.enter_context(self.nc.semaphore(name))

    def dma_sem(self, name):
        ds = DmaSem(self.new_sem(name))
        self.dma_sems.append(ds)
        return ds

    def dma_ring(self, name, n):
        return DmaRing([self.dma_sem(f"{name}{i}") for i in range(n)])

    def barrier(self):
        self.region += 1

    def _collect(self, op, reads, writes, extra):
        deps = []
        for b in reads:
            if b.w is not None:
                deps.append(b.w)
            if b.excl:
                for r in b.rs:
                    if r.eng != op.eng:
                        deps.append(r)
        for b in writes:
            if b.w is not None:
                deps.append(b.w)
            deps.extend(b.rs)
        deps.extend(extra)
        seen = set()
        out = []
        for d in deps:
            if d is op or id(d) in seen:
                continue
            seen.add(id(d))
            out.append(d)
        op.deps = out
        op.gidx = len(self.all_ops)
        op.region = self.region
        self.all_ops.append(op)
        for b in writes:
            b.w = op
            b.rs = []
        for b in reads:
            b.rs.append(op)

    def add(self, eng, name, kw, reads=(), writes=(), extra=()):
        if self.cpy_to_dve and eng == "act" and name == "copy":
            eng, name = "dve", "tensor_copy"
        op = Op(eng, name, kw)
        self._collect(op, reads, writes, extra)
        self.ops[eng].append(op)
        return op

    def dma(self, eng, name, kw, ds, reads=(), writes=(), extra=()):
        op = Op(eng, name, kw)
        op.is_dma = True
        extra = list(extra)
        g = DmaGroup(ds)
        if ds.last_group is not None:
            extra.append(ds.last_group.last_op)
        ds.last_group = g
        ds.count += 1
        g.final = 16 * ds.count
        g.last_op = op
        op.group = g
        op.sem = ds.sem
        op.signal = True
        self._collect(op, reads, writes, extra)
        self.ops[eng].append(op)
        return op

    @staticmethod
    def _fsize(ap):
        n = 1
        for x in ap.shape[1:]:
            n *= x
        return n

    def _dur(self, op):
        return self._dur0(op) * SCL.get("dma" if op.is_dma else op.eng, 1.0)

    def _dur0(self, op):
        kw = op.kw
        if op.is_dma:
            o = kw["out"]
            nbytes = self._fsize(o) * o.shape[0] * (4 if o.dtype == F32 else 2)
            return 2200.0 + nbytes / 120.0
        if op.eng == "pe":
            if op.name == "transpose":
                return 70.0
            return 8.0 + 0.41 * self._fsize(kw["rhs"])
        a = kw.get("in_", kw.get("in0", kw.get("out", kw.get("ap"))))
        f = self._fsize(a)
        if op.eng == "act":
            if os.environ.get("KEXP2") and f == 512 and kw.get("func") == AF.Exp:
                return (190.0 + 1024 / 1.2) / 2
            return 190.0 + f / 1.2 + (100.0 if "accum_out" in kw else 0.0)
        if op.eng == "dve":
            if op.name == "reciprocal":
                return 80.0 + 6.6 * f
            return 100.0 + f / 0.85
        return 200.0 + f / 0.48

    def schedule(self, dry=False, beta=None):
        import heapq
        if beta is None:
            beta = float(os.environ.get("KBETA", "0.5"))
        regions = {}
        for op in self.all_ops:
            regions.setdefault(op.region, []).append(op)
        new_ops = {e: [] for e in self.ENGS}
        t0 = 0.0
        tail = []
        for r in sorted(regions):
            ops = regions[r]
            reg_first = {}
            reg_last = {}
            dma_last = {}
            inreg = set(id(o) for o in ops)
            succ = {}
            indeg = {}
            ready = {}
            for o in ops:
                cnt = 0
                for d in o.deps:
                    if id(d) in inreg:
                        cnt += 1
                        succ.setdefault(id(d), []).append(o)
                indeg[id(o)] = cnt
                ready[id(o)] = t0
            bl = {}
            if beta:
                for o in reversed(ops):
                    m = 0.0
                    for sc in succ.get(id(o), ()):
                        v = bl[id(sc)]
                        if v > m:
                            m = v
                    bl[id(o)] = m + self._dur(o)
            heap = [(t0 - beta * bl.get(id(o), 0.0), o.gidx, o) for o in ops if indeg[id(o)] == 0]
            heapq.heapify(heap)
            free = {e: t0 for e in self.ENGS}
            tmax = t0
            while heap:
                _, _, o = heapq.heappop(heap)
                rt = ready[id(o)]
                st = max(rt, free[o.eng])
                if o.is_dma:
                    issue = 1000.0 if o.eng == "pool" else 80.0
                    free[o.eng] = st + issue
                    fin = st + issue + self._dur(o)
                else:
                    fin = st + self._dur(o)
                    free[o.eng] = fin
                o.fin = fin
                tmax = max(tmax, fin)
                new_ops[o.eng].append(o)
                if o.eng not in reg_first:
                    reg_first[o.eng] = o
                if o.is_dma:
                    cur = dma_last.get(id(o.sem))
                    if cur is None or o.group.final > cur.group.final:
                        dma_last[id(o.sem)] = o
                else:
                    reg_last[o.eng] = o
                for sc in succ.get(id(o), ()):
                    if o.eng == "pe" and sc.eng == "pe" and not sc.is_dma and not o.is_dma:
                        lat = 0.0
                    elif o.eng == sc.eng and not o.is_dma:
                        lat = 120.0
                    else:
                        lat = 220.0
                    lat *= SCL.get("lat", 1.0)
                    ready[id(sc)] = max(ready[id(sc)], fin + lat)
                    indeg[id(sc)] -= 1
                    if indeg[id(sc)] == 0:
                        heapq.heappush(heap, (ready[id(sc)] - beta * bl.get(id(sc), 0.0), sc.gidx, sc))
            if not dry:
                for e, o in reg_first.items():
                    have = set(id(d) for d in o.deps)
                    o.deps = list(o.deps) + [d for d in tail if id(d) not in have and d is not o]
            tail = list(reg_last.values()) + list(dma_last.values())
            if os.environ.get("KDEBUG"):
                busy = {}
                for o in ops:
                    if not o.is_dma:
                        busy[o.eng] = busy.get(o.eng, 0.0) + self._dur(o)
                print("region", r, "ops", len(ops), "dur us", round((tmax - t0) / 1000, 1), {k: round(v / 1000) for k, v in busy.items()})
            t0 = tmax
        assert sum(len(v) for v in new_ops.values()) == len(self.all_ops)
        if dry:
            return t0
        self.ops = new_ops
        self.est_ns = t0

    def finalize(self):
        if os.environ.get("KSCHED", "1") == "1":
            self.schedule()
        for e in self.ENGS:
            for i, op in enumerate(self.ops[e]):
                op.idx = i
        for e in self.ENGS:
            for op in self.ops[e]:
                best = {}
                out = []
                seen_groups = set()
                for d in op.deps:
                    if d.is_dma:
                        if id(d.group) not in seen_groups:
                            seen_groups.add(id(d.group))
                            out.append(d)
                    else:
                        if d.eng == "pe" and op.eng == "pe" and not op.is_dma:
                            continue
                        cur = best.get(d.eng)
                        if cur is None or d.idx > cur.idx:
                            best[d.eng] = d
                out.extend(best.values())
                for d in out:
                    d.signal = True
                op.deps = out
        for e in ("pe", "act", "dve", "pool"):
            nsem = 0
            cnt = 0
            cur = None
            for op in self.ops[e]:
                if op.is_dma or not op.signal:
                    continue
                if cur is None or cnt >= SEM_ROT:
                    cur = self.new_sem(f"s_{e}{nsem}")
                    nsem += 1
                    cnt = 0
                cnt += 1
                op.sem = cur
                op.val = cnt
        for e in self.ENGS:
            for op in self.ops[e]:
                if op.is_dma:
                    op.val = op.group.final

    def emit(self, block):
        self.finalize()
        stats = {}

        def run(e):
            def body(eng):
                seen = {}
                nw = 0
                for op in self.ops[e]:
                    for d in op.deps:
                        k = id(d.sem)
                        if seen.get(k, 0) >= d.val:
                            continue
                        seen[k] = d.val
                        eng.wait_ge(d.sem, d.val)
                        nw += 1
                    ins = getattr(eng, op.name)(**op.kw)
                    if op.signal:
                        ins.then_inc(op.sem, 16 if op.is_dma else 1)
                if e == "sp":
                    for ds in self.dma_sems:
                        if ds.count:
                            eng.wait_ge(ds.sem, 16 * ds.count)
                stats[e] = (len(self.ops[e]), nw)
            return body

        block.tensor(run("pe"))
        block.scalar(run("act"))
        block.vector(run("dve"))
        block.gpsimd(run("pool"))
        block.sync(run("sp"))
        return stats


class DmaRing:
    def __init__(self, sems):
        self.sems = sems
        self.i = 0

    def next(self):
        s = self.sems[self.i % len(self.sems)]
        self.i += 1
        return s


class R:
    __slots__ = ("ap", "b")

    def __init__(self, ap, name=""):
        self.ap = ap
        self.b = Buf(name)


class Ring:
    def __init__(self, items):
        self.items = items
        self.i = 0

    def next(self):
        r = self.items[self.i % len(self.items)]
        self.i += 1
        return r


class Arena:
    def __init__(self, regions):
        self.regions = regions
        self.reset()

    def reset(self):
        self.off = [0 for _ in self.regions]

    def alloc(self, shape, dtype, name=""):
        n = 1
        for s in shape:
            n *= s
        esz = 4 if dtype == F32 else 2
        nel = n * esz // 2
        for i, reg in enumerate(self.regions):
            o = (self.off[i] + 1) // 2 * 2
            if o + nel <= reg.shape[1]:
                self.off[i] = o + nel
                ap = reg[:, o:o + nel]
                if dtype != BF16:
                    ap = ap.bitcast(dtype)
                if len(shape) == 2:
                    ap = ap.rearrange("p (a b) -> p a b", b=shape[1])
                elif len(shape) == 3:
                    ap = ap.rearrange("p (a b c) -> p a b c", b=shape[1], c=shape[2])
                return R(ap, name)
        raise RuntimeError(f"arena overflow allocating {name} {shape}; offs={self.off}")


def run_skewed(stages, items):
    n, ns = len(items), len(stages)
    for step in range(n + ns - 1):
        for si in range(ns):
            j = step - si
            if 0 <= j < n:
                stages[si](items[j])


class SubTile:
    def __init__(self, row0, n, kt, tt, c0):
        self.row0, self.n, self.kt, self.tt, self.c0 = row0, n, kt, tt, c0


class SeqDesc:
    pass


class Builder:
    def __init__(self, NSEQ, SL, SAMPLE, parts=("mla", "da", "fin")):
        self.NSEQ, self.SL, self.SAMPLE = NSEQ, SL, SAMPLE
        self.parts = parts
        self.TK = max(SL, 1152 if SAMPLE else 0)
        self.W = {}
        self.NTT = SL // 128 + 1

    def dram(self, name, shape, kind="ExternalInput"):
        return self.nc.dram_tensor(name, list(shape), F32, kind=kind).ap()

    def sb(self, name, shape, dtype):
        return self.st.enter_context(self.nc.sbuf_tensor("sb_" + name, list(shape), dtype))

    def build(self):
        nc = self.nc = bass.Bass("TRN2", target_bir_lowering=False)
        NSEQ, SL, TK = self.NSEQ, self.SL, self.TK
        NP = NSEQ * SL
        dr = self.dram
        self.xp = dr("xp", [NP, D])
        self.w_in = dr("w_in", [D, IN_COLS])
        self.i_normg = dr("norm_g", [1, D])
        self.i_gateb = dr("gate_bT", [128, 16])
        self.i_lam = dr("da_lambda", [1, 256])
        self.i_hng = dr("hng", [128, 1])
        self.i_gq = dr("gq", [1, 384])
        self.i_gkv = dr("gkv", [1, 256])
        self.i_wuq = dr("w_uq", [384, 768])
        self.i_wuk = dr("w_uk", [256, 512])
        self.i_wuv = dr("w_uv", [256, 512])
        self.i_wba = dr("w_ba", [512, D])
        self.i_wbb = dr("w_bb", [512, D])
        self.i_wout = dr("w_out", [D, D])
        self.i_gf = dr("gf", [1, D])
        self.i_ident = dr("ident", [128, 128])
        self.i_shift = dr("shiftm", [128, 128])
        NTT = self.NTT
        self.i_cosD = dr("cosD", [128, NTT * 8])
        self.i_sinD = dr("sinD", [128, NTT * 8])
        self.i_cosM = dr("cosM", [128, NTT * 16])
        self.i_sinM = dr("sinM", [128, NTT * 16])
        o = lambda n, s: dr(n, s, kind="ExternalOutput")
        self.o_y = o("yp", [NP, D])
        self.o_k = o("kp", [NP, 512])
        self.o_v = o("vp", [NP, 512])
        self.o_lat = o("latp", [NP, 256])
        self.o_kr = o("krp", [NP, 32])
        if self.SAMPLE:
            self.xs = dr("xs", [64, D])
            self.cdk = dr("cdk", [1024, 512])
            self.cdv = dr("cdv", [1024, 512])
            self.clat = dr("clat", [1024, 256])
            self.ckr = dr("ckr", [1024, 32])
            self.o_ys = o("ys", [64, D])
            self.o_ks = o("ks", [64, 512])
            self.o_vs = o("vs", [64, 512])
            self.o_lats = o("lats", [64, 256])
            self.o_krs = o("krs", [64, 32])

        with ExitStack() as st:
            self.st = st
            S = self.S = Sched(nc, st)
            sb = self.sb
            self.U_N = max(8 * TK + (TK // 128 + 1) * 520, 8 * TK + 12288, 40960)
            self.U = sb("U", [128, self.U_N], BF16)
            self.SLX = SL
            SLX = self.SLX
            self.OB = sb("OB", [128, 4, SLX], BF16)
            self.OA_N = max(4 * SLX, 9728)
            self.OA = sb("OA", [128, self.OA_N], BF16)
            self.OBb = [Buf(f"OB{i}") for i in range(SL // 512 + 1)]
            self.OAb = [Buf(f"OA{i}") for i in range(SL // 512 + 1)]
            self.Ub = Buf("U")
            self.ident = R(sb("ident", [128, 128], BF16))
            self.ones = R(sb("ones", [128, 128], BF16))
            self.shiftm = R(sb("shiftm", [128, 128], BF16))
            self.sel64 = R(sb("sel64", [128, 64], BF16))
            self.cosD = R(sb("cosD", [128, NTT, 8], F32))
            self.sinD = R(sb("sinD", [128, NTT, 8], F32))
            self.cosM = R(sb("cosM", [128, NTT, 16], F32))
            self.sinM = R(sb("sinM", [128, NTT, 16], F32))
            self.g_in = R(sb("g_in", [128, D], F32))
            self.gateb = R(sb("gateb", [128, 16], F32))
            self.vec = R(sb("vec", [128, 16], F32))
            self.lamt = R(sb("lamt", [128, 256], F32))
            self.stats = Ring([R(sb(f"stat{i}", [128, 4], F32)) for i in range(6)])
            self.cstage = R(sb("cstage", [128, 128], F32))
            rem = nc.sbuf_bytes_remaining
            tn = (rem - 64) // 2 // 2 * 2
            self.Tt = sb("T", [128, tn], BF16)
            self.PS = [R(st.enter_context(nc.psum_tensor(f"ps{i}", [128, 512], F32)), f"ps{i}") for i in range(8)]
            for r_ in self.PS:
                r_.b.excl = True
            self.ds_const = S.dma_sem("dconst")
            self.dr_w = S.dma_ring("dw", 4)
            self.dr_x = S.dma_ring("dx", 2)
            self.dr_x2 = S.dma_ring("dxo", 2)
            self.dr_o = {k: S.dma_ring("do" + k, 2) for k in ("k", "v", "lat", "kr", "y")}
            self.ds_cache = S.dma_sem("dcache")

            self.consts()
            seqs = []
            for s in range(NSEQ):
                sd = SeqDesc()
                sd.x = self.xp
                sd.prompt = True
                sd.ncache = 0
                sd.blocks = []
                for b in range(SL // 512):
                    sd.blocks.append([SubTile(s * SL + (4 * b + j) * 128, 128, 4 * b + j, 4 * b + j, (4 * b + j) * 128) for j in range(4)])
                sd.o_y, sd.o_k, sd.o_v, sd.o_lat, sd.o_kr = self.o_y, self.o_k, self.o_v, self.o_lat, self.o_kr
                sd.obi = list(range(SL // 512))
                sd.OB = self.OB
                sd.OA = self.OA[:, 0:4 * SL].rearrange("p (h t) -> p h t", t=SL)
                seqs.append(sd)
            if self.SAMPLE:
                sd = SeqDesc()
                sd.x = self.xs
                sd.prompt = False
                sd.ncache = 8
                sd.blocks = [[SubTile(0, 64, 8, NTT - 1, 0)]]
                sd.obi = [SL // 512]
                lb = self.lamt.ap[:, :].bitcast(BF16)
                sd.OB = lb[:, 0:256].rearrange("p (h t) -> p h t", t=64)
                sd.OA = lb[:, 256:512].rearrange("p (h t) -> p h t", t=64)
                sd.o_y, sd.o_k, sd.o_v, sd.o_lat, sd.o_kr = self.o_ys, self.o_ks, self.o_vs, self.o_lats, self.o_krs
                seqs.append(sd)
            prompts = [q for q in seqs if q.prompt]
            smp = [q for q in seqs if not q.prompt]
            groups = [[q] for q in prompts[:-1]] + [prompts[-1:] + smp]
            for grp in groups:
                for pname, fn in (("mla", self.mla_pass), ("da", self.da_pass), ("fin", self.fin_pass)):
                    if pname in self.parts:
                        for gi, sd in enumerate(grp):
                            fn(sd, reload=(gi == 0))
            with nc.allow_low_precision(reason="bf16 matmul operands by design"), nc.Block() as block:
                self.stats_out = S.emit(block)
        return nc

    def consts(self):
        S = self.S
        ds = self.ds_const
        cs = self.cstage
        for src, dst in ((self.i_ident, self.ident), (self.i_shift, self.shiftm)):
            S.dma("sp", "dma_start", dict(out=cs.ap[:], in_=src), ds, writes=[cs.b])
            S.add("dve", "tensor_copy", dict(out=dst.ap[:], in_=cs.ap[:]), reads=[cs.b], writes=[dst.b])
        S.add("pool", "memset", dict(ap=self.ones.ap[:], constant=1.0), writes=[self.ones.b])
        S.add("pool", "memset", dict(ap=self.sel64.ap[:], constant=0.0), writes=[self.sel64.b])
        S.add("pool", "memset", dict(ap=self.sel64.ap[64:65, :], constant=1.0), writes=[self.sel64.b])
        NTT = self.NTT
        for src, dst, k in ((self.i_cosD, self.cosD, 8), (self.i_sinD, self.sinD, 8), (self.i_cosM, self.cosM, 16), (self.i_sinM, self.sinM, 16)):
            S.dma("sp", "dma_start", dict(out=dst.ap[:], in_=src.rearrange("p (t k) -> p t k", k=k)), ds, writes=[dst.b])
        S.dma("sp", "dma_start", dict(out=self.g_in.ap[:], in_=self.i_normg.partition_broadcast(128)), ds, writes=[self.g_in.b])
        S.dma("sp", "dma_start", dict(out=self.gateb.ap[:], in_=self.i_gateb), ds, writes=[self.gateb.b])
        S.dma("sp", "dma_start", dict(out=self.lamt.ap[:], in_=self.i_lam.partition_broadcast(128)), ds, writes=[self.lamt.b])
        v = self.vec
        S.add("pool", "memset", dict(ap=v.ap[:, 6:7], constant=-0.5), writes=[v.b])
        S.dma("sp", "dma_start", dict(out=v.ap[:, 1:2], in_=self.i_hng), ds, writes=[v.b])
        lt = self.lamt
        S.add("dve", "tensor_tensor", dict(out=lt.ap[:, 0:64], in0=lt.ap[:, 0:64], in1=lt.ap[:, 64:128], op=ALU.mult), reads=[lt.b], writes=[lt.b])
        S.add("dve", "tensor_tensor", dict(out=lt.ap[:, 128:192], in0=lt.ap[:, 128:192], in1=lt.ap[:, 192:256], op=ALU.mult), reads=[lt.b], writes=[lt.b])
        S.add("dve", "tensor_reduce", dict(out=v.ap[:, 2:3], in_=lt.ap[:, 0:64], op=ALU.add, axis=mybir.AxisListType.X), reads=[lt.b], writes=[v.b])
        S.add("dve", "tensor_reduce", dict(out=v.ap[:, 3:4], in_=lt.ap[:, 128:192], op=ALU.add, axis=mybir.AxisListType.X), reads=[lt.b], writes=[v.b])
        S.add("act", "activation", dict(out=v.ap[:, 4:6], in_=v.ap[:, 2:4], func=AF.Exp), reads=[v.b], writes=[v.b])
        S.add("dve", "scalar_tensor_tensor", dict(out=v.ap[:, 0:1], in0=v.ap[:, 5:6], scalar=-LAM_INIT, in1=v.ap[:, 4:5], op0=ALU.add, op1=ALU.subtract), reads=[v.b], writes=[v.b])
        S.add("dve", "tensor_scalar", dict(out=v.ap[:, 1:2], in0=v.ap[:, 1:2], scalar1=(1.0 - LAM_INIT), scalar2=None, op0=ALU.mult), reads=[v.b], writes=[v.b])

    def new_arena(self, extra_regions=()):
        self.S.barrier()
        self.ar = Arena([self.Tt[:, :]] + list(extra_regions))
        ar = self.ar
        self.xring = Ring([ar.alloc([D], F32, f"xt{i}") for i in range(2)])
        self.hbring = Ring([ar.alloc([D], BF16, f"hb{i}") for i in range(2)])

    def rstd(self, src_ap, src_bufs, n, F, junk):
        S = self.S
        stt = self.stats.next()
        S.add("act", "activation", dict(out=junk.ap[0:n, 0:F], in_=src_ap, func=AF.Square, accum_out=stt.ap[0:n, 0:1]),
              reads=src_bufs, writes=[junk.b, stt.b])
        if getattr(self, "rstd_pow", False):
            S.add("dve", "tensor_scalar", dict(out=stt.ap[0:n, 1:2], in0=stt.ap[0:n, 0:1], scalar1=1.0 / F, scalar2=EPS, op0=ALU.mult, op1=ALU.add),
                  reads=[stt.b], writes=[stt.b])
            S.add("pool", "tensor_tensor", dict(out=stt.ap[0:n, 2:3], in0=stt.ap[0:n, 1:2], in1=self.vec.ap[0:n, 6:7], op=ALU.pow),
                  reads=[stt.b, self.vec.b], writes=[stt.b])
            return stt
        S.add("act", "activation", dict(out=stt.ap[0:n, 1:2], in_=stt.ap[0:n, 0:1], func=AF.Ln, scale=1.0 / F, bias=EPS),
              reads=[stt.b], writes=[stt.b])
        S.add("act", "activation", dict(out=stt.ap[0:n, 2:3], in_=stt.ap[0:n, 1:2], func=AF.Exp, scale=-0.5),
              reads=[stt.b], writes=[stt.b])
        return stt

    def load_x(self, sd, stl, ring=None):
        S = self.S
        xt = self.xring.next()
        n = stl.n
        S.dma("sp", "dma_start", dict(out=xt.ap[0:n, :], in_=sd.x[stl.row0:stl.row0 + n, :]), (ring or self.dr_x).next(), writes=[xt.b])
        return xt

    def prologue(self, sd, stl, hT_ap, hT_buf, copy_eng="act", outs=None):
        S = self.S
        n = stl.n
        xt = self.load_x(sd, stl)
        hb = self.hbring.next()
        stt = self.rstd(xt.ap[0:n, :], [xt.b], n, D, hb)
        S.add("dve", "scalar_tensor_tensor", dict(out=hb.ap[0:n, :], in0=xt.ap[0:n, :], scalar=stt.ap[0:n, 2:3], in1=self.g_in.ap[0:n, :],
                                                       op0=ALU.mult, op1=ALU.mult), reads=[xt.b, stt.b, self.g_in.b], writes=[hb.b])
        pb = self.PS[0]
        pbf = pb.ap[:, :].bitcast(BF16).rearrange("p (c t) -> p c t", t=128)
        for c in range(8):
            S.add("pe", "transpose", dict(out=pbf[:, c, 0:n], in_=hb.ap[0:n, c * 128:(c + 1) * 128], identity=self.ident.ap[0:n, 0:n]),
                  reads=[hb.b, self.ident.b], writes=[pb.b])
        if outs is None:
            outs = [(hT_ap, 0, 8, [hT_buf])]
        for ap_, lo, hi, bufs in outs:
            if copy_eng == "act":
                S.add("act", "copy", dict(out=ap_, in_=pbf[:, lo:hi, 0:n]), reads=[pb.b], writes=bufs)
            else:
                S.add("dve", "tensor_copy", dict(out=ap_, in_=pbf[:, lo:hi, 0:n]), reads=[pb.b], writes=bufs)
        return xt

    def proj_tok(self, hT, n, w_ap, ncols, bank, wb):
        S = self.S
        for c in range(8):
            S.add("pe", "matmul", dict(out=bank.ap[0:n, 0:ncols], lhsT=hT.ap[:, c, 0:n], rhs=w_ap[:, c, :], start=(c == 0), stop=(c == 7)),
                  reads=[hT.b] + wb, writes=[bank.b])

    def load_w(self, dst_ap, src_ap, K, c0, c1, name=None):
        S = self.S
        pieces = []
        a = c0
        while a < c1:
            b = min(a + 1024, c1)
            pb_ = Buf(f"w_{name}_{a}")
            S.dma("pool", "dma_start", dict(out=dst_ap[:, :, a - c0:b - c0], in_=src_ap[:, a:b].rearrange("(c p) n -> p c n", p=128)),
                  self.dr_w.next(), writes=[pb_])
            pieces.append((a - c0, b - c0, pb_))
            a = b
        self.W[name] = pieces

    def wbufs(self, name, a=None, b=None):
        out = []
        for lo, hi, pb_ in self.W[name]:
            if a is None or (lo < b and a < hi):
                out.append(pb_)
        return out

    def rope(self, eng, src4, dst4, n, cos_ap, sin_ap, G, half, tmp, src_bufs, dst_bufs):
        S = self.S
        t4 = tmp.ap[0:n, 0:G * 2 * half].rearrange("p (g k) -> p g k", k=2 * half)
        cb = cos_ap.unsqueeze(1).broadcast_to([n, G, half])
        sbb = sin_ap.unsqueeze(1).broadcast_to([n, G, half])
        rb = src_bufs + [self.cosM.b, self.sinM.b, self.cosD.b, self.sinD.b]
        S.add(eng, "scalar_tensor_tensor", dict(out=t4[:, :, 0:half], in0=src4[:, :, half:2 * half], scalar=-1.0, in1=sbb, op0=ALU.mult, op1=ALU.mult),
              reads=rb, writes=[tmp.b])
        S.add(eng, "tensor_tensor", dict(out=t4[:, :, half:2 * half], in0=src4[:, :, 0:half], in1=sbb, op=ALU.mult), reads=rb, writes=[tmp.b])
        cb4 = cos_ap.unsqueeze(1).unsqueeze(2).broadcast_to([n, G, 2, half])
        S.add(eng, "tensor_tensor", dict(out=dst4.rearrange("p g (t k) -> p g t k", t=2), in0=src4.rearrange("p g (t k) -> p g t k", t=2), in1=cb4, op=ALU.mult),
              reads=rb, writes=dst_bufs)
        S.add(eng, "tensor_tensor", dict(out=dst4, in0=dst4, in1=t4, op=ALU.add), reads=[tmp.b] + dst_bufs, writes=dst_bufs)

    def key_tiles(self, sd, blk, bi):
        if sd.prompt:
            out = [(kt, 128, 0, False) for kt in range(4 * bi)]
            for j in range(4):
                out.append((4 * bi + j, 128, 128 * j, True))
            return out
        return [(kt, 128, 0, False) for kt in range(sd.ncache)] + [(sd.ncache, 64, 0, False)]

    def mla_pass(self, sd, reload=True):
        S = self.S
        S.cpy_to_dve = "m" in os.environ.get("K_CPY", "")
        self.rstd_pow = False
        TK = self.TK
        NKT = TK // 128
        w1 = self.OA[:, 0:9728]
        regs = [self.OA[:, 9728:self.OA_N]] if self.OA_N > 9728 + 1024 else []
        if not sd.prompt and self.SL >= 2048:
            regs.append(self.U[:, 8 * TK + 10 * 520:8 * TK + NKT * 520])
        self.new_arena(regs)
        ar = self.ar
        w1in = w1[:, 0:5376].rearrange("p (c n) -> p c n", n=672)
        wuq = w1[:, 5376:7680].rearrange("p (c n) -> p c n", n=768)
        wuk = w1[:, 7680:8704].rearrange("p (c n) -> p c n", n=512)
        wuv = w1[:, 8704:9728].rearrange("p (c n) -> p c n", n=512)
        if reload:
            self.load_w(w1in, self.w_in, 8, C_CQ, C_MZ, "w1in")
            self.load_w(wuq, self.i_wuq, 3, 0, 768, "wuq")
            self.load_w(wuk, self.i_wuk, 2, 0, 512, "wuk")
            self.load_w(wuv, self.i_wuv, 2, 0, 512, "wuv")
        KT = self.U[:, 0:8 * TK].rearrange("p (h t) -> p h t", t=TK)
        VM = self.U[:, 8 * TK:8 * TK + NKT * 520].rearrange("p (k c) -> p k c", c=520)
        VMf = self.U[:, 8 * TK:8 * TK + (NKT + 1) * 520]
        ktb = [Buf(f"ktm{i}") for i in range(NKT)]
        vmb = [Buf(f"vm{i}") for i in range(NKT)]
        gq = ar.alloc([384], F32, "gq")
        gkv = ar.alloc([256], F32, "gkv")
        S.dma("sp", "dma_start", dict(out=gq.ap[:], in_=self.i_gq.partition_broadcast(128)), self.ds_const, writes=[gq.b])
        S.dma("sp", "dma_start", dict(out=gkv.ap[:], in_=self.i_gkv.partition_broadcast(128)), self.ds_const, writes=[gkv.b])
        nkc = (sd.ncache + 1) * 128 if not sd.prompt else TK
        nvz = (nkc // 128 + 1) * 520
        ng = nkc // 512
        for g in range(ng):
            t_lo, t_hi = 4 * g, (4 * g + 4 if g < ng - 1 else nkc // 128 + 1)
            bufs_g = vmb[t_lo:min(t_hi, NKT)]
            S.add("dve" if g % 2 == 0 else "pool", "memset", dict(ap=VMf[:, t_lo * 520:t_hi * 520], constant=0.0), writes=bufs_g)
            S.add("pool", "memset", dict(ap=VM[:, t_lo:min(t_hi, nkc // 128), 512:520], constant=1.0), writes=bufs_g)
        for g in range(ng):
            c_hi = 512 * (g + 1) if g < ng - 1 else nkc
            S.add("pool" if g % 2 == 0 else "dve", "memset", dict(ap=KT[96:128, :, 512 * g:c_hi], constant=0.0), writes=ktb[4 * g:c_hi // 128])
        if STOP <= 1:
            return
        hTr = Ring([ar.alloc([8, 128], BF16, f"hTs{i}") for i in range(2)])
        pTr = Ring([ar.alloc([512], BF16, f"pT{i}") for i in range(3)])
        QT = ar.alloc([8, 512], BF16, "QT")
        cfr = Ring([ar.alloc([256], F32, f"cf{i}") for i in range(1)])
        krfr = Ring([ar.alloc([32], F32, f"krf{i}") for i in range(2)])
        cbf = ar.alloc([256], BF16, "cbf")
        cqbf = ar.alloc([384], BF16, "cqbf")
        ccT = ar.alloc([5, 128], BF16, "ccT")
        kfull = ar.alloc([8, 96], BF16, "kfull")
        qtok = ar.alloc([8, 96], BF16, "qtok")
        rtmp = ar.alloc([8 * 32], F32, "rtmp")
        u_sb = ar.alloc([512], F32, "u_sb")
        on_sb = ar.alloc([512], BF16, "on_sb")
        rr = ar.alloc([512], BF16, "rr")
        S.add("pool", "memset", dict(ap=rr.ap[:, :], constant=0.0), writes=[rr.b])
        rr32 = R(u_sb.ap, "rr32")
        junk = ar.alloc([384], BF16, "junk")
        S.add("pool", "memset", dict(ap=QT.ap[96:128, :, :], constant=0.0), writes=[QT.b])
        S.add("pool", "memset", dict(ap=on_sb.ap[:, :], constant=0.0), writes=[on_sb.b])
        PS = self.PS

        ccTr = Ring([ccT, ar.alloc([5, 128], BF16, "ccT1")])
        kfr_ = Ring([kfull, ar.alloc([8, 96], BF16, "kfull1")])
        rtmpC = ar.alloc([4 * 32], F32, "rtmpC")
        if os.environ.get("KDEBUG"):
            print("MLA arena", ar.off, [r.shape for r in ar.regions])

        def ctrans(cc, cb_ap, cb_b, n):
            pb = PS[4]
            pbf = pb.ap[:, :].bitcast(BF16).rearrange("p (c t) -> p c t", t=128)
            for c in range(2):
                S.add("pe", "transpose", dict(out=pbf[:, c, 0:n], in_=cb_ap[0:n, c * 128:(c + 1) * 128], identity=self.ident.ap[0:n, 0:n]),
                      reads=[cb_b, self.ident.b], writes=[pb.b])
            S.add("dve", "tensor_copy", dict(out=cc.ap[:, 0:2, 0:n], in_=pbf[:, 0:2, 0:n]), reads=[pb.b], writes=[cc.b])

        def kside2(cc, kf_, kt, n):
            for c in range(2):
                S.add("pe", "matmul", dict(out=PS[5].ap[0:n, :], lhsT=cc.ap[:, c, 0:n], rhs=wuk[:, c, :], start=(c == 0), stop=(c == 1)),
                      reads=[cc.b] + self.wbufs("wuk"), writes=[PS[5].b])
            for c in range(2):
                S.add("pe", "matmul", dict(out=PS[6].ap[0:n, :], lhsT=cc.ap[:, c, 0:n], rhs=wuv[:, c, :], start=(c == 0), stop=(c == 1)),
                      reads=[cc.b] + self.wbufs("wuv"), writes=[PS[6].b])
            S.add("dve", "tensor_copy", dict(out=kf_.ap[0:n, :, 0:64], in_=PS[5].ap[0:n, :].rearrange("p (h d) -> p h d", d=64)),
                  reads=[PS[5].b], writes=[kf_.b])
            S.add("dve", "tensor_copy", dict(out=VM[0:n, kt, 0:512], in_=PS[6].ap[0:n, :]), reads=[PS[6].b], writes=[vmb[kt]])
            pb2 = PS[7]
            pbf2 = pb2.ap[:, :].bitcast(BF16).rearrange("p (c t) -> p c t", t=128)
            for h in range(8):
                S.add("pe", "transpose", dict(out=pbf2[0:96, h, 0:n], in_=kf_.ap[0:n, h, :], identity=self.ident.ap[0:n, 0:n]),
                      reads=[kf_.b, self.ident.b], writes=[pb2.b])
            S.add("dve", "tensor_copy", dict(out=KT[0:96, :, kt * 128:kt * 128 + n], in_=pbf2[0:96, :, 0:n]), reads=[pb2.b], writes=[ktb[kt]])

        if sd.ncache:
            cst = ar.alloc([8, 256], BF16, "cst")
            kst = ar.alloc([8, 32], BF16, "kst")
            S.dma("pool", "dma_start", dict(out=cst.ap[:], in_=self.clat.rearrange("(t p) c -> p t c", p=128)), self.ds_cache, writes=[cst.b])
            S.dma("pool", "dma_start", dict(out=kst.ap[:], in_=self.ckr.rearrange("(t p) c -> p t c", p=128)), self.ds_cache, writes=[kst.b])

            def cB(t):
                cc, kf_ = ccTr.next(), kfr_.next()
                S.add("pool", "tensor_copy", dict(out=kf_.ap[:, :, 64:96], in_=kst.ap[:, t, :].unsqueeze(1).broadcast_to([128, 8, 32])),
                      reads=[kst.b], writes=[kf_.b])
                ctrans(cc, cst.ap[:, t, :], cst.b, 128)
                cctx[t] = (cc, kf_)

            def cC(t):
                cc, kf_ = cctx[t]
                kside2(cc, kf_, t, 128)
            cctx = {}
            run_skewed([cB, cC], list(range(sd.ncache)))

        def stA(c):
            c["hT"] = hTr.next()
            self.prologue(sd, c["stl"], c["hT"].ap[:, :, 0:c["stl"].n], c["hT"].b, copy_eng="dve")

        def stB(c):
            stl, hT = c["stl"], c["hT"]
            n = stl.n
            self.proj_tok(hT, n, w1in[:, :, 0:384], 384, PS[1], self.wbufs("w1in", 0, 384))
            self.proj_tok(hT, n, w1in[:, :, 384:672], 288, PS[2], self.wbufs("w1in", 384, 672))
            cc, kf_ = ccTr.next(), kfr_.next()
            c["cc"], c["kf"] = cc, kf_
            stt = self.rstd(PS[2].ap[0:n, 0:256], [PS[2].b], n, 256, junk)
            cf = cfr.next()
            S.add("dve", "scalar_tensor_tensor", dict(out=cf.ap[0:n, :], in0=PS[2].ap[0:n, 0:256], scalar=stt.ap[0:n, 2:3], in1=gkv.ap[0:n, :],
                                                      op0=ALU.mult, op1=ALU.mult), reads=[PS[2].b, stt.b, gkv.b], writes=[cf.b])
            S.dma("pool", "dma_start", dict(out=sd.o_lat[stl.row0:stl.row0 + n, :], in_=cf.ap[0:n, :]), self.dr_o["lat"].next(), reads=[cf.b])
            S.add("pool", "tensor_copy", dict(out=cbf.ap[0:n, :], in_=cf.ap[0:n, :]), reads=[cf.b], writes=[cbf.b])
            krf = krfr.next()
            src = PS[2].ap[0:n, 256:288].unsqueeze(1)
            dst = krf.ap[0:n, :].unsqueeze(1)
            self.rope("dve", src, dst, n, self.cosM.ap[0:n, stl.tt, :], self.sinM.ap[0:n, stl.tt, :], 1, 16, rtmp, [PS[2].b], [krf.b])
            S.dma("pool", "dma_start", dict(out=sd.o_kr[stl.row0:stl.row0 + n, :], in_=krf.ap[0:n, :]), self.dr_o["kr"].next(), reads=[krf.b])
            S.add("pool", "tensor_copy", dict(out=kf_.ap[0:n, :, 64:96], in_=krf.ap[0:n, :].unsqueeze(1).broadcast_to([n, 8, 32])),
                  reads=[krf.b], writes=[kf_.b])
            stq = self.rstd(PS[1].ap[0:n, 0:384], [PS[1].b], n, 384, junk)
            S.add("dve", "scalar_tensor_tensor", dict(out=cqbf.ap[0:n, :], in0=PS[1].ap[0:n, 0:384], scalar=stq.ap[0:n, 2:3], in1=gq.ap[0:n, :],
                                                      op0=ALU.mult, op1=ALU.mult), reads=[PS[1].b, stq.b, gq.b], writes=[cqbf.b])
            pb = PS[3]
            pbf = pb.ap[:, :].bitcast(BF16).rearrange("p (c t) -> p c t", t=128)
            for ch in range(3):
                S.add("pe", "transpose", dict(out=pbf[:, ch, 0:n], in_=cqbf.ap[0:n, ch * 128:(ch + 1) * 128], identity=self.ident.ap[0:n, 0:n]),
                      reads=[cqbf.b, self.ident.b], writes=[pb.b])
            S.add("dve", "tensor_copy", dict(out=cc.ap[:, 2:5, 0:n], in_=pbf[:, 0:3, 0:n]), reads=[pb.b], writes=[cc.b])
            ctrans(cc, cbf.ap, cbf.b, n)

        def stC(c):
            stl, cc, kf_, j = c["stl"], c["cc"], c["kf"], c["j"]
            n = stl.n
            kside2(cc, kf_, stl.kt, n)
            for hf in range(2):
                bank = PS[5 + hf]
                for ch in range(3):
                    S.add("pe", "matmul", dict(out=bank.ap[0:n, 0:384], lhsT=cc.ap[:, 2 + ch, 0:n], rhs=wuq[:, ch, hf * 384:(hf + 1) * 384],
                                               start=(ch == 0), stop=(ch == 2)), reads=[cc.b] + self.wbufs("wuq", hf * 384, (hf + 1) * 384), writes=[bank.b])
                q3 = bank.ap[0:n, 0:384].rearrange("p (h d) -> p h d", d=96)
                S.add("dve", "tensor_copy", dict(out=qtok.ap[0:n, 4 * hf:4 * hf + 4, 0:64], in_=q3[:, :, 0:64]), reads=[bank.b], writes=[qtok.b])
                self.rope("dve", q3[:, :, 64:96], qtok.ap[0:n, 4 * hf:4 * hf + 4, 64:96], n, self.cosM.ap[0:n, stl.tt, :], self.sinM.ap[0:n, stl.tt, :],
                          4, 16, rtmpC, [bank.b], [qtok.b])
            pb2 = PS[7]
            pbf2 = pb2.ap[:, :].bitcast(BF16).rearrange("p (c t) -> p c t", t=128)
            for h in range(8):
                S.add("pe", "transpose", dict(out=pbf2[0:96, h, 0:n], in_=qtok.ap[0:n, h, :], identity=self.ident.ap[0:n, 0:n]),
                      reads=[qtok.b, self.ident.b], writes=[pb2.b])
            S.add("act", "copy", dict(out=QT.ap[0:96, :, j * 128:j * 128 + n], in_=pbf2[0:96, :, 0:n]), reads=[pb2.b], writes=[QT.b])

        for bi, blk in enumerate(sd.blocks):
            nq = sum(s.n for s in blk)
            run_skewed([stA, stB, stC], [dict(stl=stl, j=j) for j, stl in enumerate(blk)])
            if STOP <= 7:
                continue
            tiles = self.key_tiles(sd, blk, bi)
            c0 = blk[0].c0
            units = [(h, ti) for h in range(8) for ti in range(len(tiles))]
            accs = Ring([PS[3], PS[4], PS[5]])
            sring = Ring([PS[0], PS[1], PS[2]])
            pend = []
            cur_acc = {}

            def issue_S(u):
                h, ti = u
                kt, nk, q0, mask = tiles[ti]
                sb_ = sring.next()
                S.add("pe", "matmul", dict(out=sb_.ap[0:nk, q0:nq], lhsT=KT[:, h, kt * 128:kt * 128 + nk], rhs=QT.ap[:, h, q0:nq], start=True, stop=True),
                      reads=[ktb[kt], QT.b], writes=[sb_.b])
                pT = pTr.next()
                S.add("act", "activation", dict(out=pT.ap[0:nk, q0:nq], in_=sb_.ap[0:nk, q0:nq], func=AF.Exp, scale=SC_MLA), reads=[sb_.b], writes=[pT.b])
                if mask:
                    S.add("pool", "memset", dict(ap=pT.ap[64:128, q0:q0 + 64], constant=0.0), writes=[pT.b])
                return pT

            def issue_AV(u, pT):
                h, ti = u
                kt, nk, q0, mask = tiles[ti]
                if ti == 0:
                    cur_acc[h] = accs.next()
                acc = cur_acc[h]
                S.add("pe", "matmul", dict(out=acc.ap[:, q0:nq], lhsT=VMf[0:nk, kt * 520 + h:kt * 520 + h + 1017:8], rhs=pT.ap[0:nk, q0:nq], start=(ti == 0), stop=(ti == len(tiles) - 1)),
                      reads=[vmb[kt], pT.b] + ([vmb[kt + 1]] if kt + 1 < NKT else []), writes=[acc.b])
                if ti == len(tiles) - 1:
                    fin(h, acc)

            deferred = []

            def defer(n, fn):
                deferred.append([n, fn])

            def tick(flush=False):
                while True:
                    due = [d for d in deferred if flush or d[0] <= 0]
                    if not due:
                        break
                    for d in due:
                        deferred.remove(d)
                    for d in due:
                        d[1]()
                for d in deferred:
                    d[0] -= 1

            def fin(h, acc):
                t, od = h // 2, h % 2
                S.add("act", "activation", dict(out=rr32.ap[64:65, 0:nq], in_=acc.ap[64:65, 0:nq], func=AF.Ln), reads=[acc.b], writes=[rr32.b])
                S.add("act", "activation", dict(out=rr.ap[64:65, 0:nq], in_=rr32.ap[64:65, 0:nq], func=AF.Exp, scale=-1.0), reads=[rr32.b], writes=[rr.b])
                S.add("dve", "tensor_copy", dict(out=u_sb.ap[0:64, 0:nq], in_=acc.ap[0:64, 0:nq]), reads=[acc.b], writes=[u_sb.b])

                def stage_b():
                    S.add("pe", "matmul", dict(out=PS[6].ap[0:64, 0:nq], lhsT=self.sel64.ap[:, 0:64], rhs=rr.ap[:, 0:nq], start=True, stop=True),
                          reads=[rr.b, self.sel64.b], writes=[PS[6].b])
                    if od == 0:
                        S.add("dve", "tensor_tensor", dict(out=sd.OB[0:64, t, c0:c0 + nq], in0=u_sb.ap[0:64, 0:nq], in1=PS[6].ap[0:64, 0:nq], op=ALU.mult),
                              reads=[u_sb.b, PS[6].b], writes=[self.OBb[sd.obi[bi]]])
                    else:
                        S.add("dve", "tensor_tensor", dict(out=on_sb.ap[0:64, 0:nq], in0=u_sb.ap[0:64, 0:nq], in1=PS[6].ap[0:64, 0:nq], op=ALU.mult),
                              reads=[u_sb.b, PS[6].b], writes=[on_sb.b])

                        def stage_c():
                            S.add("pe", "matmul", dict(out=PS[7].ap[:, 0:nq], lhsT=self.shiftm.ap[:, :], rhs=on_sb.ap[:, 0:nq], start=True, stop=True),
                                  reads=[on_sb.b, self.shiftm.b], writes=[PS[7].b])
                            S.add("dve", "tensor_copy", dict(out=sd.OB[64:128, t, c0:c0 + nq], in_=PS[7].ap[64:128, 0:nq]), reads=[PS[7].b], writes=[self.OBb[sd.obi[bi]]])
                        defer(2, stage_c)
                defer(3, stage_b)

            LOOK = 2
            for i, u in enumerate(units):
                pend.append((u, issue_S(u)))
                if len(pend) > LOOK:
                    issue_AV(*pend.pop(0))
                    tick()
            while pend:
                issue_AV(*pend.pop(0))
                tick()
            tick(flush=True)

    def da_pass(self, sd, reload=True):
        S = self.S
        S.cpy_to_dve = "d" in os.environ.get("K_CPY", "")
        self.rstd_pow = False
        TK = self.TK
        NKT = TK // 128
        regs = [self.U[:, 8 * TK + 12288:self.U_N]] if self.U_N - (8 * TK + 12288) > 1024 else []
        if not sd.prompt and self.SL >= 2048:
            regs.append(self.U[:, 4 * TK + 9 * 512:8 * TK])
        self.new_arena(regs)
        ar = self.ar
        w2 = self.U[:, 8 * TK:8 * TK + 12288].rearrange("p (c n) -> p c n", n=1536)
        if reload:
            self.load_w(w2, self.w_in, 8, 0, 1536, "w2")
        KT = self.U[:, 0:4 * TK].rearrange("p (h t) -> p h t", t=TK)
        VD = self.U[:, 4 * TK:8 * TK].rearrange("p (k c) -> p k c", c=512)
        ktb = [Buf(f"ktd{i}") for i in range(NKT)]
        vdb = [Buf(f"vd{i}") for i in range(NKT)]
        hTr = Ring([ar.alloc([8, 128], BF16, f"hTs{i}") for i in range(2)])
        pTr = Ring([ar.alloc([512], BF16, f"pT{i}") for i in range(4)])
        QT = ar.alloc([4, 2, 512], BF16, "QT")
        S.add("pool", "memset", dict(ap=QT.ap[64:128, :, 0, :], constant=0.0), writes=[QT.b])
        S.add("pool", "memset", dict(ap=QT.ap[0:64, :, 1, :], constant=0.0), writes=[QT.b])
        kfr = Ring([ar.alloc([512], F32, f"kf{i}") for i in range(1)])
        vfr = Ring([ar.alloc([512], F32, f"vf{i}") for i in range(1)])
        kb = ar.alloc([512], BF16, "kb")
        qb = ar.alloc([512], BF16, "qb")
        rtmp = ar.alloc([8 * 16], F32, "rtmp")
        t0 = ar.alloc([512], F32, "t0")
        t1 = ar.alloc([512], F32, "t1")
        oraw = ar.alloc([512], F32, "oraw")
        sq = kb
        lnr = ar.alloc([512], F32, "lnr")
        PS = self.PS
        OA = sd.OA

        def k_transposes(kb_ap, kb_b, kt, n):
            pb = PS[4]
            pbf = pb.ap[:, :].bitcast(BF16).rearrange("p (c t) -> p c t", t=128)
            for h in range(4):
                S.add("pe", "transpose", dict(out=pbf[:, h, 0:n], in_=kb_ap[0:n, h * 128:(h + 1) * 128], identity=self.ident.ap[0:n, 0:n]),
                      reads=[kb_b, self.ident.b], writes=[pb.b])
            S.add("act", "copy", dict(out=KT[:, :, kt * 128:kt * 128 + n], in_=pbf[:, 0:4, 0:n]), reads=[pb.b], writes=[ktb[kt]])

        if sd.ncache:
            kst = ar.alloc([8, 512], BF16, "kst")
            S.dma("pool", "dma_start", dict(out=kst.ap[:], in_=self.cdk.rearrange("(t p) c -> p t c", p=128)), self.ds_cache, writes=[kst.b])
            S.dma("pool", "dma_start", dict(out=VD[:, 0:8, :], in_=self.cdv.rearrange("(t p) c -> p t c", p=128)), self.ds_cache, writes=vdb[0:8])
            for t in range(sd.ncache):
                k_transposes(kst.ap[:, t, :], kst.b, t, 128)

        for bi, blk in enumerate(sd.blocks):
            nq = sum(s.n for s in blk)
            c0 = blk[0].c0
            for j, stl in enumerate(blk):
                n = stl.n
                kt = stl.kt
                hT = hTr.next()
                self.prologue(sd, stl, hT.ap[:, :, 0:n], hT.b)
                self.proj_tok(hT, n, w2[:, :, 0:512], 512, PS[1], self.wbufs("w2", 0, 512))
                self.proj_tok(hT, n, w2[:, :, 512:1024], 512, PS[2], self.wbufs("w2", 512, 1024))
                self.proj_tok(hT, n, w2[:, :, 1024:1536], 512, PS[3], self.wbufs("w2", 1024, 1536))
                cosd = self.cosD.ap[0:n, stl.tt, :]
                sind = self.sinD.ap[0:n, stl.tt, :]
                kf = kfr.next()
                S.add("act", "copy", dict(out=kf.ap[0:n, :], in_=PS[2].ap[0:n, :]), reads=[PS[2].b], writes=[kf.b])
                k4 = PS[2].ap[0:n, :].rearrange("p (g d) -> p g d", d=64)[:, :, 0:16]
                kf4 = kf.ap[0:n, :].rearrange("p (g d) -> p g d", d=64)[:, :, 0:16]
                self.rope("dve", k4, kf4, n, cosd, sind, 8, 8, rtmp, [PS[2].b], [kf.b])
                S.dma("pool", "dma_start", dict(out=sd.o_k[stl.row0:stl.row0 + stl.n, :], in_=kf.ap[0:stl.n, :]), self.dr_o["k"].next(), reads=[kf.b])
                S.add("pool", "tensor_copy", dict(out=kb.ap[0:n, :], in_=kf.ap[0:n, :]), reads=[kf.b], writes=[kb.b])
                k_transposes(kb.ap, kb.b, kt, n)
                vf = vfr.next()
                S.add("act", "copy", dict(out=vf.ap[0:n, :], in_=PS[3].ap[0:n, :]), reads=[PS[3].b], writes=[vf.b])
                S.dma("pool", "dma_start", dict(out=sd.o_v[stl.row0:stl.row0 + stl.n, :], in_=vf.ap[0:stl.n, :]), self.dr_o["v"].next(), reads=[vf.b])
                S.add("dve", "tensor_copy", dict(out=VD[0:n, kt, :], in_=PS[3].ap[0:n, :]), reads=[PS[3].b], writes=[vdb[kt]])
                S.add("act", "copy", dict(out=qb.ap[0:n, :], in_=PS[1].ap[0:n, :]), reads=[PS[1].b], writes=[qb.b])
                q4 = PS[1].ap[0:n, :].rearrange("p (g d) -> p g d", d=64)[:, :, 0:16]
                qb4 = qb.ap[0:n, :].rearrange("p (g d) -> p g d", d=64)[:, :, 0:16]
                self.rope("dve", q4, qb4, n, cosd, sind, 8, 8, rtmp, [PS[1].b], [qb.b])
                pb = PS[5]
                pbf = pb.ap[:, :].bitcast(BF16).rearrange("p (c t) -> p c t", t=128)
                for h in range(4):
                    S.add("pe", "transpose", dict(out=pbf[:, h, 0:n], in_=qb.ap[0:n, h * 128:(h + 1) * 128], identity=self.ident.ap[0:n, 0:n]),
                          reads=[qb.b, self.ident.b], writes=[pb.b])
                S.add("act", "copy", dict(out=QT.ap[0:64, :, 0, j * 128:j * 128 + n], in_=pbf[0:64, 0:4, 0:n]), reads=[pb.b], writes=[QT.b])
                S.add("dve", "tensor_copy", dict(out=QT.ap[64:128, :, 1, j * 128:j * 128 + n], in_=pbf[64:128, 0:4, 0:n]), reads=[pb.b], writes=[QT.b])
            tiles = self.key_tiles(sd, blk, bi)
            units = [(h, c, ti) for h in range(4) for c in range(2) for ti in range(len(tiles))]
            accs = Ring([(PS[3], PS[4]), (PS[5], PS[6])])
            sring = Ring([PS[0], PS[1], PS[2]])
            pend = []
            cur_acc = {}
            tt = {0: t0, 1: t1}

            def issue_S(u):
                h, c, ti = u
                kt, nk, q0, mask = tiles[ti]
                sb_ = sring.next()
                S.add("pe", "matmul", dict(out=sb_.ap[0:nk, q0:nq], lhsT=KT[:, h, kt * 128:kt * 128 + nk], rhs=QT.ap[:, h, c, q0:nq],
                                               start=True, stop=True), reads=[ktb[kt], QT.b], writes=[sb_.b])
                pT = pTr.next()
                S.add("act", "activation", dict(out=pT.ap[0:nk, q0:nq], in_=sb_.ap[0:nk, q0:nq], func=AF.Exp, scale=SC_DA), reads=[sb_.b], writes=[pT.b])
                if mask:
                    S.add("pool", "memset", dict(ap=pT.ap[64:128, q0:q0 + 64], constant=0.0), writes=[pT.b])
                return pT

            def issue_AV(u, pT):
                h, c, ti = u
                kt, nk, q0, mask = tiles[ti]
                if ti == 0:
                    cur_acc[(h, c)] = accs.next()
                au, asum = cur_acc[(h, c)]
                first, last = (ti == 0), (ti == len(tiles) - 1)
                S.add("pe", "matmul", dict(out=au.ap[:, q0:nq], lhsT=VD[0:nk, kt, h * 128:(h + 1) * 128], rhs=pT.ap[0:nk, q0:nq], start=first, stop=last),
                      reads=[vdb[kt], pT.b], writes=[au.b])
                S.add("pe", "matmul", dict(out=asum.ap[:, q0:nq], lhsT=self.ones.ap[0:nk, :], rhs=pT.ap[0:nk, q0:nq], start=first, stop=last),
                      reads=[self.ones.b, pT.b], writes=[asum.b])
                if last:
                    t = tt[c]
                    if c == 0:
                        S.add("dve", "reciprocal", dict(out=t.ap[:, 0:nq], in_=asum.ap[:, 0:nq]), reads=[asum.b], writes=[t.b])
                    else:
                        S.add("act", "activation", dict(out=t.ap[:, 0:nq], in_=asum.ap[:, 0:nq], func=AF.Ln), reads=[asum.b], writes=[t.b])
                        S.add("act", "activation", dict(out=t.ap[:, 0:nq], in_=t.ap[:, 0:nq], func=AF.Exp, scale=-1.0), reads=[t.b], writes=[t.b])
                    S.add("dve", "tensor_tensor", dict(out=t.ap[:, 0:nq], in0=t.ap[:, 0:nq], in1=au.ap[:, 0:nq], op=ALU.mult), reads=[au.b, t.b], writes=[t.b])
                    if c == 1:
                        fin(h)

            deferred = []

            def defer(n, fn):
                deferred.append([n, fn])

            def tick(flush=False):
                while True:
                    due = [d for d in deferred if flush or d[0] <= 0]
                    if not due:
                        break
                    for d in due:
                        deferred.remove(d)
                    for d in due:
                        d[1]()
                for d in deferred:
                    d[0] -= 1

            def fin(h):
                v = self.vec
                S.add("dve", "scalar_tensor_tensor", dict(out=oraw.ap[:, 0:nq], in0=t1.ap[:, 0:nq], scalar=v.ap[:, 0:1], in1=t0.ap[:, 0:nq], op0=ALU.mult, op1=ALU.add),
                      reads=[t0.b, t1.b, v.b], writes=[oraw.b])
                S.add("act", "activation", dict(out=sq.ap[:, 0:nq], in_=oraw.ap[:, 0:nq], func=AF.Square), reads=[oraw.b], writes=[sq.b])

                def stage_b():
                    S.add("pe", "matmul", dict(out=PS[7].ap[:, 0:nq], lhsT=self.ones.ap[:, :], rhs=sq.ap[:, 0:nq], start=True, stop=True), reads=[sq.b, self.ones.b], writes=[PS[7].b])
                    S.add("act", "activation", dict(out=lnr.ap[:, 0:nq], in_=PS[7].ap[:, 0:nq], func=AF.Ln, scale=1.0 / 128, bias=EPS), reads=[PS[7].b], writes=[lnr.b])
                    S.add("act", "activation", dict(out=lnr.ap[:, 0:nq], in_=lnr.ap[:, 0:nq], func=AF.Exp, scale=-0.5), reads=[lnr.b], writes=[lnr.b])
                    S.add("dve", "scalar_tensor_tensor", dict(out=OA[:, h, c0:c0 + nq], in0=oraw.ap[:, 0:nq], scalar=v.ap[:, 1:2], in1=lnr.ap[:, 0:nq], op0=ALU.mult, op1=ALU.mult),
                          reads=[oraw.b, lnr.b, v.b], writes=[self.OAb[sd.obi[bi]]])
                defer(5, stage_b)

            LOOK = 2
            for i, u in enumerate(units):
                pend.append((u, issue_S(u)))
                if len(pend) > LOOK:
                    issue_AV(*pend.pop(0))
                    tick()
            while pend:
                issue_AV(*pend.pop(0))
                tick()
            tick(flush=True)

    def fin_pass(self, sd, reload=True):
        S = self.S
        S.cpy_to_dve = "f" in os.environ.get("K_CPY", "")
        self.rstd_pow = os.environ.get("K_POW", "1") == "1"
        SL = self.SL
        self.new_arena([self.U[:, 40960:self.U_N]] if self.U_N - 40960 > 1024 else [])
        ar = self.ar
        U = self.U
        wz = U[:, 0:4096].rearrange("p (c n) -> p c n", n=512)
        wg = U[:, 4096:24576].rearrange("p (c n) -> p c n", n=2560)
        wba = U[:, 24576:28672].rearrange("p (c n) -> p c n", n=1024)
        wbb = U[:, 28672:32768].rearrange("p (c n) -> p c n", n=1024)
        wout = U[:, 32768:40960].rearrange("p (c n) -> p c n", n=1024)
        if reload:
            self.load_w(wz, self.w_in, 8, C_DAZ, C_DAZ + 512, "wz")
            self.load_w(wg, self.w_in, 8, C_MZ, IN_COLS, "wg")
            self.load_w(wba, self.i_wba, 4, 0, 1024, "wba")
            self.load_w(wbb, self.i_wbb, 4, 0, 1024, "wbb")
            self.load_w(wout, self.i_wout, 8, 0, 1024, "wout")
        gf = ar.alloc([D], F32, "gf")
        S.dma("sp", "dma_start", dict(out=gf.ap[:], in_=self.i_gf.partition_broadcast(128)), self.ds_const, writes=[gf.b])
        hT = ar.alloc([8, 512], BF16, "hT")
        oz = ar.alloc([8, 512], BF16, "oz")
        mT = ar.alloc([8, 512], BF16, "mT")
        sg = Ring([ar.alloc([512], F32, f"sg{i}") for i in range(2)])
        tg = Ring([ar.alloc([512], F32, f"tg{i}") for i in range(2)])
        PS = self.PS
        OA = sd.OA
        OB = sd.OB
        zb = Ring([PS[i] for i in range(1, 8)])
        ojunk = [None]
        if os.environ.get("K_PX", "0") == "1":
            pro_x = Ring([self.xring.items[0], self.xring.items[1]])
            out_x = Ring([self.xring.items[1]])
        else:
            pro_x = Ring([self.xring.items[0]])
            out_x = Ring([self.xring.items[1]])
        for bi, blk in enumerate(sd.blocks):
            nq = sum(s.n for s in blk)
            c0 = blk[0].c0
            self.xring = pro_x
            if sd.prompt and bi >= 1:
                c0p = sd.blocks[bi - 1][0].c0
                hch = [OA[:, c, c0p:c0p + 512] for c in range(4)] + [OB[:, c, c0p:c0p + 512] for c in range(4)]
                hbufs = [self.OAb[bi - 1], self.OBb[bi - 1]]
                for j, stl in enumerate(blk):
                    self.prologue(sd, stl, None, None, outs=[(OA[:, :, c0p + j * 128:c0p + j * 128 + stl.n], 0, 4, [self.OAb[bi - 1]]),
                                                             (OB[:, :, c0p + j * 128:c0p + j * 128 + stl.n], 4, 8, [self.OBb[bi - 1]])])
            else:
                hch = [hT.ap[:, c, :] for c in range(8)]
                hbufs = [hT.b]
                for j, stl in enumerate(blk):
                    self.prologue(sd, stl, hT.ap[:, :, j * 128:j * 128 + stl.n], hT.b)
            for m in range(8):
                bank = zb.next()
                wsrc = wz[:, :, m * 128:(m + 1) * 128] if m < 4 else wg[:, :, (m - 4) * 128:(m - 3) * 128]
                wzb = self.wbufs("wz") if m < 4 else self.wbufs("wg", (m - 4) * 128, (m - 3) * 128)
                for c in range(8):
                    S.add("pe", "matmul", dict(out=bank.ap[:, 0:nq], lhsT=wsrc[:, c, :], rhs=hch[c][:, 0:nq], start=(c == 0), stop=(c == 7)),
                          reads=hbufs + wzb, writes=[bank.b])
                s_ = sg.next()
                S.add("act", "activation", dict(out=s_.ap[:, 0:nq], in_=bank.ap[:, 0:nq], func=AF.Silu), reads=[bank.b], writes=[s_.b])
                osrc = OA[:, m, c0:c0 + nq] if m < 4 else OB[:, m - 4, c0:c0 + nq]
                ob = [self.OAb[sd.obi[bi]]] if m < 4 else [self.OBb[sd.obi[bi]]]
                S.add("dve", "tensor_tensor", dict(out=oz.ap[:, m, 0:nq], in0=s_.ap[:, 0:nq], in1=osrc, op=ALU.mult), reads=[s_.b] + ob, writes=[oz.b])
            for m in range(8):
                bya, byb, bga, bgb = zb.next(), zb.next(), zb.next(), zb.next()
                for c in range(4):
                    S.add("pe", "matmul", dict(out=bya.ap[:, 0:nq], lhsT=wba[:, c, m * 128:(m + 1) * 128], rhs=oz.ap[:, c, 0:nq], start=(c == 0), stop=(c == 3)),
                          reads=[oz.b] + self.wbufs("wba"), writes=[bya.b])
                for c in range(4):
                    S.add("pe", "matmul", dict(out=byb.ap[:, 0:nq], lhsT=wbb[:, c, m * 128:(m + 1) * 128], rhs=oz.ap[:, 4 + c, 0:nq], start=(c == 0), stop=(c == 3)),
                          reads=[oz.b] + self.wbufs("wbb"), writes=[byb.b])
                for c in range(8):
                    S.add("pe", "matmul", dict(out=bga.ap[:, 0:nq], lhsT=wg[:, c, 512 + m * 128:512 + (m + 1) * 128], rhs=hch[c][:, 0:nq], start=(c == 0), stop=(c == 7)),
                          reads=hbufs + self.wbufs("wg", 512 + m * 128, 512 + (m + 1) * 128), writes=[bga.b])
                for c in range(8):
                    S.add("pe", "matmul", dict(out=bgb.ap[:, 0:nq], lhsT=wg[:, c, 1536 + m * 128:1536 + (m + 1) * 128], rhs=hch[c][:, 0:nq], start=(c == 0), stop=(c == 7)),
                          reads=hbufs + self.wbufs("wg", 1536 + m * 128, 1536 + (m + 1) * 128), writes=[bgb.b])
                ga, gb_ = sg.next(), sg.next()
                S.add("act", "activation", dict(out=ga.ap[:, 0:nq], in_=bga.ap[:, 0:nq], func=AF.Sigmoid, bias=self.gateb.ap[:, m:m + 1]), reads=[bga.b, self.gateb.b], writes=[ga.b])
                S.add("act", "activation", dict(out=gb_.ap[:, 0:nq], in_=bgb.ap[:, 0:nq], func=AF.Sigmoid, bias=self.gateb.ap[:, 8 + m:9 + m]), reads=[bgb.b, self.gateb.b], writes=[gb_.b])
                ta, tb = tg.next(), tg.next()
                S.add("dve", "tensor_tensor", dict(out=ta.ap[:, 0:nq], in0=ga.ap[:, 0:nq], in1=bya.ap[:, 0:nq], op=ALU.mult), reads=[ga.b, bya.b], writes=[ta.b])
                S.add("dve", "tensor_tensor", dict(out=tb.ap[:, 0:nq], in0=gb_.ap[:, 0:nq], in1=byb.ap[:, 0:nq], op=ALU.mult), reads=[gb_.b, byb.b], writes=[tb.b])
                S.add("pool", "tensor_tensor", dict(out=mT.ap[:, m, 0:nq], in0=ta.ap[:, 0:nq], in1=tb.ap[:, 0:nq], op=ALU.add), reads=[ta.b, tb.b], writes=[mT.b])
            if sd.prompt and bi == 0 and len(sd.blocks) > 1:
                x2 = R(hT.ap[:, 0:4, :].rearrange("p a b -> p (a b)").bitcast(F32), "xt2")
                x2.b.w = hT.b.w
                x2.b.rs = list(hT.b.rs)
                if os.environ.get("K_PX", "0") == "1":
                    out_x.items[:] = [x2]
                else:
                    out_x.items.append(x2)
                jk = R(hT.ap[:, 4:6, :].rearrange("p a b -> p (a b)"), "ojunk")
                jk.b.w = hT.b.w
                jk.b.rs = list(hT.b.rs)
                ojunk[0] = jk
            self.xring = out_x
            for j, stl in enumerate(blk):
                n = stl.n
                xt = self.load_x(sd, stl, self.dr_x2)
                for hf in range(2):
                    bank = zb.next()
                    for c in range(8):
                        S.add("pe", "matmul", dict(out=bank.ap[0:n, :], lhsT=mT.ap[:, c, j * 128:j * 128 + n], rhs=wout[:, c, hf * 512:(hf + 1) * 512],
                                                                                  start=(c == 0), stop=(c == 7)), reads=[mT.b] + self.wbufs("wout", hf * 512, (hf + 1) * 512), writes=[bank.b])
                    S.add("dve", "tensor_tensor", dict(out=xt.ap[0:n, hf * 512:(hf + 1) * 512], in0=xt.ap[0:n, hf * 512:(hf + 1) * 512], in1=bank.ap[0:n, :], op=ALU.add),
                          reads=[bank.b, xt.b], writes=[xt.b])
                stt = self.rstd(xt.ap[0:n, :], [xt.b], n, D, ojunk[0] if ojunk[0] is not None else self.hbring.next())
                S.add("dve", "scalar_tensor_tensor", dict(out=xt.ap[0:n, :], in0=xt.ap[0:n, :], scalar=stt.ap[0:n, 2:3], in1=gf.ap[0:n, :], op0=ALU.mult, op1=ALU.mult),
                      reads=[xt.b, stt.b, gf.b], writes=[xt.b])
                S.dma("pool", "dma_start", dict(out=sd.o_y[stl.row0:stl.row0 + stl.n, :], in_=xt.ap[0:stl.n, :]), self.dr_o["y"].next(), reads=[xt.b])


def rope_tables(SL, sample):
    ntt = SL // 128 + 1
    pos = np.zeros((128, ntt), np.float64)
    for t in range(ntt - 1):
        pos[:, t] = t * 128 + np.arange(128)
    pos[:, ntt - 1] = 1024 + np.arange(128)
    out = {}
    for name, rot in (("D", 16), ("M", 32)):
        half = rot // 2
        inv = (np.float32(500000.0) ** (-np.arange(half, dtype=np.float32) * np.float32(2.0) / np.float32(rot))).astype(np.float32)
        ang = (pos.astype(np.float32)[:, :, None] * inv[None, None, :]).astype(np.float32)
        out["cos" + name] = np.cos(ang.astype(np.float64)).astype(np.float32).reshape(128, ntt * half)
        out["sin" + name] = np.sin(ang.astype(np.float64)).astype(np.float32).reshape(128, ntt * half)
    return out


_CACHE = {}


def get_nc(NSEQ, SL, SAMPLE, parts=("mla", "da", "fin")):
    key = (NSEQ, SL, SAMPLE, parts)
    if key not in _CACHE:
        b = Builder(NSEQ, SL, SAMPLE, parts)
        nc = b.build()
        _CACHE[key] = (nc, b)
    return _CACHE[key]


def shared_inputs(inp, SL):
    f = lambda a: np.ascontiguousarray(np.asarray(a, dtype=np.float32))
    sh = {
        "w_in": f(inp["w_in"][0]),
        "norm_g": f(inp["norm_g"][0]).reshape(1, D),
        "gate_bT": f(np.asarray(inp["gate_b"][0]).reshape(16, 128).T),
        "da_lambda": f(inp["da_lambda"][0]).reshape(1, 256),
        "hng": f(inp["da_head_norm_g"][0]).reshape(128, 1),
        "gq": f(inp["mla_q_norm_g"][0]).reshape(1, 384),
        "gkv": f(inp["mla_kv_norm_g"][0]).reshape(1, 256),
        "w_uq": f(inp["mla_w_uq"][0]),
        "w_uk": f(inp["mla_w_uk"][0]),
        "w_uv": f(np.asarray(inp["mla_w_uv"][0]).reshape(256, 8, 64).transpose(0, 2, 1).reshape(256, 512)),
        "w_ba": f(inp["w_branch_a"][0]),
        "w_bb": f(inp["w_branch_b"][0]),
        "w_out": f(inp["w_out"][0]),
        "gf": f(inp["final_norm_g"]).reshape(1, D),
        "ident": np.eye(128, dtype=np.float32),
        "shiftm": np.eye(128, k=64, dtype=np.float32),
    }
    sh.update(rope_tables(SL, True))
    return sh


def kernel(**inputs):
    NCORES = 8
    xp = np.asarray(inputs["x_prompt"], dtype=np.float32)
    xs = np.asarray(inputs["x_sample"], dtype=np.float32)
    B, SL, _ = xp.shape
    NSEQ = B // NCORES
    nc, _ = get_nc(NSEQ, SL, True)
    sh = shared_inputs(inputs, SL)
    cdk = np.asarray(inputs["cache_da_k"], dtype=np.float32)[0]
    cdv = np.asarray(inputs["cache_da_v"], dtype=np.float32)[0]
    clat = np.asarray(inputs["cache_mla_latent"], dtype=np.float32)[0]
    ckr = np.asarray(inputs["cache_mla_krope"], dtype=np.float32)[0]
    in_maps = []
    for c in range(NCORES):
        m = dict(sh)
        m["xp"] = np.ascontiguousarray(xp[c * NSEQ:(c + 1) * NSEQ].reshape(NSEQ * SL, D))
        m["xs"] = np.ascontiguousarray(xs[c])
        m["cdk"] = np.ascontiguousarray(cdk[c].reshape(1024, 512))
        m["cdv"] = np.ascontiguousarray(cdv[c].reshape(1024, 512))
        m["clat"] = np.ascontiguousarray(clat[c])
        m["ckr"] = np.ascontiguousarray(ckr[c])
        in_maps.append(m)
    res = run_bass_kernel_spmd(nc, in_maps, core_ids=list(range(NCORES))).results
    cat = lambda k: np.concatenate([np.asarray(r[k]) for r in res], axis=0)
    y_p = cat("yp").reshape(B, SL, D)
    y_s = cat("ys").reshape(NCORES, 64, D)
    k_p = cat("kp").reshape(1, B, SL, 4, 128)
    v_p = cat("vp").reshape(1, B, SL, 4, 128)
    lat_p = cat("latp").reshape(1, B, SL, 256)
    kr_p = cat("krp").reshape(1, B, SL, 32)
    k_s = cat("ks").reshape(1, NCORES, 64, 4, 128)
    v_s = cat("vs").reshape(1, NCORES, 64, 4, 128)
    lat_s = cat("lats").reshape(1, NCORES, 64, 256)
    kr_s = cat("krs").reshape(1, NCORES, 64, 32)
    return tuple(np.ascontiguousarray(a, dtype=np.float32) for a in (y_p, y_s, k_p, v_p, lat_p, kr_p, k_s, v_s, lat_s, kr_s))
```
